# Optimizing a Trainium2 kernel written in Bass

```python
import math
import numpy as np
import jax
import jax.numpy as jnp
from jax import lax


D_MODEL = 1024
BATCH = 8
SEQ = 8192
DEPTH = 4
DEC_BATCH = 4
DEC_SEQ = 4096
PAST_LEN = 128

N_MEM = 256
EPS = 1e-6
D_FF = 2816
SGU_WIDTH = 512
SGU_GROUPS = 4
SGU_CHUNK = 128
SGU_GDIM = SGU_WIDTH // SGU_GROUPS
POOL_WIDTH = 512
POOL_WINDOWS = (2, 4, 8, 16)
POOL_GROUPS = len(POOL_WINDOWS)
POOL_GDIM = POOL_WIDTH // POOL_GROUPS
SSD_INNER = 1024
SSD_HEADDIM = 64
SSD_HEADS = SSD_INNER // SSD_HEADDIM
SSD_GROUPS = 2
SSD_HPG = SSD_HEADS // SSD_GROUPS
SSD_STATE = 64
SSD_CHUNK = 128
SSD_CONV = 4
SSD_CONV_LEFT = 2
SSD_GN = SSD_GROUPS * SSD_STATE
SSD_CONV_DIM = SSD_INNER + 2 * SSD_GN
DT_MIN = 0.001
DT_MAX = 0.1
XA_HEADS = 4
XA_HEADDIM = D_MODEL // XA_HEADS
N_BRANCH = 3
IN_SPLITS = (SGU_WIDTH, 2 * SGU_WIDTH, 2 * SGU_WIDTH + POOL_WIDTH, 2 * SGU_WIDTH + POOL_WIDTH + SSD_INNER, 2 * SGU_WIDTH + POOL_WIDTH + SSD_INNER + SSD_CONV_DIM)
N_IN = 2 * SGU_WIDTH + POOL_WIDTH + SSD_INNER + SSD_CONV_DIM + 2 * SSD_HEADS

kernel_name = 'hybrid_gated_sgu_pool_ssd_encoder'


def rms_norm(x, g):
    xf = x.astype(jnp.float32)
    y = xf * lax.rsqrt(jnp.mean(xf * xf, axis=-1, keepdims=True) + EPS)
    return (y * g.astype(jnp.float32)).astype(x.dtype)


def layer_norm(x, g, b):
    xf = x.astype(jnp.float32)
    xc = xf - jnp.mean(xf, axis=-1, keepdims=True)
    var = jnp.mean(xc * xc, axis=-1, keepdims=True)
    y = xc * lax.rsqrt(var + EPS) * g.astype(jnp.float32) + b.astype(jnp.float32)
    return y.astype(x.dtype)


def swiglu_ffn(x, w13, w2):
    a, b = jnp.split(x @ w13, 2, axis=-1)
    return (jax.nn.silu(a) * b) @ w2


def spatial_gating(u, v, ln_g, ln_b, ws, bias):
    b, s, _ = v.shape
    nc = s // SGU_CHUNK
    vn = layer_norm(v, ln_g, ln_b).reshape(b, nc, SGU_CHUNK, SGU_GROUPS, SGU_GDIM)
    mixed = jnp.einsum('gts,bcsgd->bctgd', ws, vn) + bias.T[None, None, :, :, None]
    return u * mixed.reshape(b, s, SGU_WIDTH)


def multiscale_pool(p, w_grp, scale):
    b, s, _ = p.shape
    pf = p.astype(jnp.float32).reshape(b, s, POOL_GROUPS, POOL_GDIM)
    csum = jnp.concatenate([jnp.zeros_like(pf[:, :1]), jnp.cumsum(pf, axis=1)], axis=1)
    pos = np.arange(s)
    outs = []
    for gi, w in enumerate(POOL_WINDOWS):
        lo = np.clip(pos - w // 2, 0, s)
        hi = np.clip(pos + w - w // 2, 0, s)
        cnt = jnp.asarray((hi - lo).astype(np.float32))[None, :, None]
        mean = (csum[:, hi, gi] - csum[:, lo, gi]) / cnt
        outs.append(mean - pf[:, :, gi])
    pooled = jnp.stack(outs, axis=2).astype(p.dtype)
    mixed = jnp.einsum('bsgd,gde->bsge', pooled, w_grp).reshape(b, s, POOL_WIDTH)
    return mixed * scale


def centred_dwconv(x, w, bias):
    s = x.shape[1]
    xp = jnp.pad(x, ((0, 0), (SSD_CONV_LEFT, SSD_CONV - 1 - SSD_CONV_LEFT), (0, 0)))
    y = bias
    for k in range(SSD_CONV):
        y = y + xp[:, k:k + s] * w[k]
    return y


def ssd_scan(x, dt, a, bm, cm):
    b, s = x.shape[:2]
    q = SSD_CHUNK
    nc = s // q
    xs = (x * dt[..., None]).reshape(b, nc, q, SSD_GROUPS, SSD_HPG, SSD_HEADDIM)
    da = (dt * a).reshape(b, nc, q, SSD_GROUPS, SSD_HPG)
    bm = bm.reshape(b, nc, q, SSD_GROUPS, SSD_STATE)
    cm = cm.reshape(b, nc, q, SSD_GROUPS, SSD_STATE)
    acum = jnp.cumsum(da, axis=2)
    diff = acum[:, :, :, None] - acum[:, :, None, :]
    mask = jnp.tril(jnp.ones((q, q), dtype=bool))[:, :, None, None]
    decay = jnp.exp(jnp.where(mask, diff, -jnp.inf))
    cb = jnp.einsum('bcqgn,bcsgn->bcqsg', cm, bm)
    y_diag = jnp.einsum('bcqsge,bcsgep->bcqgep', cb[..., None] * decay, xs)
    decay_states = jnp.exp(acum[:, :, -1:] - acum)
    states = jnp.einsum('bcqgn,bcqgep->bcgepn', bm, xs * decay_states[..., None])
    chunk_decay = jnp.exp(acum[:, :, -1])

    def step(h, inp):
        st, dec = inp
        return h * dec[..., None, None] + st, h

    h0 = jnp.zeros((b, SSD_GROUPS, SSD_HPG, SSD_HEADDIM, SSD_STATE), jnp.float32)
    _, prev = lax.scan(step, h0, (jnp.moveaxis(states, 1, 0), jnp.moveaxis(chunk_decay, 1, 0)))
    prev = jnp.moveaxis(prev, 0, 1)
    y_off = jnp.einsum('bcqgn,bcgepn->bcqgep', cm, prev) * jnp.exp(acum)[..., None]
    return (y_diag + y_off).reshape(b, s, SSD_HEADS, SSD_HEADDIM)


def bidirectional_ssd(z, xbc, dt_raw, conv_w, conv_b, dt_bias, a_log, d_skip, norm_g):
    b, s, _ = z.shape
    xbc = jax.nn.silu(centred_dwconv(xbc, conv_w, conv_b)).astype(jnp.float32)
    xc = xbc[..., :SSD_INNER].reshape(b, s, SSD_HEADS, SSD_HEADDIM)
    bm = xbc[..., SSD_INNER:SSD_INNER + SSD_GN].reshape(b, s, SSD_GROUPS, SSD_STATE)
    cm = xbc[..., SSD_INNER + SSD_GN:].reshape(b, s, SSD_GROUPS, SSD_STATE)
    dt = jax.nn.softplus(dt_raw.astype(jnp.float32).reshape(b, s, 2, SSD_HEADS) + dt_bias.astype(jnp.float32))
    a = -jnp.exp(a_log.astype(jnp.float32))
    y_fwd = ssd_scan(xc, dt[:, :, 0], a[0], bm, cm)
    y_bwd = jnp.flip(ssd_scan(jnp.flip(xc, 1), jnp.flip(dt[:, :, 1], 1), a[1], jnp.flip(bm, 1), jnp.flip(cm, 1)), 1)
    y = y_fwd + y_bwd + xc * d_skip.astype(jnp.float32)[:, None]
    y = y.reshape(b, s, SSD_INNER) * jax.nn.silu(z.astype(jnp.float32))
    gsz = SSD_INNER // SSD_GROUPS
    y = rms_norm(y.reshape(b, s, SSD_GROUPS, gsz), norm_g.reshape(SSD_GROUPS, gsz)).reshape(b, s, SSD_INNER)
    return y.astype(z.dtype)


def memory_cross_attention(h, mem, wq, wkv, wo):
    b, s, _ = h.shape
    m = mem.shape[1]
    q = (h @ wq).reshape(b, s, XA_HEADS, XA_HEADDIM)
    k, v = jnp.split(mem @ wkv, 2, axis=-1)
    k = k.reshape(b, m, XA_HEADS, XA_HEADDIM)
    v = v.reshape(b, m, XA_HEADS, XA_HEADDIM)
    scores = jnp.einsum('bshd,bmhd->bhsm', q.astype(jnp.float32), k.astype(jnp.float32)) * (XA_HEADDIM ** -0.5)
    probs = jax.nn.softmax(scores, axis=-1).astype(v.dtype)
    o = jnp.einsum('bhsm,bmhd->bshd', probs, v).reshape(b, s, D_MODEL)
    return o @ wo


def encoder_trunk(x, mem, p):
    bsz, s, _ = x.shape
    for l in range(DEPTH):
        x = x + 0.5 * swiglu_ffn(rms_norm(x, p['ffn1_norm'][l]), p['ffn1_w13'][l], p['ffn1_w2'][l])
        n = rms_norm(x, p['mix_norm'][l])
        u, v, pin, z, xbc, dt_raw = jnp.split(n @ p['w_in'][l], IN_SPLITS, axis=-1)
        a_out = spatial_gating(jax.nn.gelu(u), jax.nn.gelu(v), p['sgu_ln_g'][l], p['sgu_ln_b'][l], p['sgu_ws'][l], p['sgu_bias'][l])
        b_out = multiscale_pool(pin, p['pool_w'][l], p['pool_scale'][l])
        c_out = bidirectional_ssd(z, xbc, dt_raw, p['conv_w'][l], p['conv_b'][l], p['dt_bias'][l], p['a_log'][l], p['d_skip'][l], p['ssd_norm'][l])
        gates = jax.nn.sigmoid(n @ p['w_gate'][l] + p['b_gate'][l]).reshape(bsz, s, N_BRANCH, D_MODEL)
        merged = (gates[:, :, 0] * (a_out @ p['w_branch_a'][l])
                  + gates[:, :, 1] * (b_out @ p['w_branch_b'][l])
                  + gates[:, :, 2] * (c_out @ p['w_branch_c'][l]))
        x = x + merged @ p['w_out'][l]
        x = x + memory_cross_attention(rms_norm(x, p['xattn_norm'][l]), rms_norm(mem, p['mem_norm'][l]), p['xattn_wq'][l], p['xattn_wkv'][l], p['xattn_wo'][l])
        x = x + 0.5 * swiglu_ffn(rms_norm(x, p['ffn2_norm'][l]), p['ffn2_w13'][l], p['ffn2_w2'][l])
    return rms_norm(x, p['final_norm'])


def setup_inputs(seed: int = 0) -> dict:
    key = jax.random.key(seed)
    ks = iter(jax.random.split(key, 48))
    L = DEPTH
    D = D_MODEL

    def normal(shape, scale):
        return jax.random.normal(next(ks), shape, jnp.float32) * scale

    def gain(shape):
        return 1.0 + 0.02 * jax.random.normal(next(ks), shape, jnp.float32)

    out = {}
    out['x_prompt'] = normal((BATCH, SEQ, D), 1.0)
    out['x_sample'] = normal((DEC_BATCH, DEC_SEQ, D), 1.0)
    out['mem_prompt'] = normal((BATCH, N_MEM, D), 1.0)
    out['mem_sample'] = normal((DEC_BATCH, N_MEM, D), 1.0)
    out['ffn1_norm'] = gain((L, D))
    out['ffn1_w13'] = normal((L, D, 2 * D_FF), D ** -0.5)
    out['ffn1_w2'] = normal((L, D_FF, D), D_FF ** -0.5)
    out['mix_norm'] = gain((L, D))
    out['w_in'] = normal((L, D, N_IN), D ** -0.5)
    out['w_gate'] = normal((L, D, N_BRANCH * D), D ** -0.5)
    out['b_gate'] = normal((L, N_BRANCH * D), 0.02)
    out['sgu_ln_g'] = gain((L, SGU_WIDTH))
    out['sgu_ln_b'] = normal((L, SGU_WIDTH), 0.02)
    out['sgu_ws'] = normal((L, SGU_GROUPS, SGU_CHUNK, SGU_CHUNK), SGU_CHUNK ** -0.5)
    out['sgu_bias'] = gain((L, SGU_GROUPS, SGU_CHUNK))
    out['pool_w'] = normal((L, POOL_GROUPS, POOL_GDIM, POOL_GDIM), POOL_GDIM ** -0.5)
    out['pool_scale'] = gain((L, POOL_WIDTH))
    out['conv_w'] = normal((L, SSD_CONV, SSD_CONV_DIM), SSD_CONV ** -0.5)
    out['conv_b'] = normal((L, SSD_CONV_DIM), 0.02)
    dt0 = jnp.exp(jax.random.uniform(next(ks), (L, 2, SSD_HEADS), jnp.float32) * (math.log(DT_MAX) - math.log(DT_MIN)) + math.log(DT_MIN))
    out['dt_bias'] = dt0 + jnp.log(-jnp.expm1(-dt0))
    out['a_log'] = jnp.log(jax.random.uniform(next(ks), (L, 2, SSD_HEADS), jnp.float32, minval=1.0, maxval=16.0))
    out['d_skip'] = gain((L, SSD_HEADS))
    out['ssd_norm'] = gain((L, SSD_INNER))
    out['w_branch_a'] = normal((L, SGU_WIDTH, D), SGU_WIDTH ** -0.5)
    out['w_branch_b'] = normal((L, POOL_WIDTH, D), POOL_WIDTH ** -0.5)
    out['w_branch_c'] = normal((L, SSD_INNER, D), SSD_INNER ** -0.5)
    out['w_out'] = normal((L, D, D), D ** -0.5)
    out['xattn_norm'] = gain((L, D))
    out['mem_norm'] = gain((L, D))
    out['xattn_wq'] = normal((L, D, D), D ** -0.5)
    out['xattn_wkv'] = normal((L, D, 2 * D), D ** -0.5)
    out['xattn_wo'] = normal((L, D, D), D ** -0.5)
    out['ffn2_norm'] = gain((L, D))
    out['ffn2_w13'] = normal((L, D, 2 * D_FF), D ** -0.5)
    out['ffn2_w2'] = normal((L, D_FF, D), D_FF ** -0.5)
    out['final_norm'] = gain((D,))
    return out


def reference(x_prompt, x_sample, mem_prompt, mem_sample, ffn1_norm, ffn1_w13, ffn1_w2, mix_norm, w_in, w_gate, b_gate, sgu_ln_g, sgu_ln_b, sgu_ws, sgu_bias, pool_w, pool_scale, conv_w, conv_b, dt_bias, a_log, d_skip, ssd_norm, w_branch_a, w_branch_b, w_branch_c, w_out, xattn_norm, mem_norm, xattn_wq, xattn_wkv, xattn_wo, ffn2_norm, ffn2_w13, ffn2_w2, final_norm):
    p = {
        'ffn1_norm': ffn1_norm, 'ffn1_w13': ffn1_w13, 'ffn1_w2': ffn1_w2,
        'mix_norm': mix_norm, 'w_in': w_in, 'w_gate': w_gate, 'b_gate': b_gate,
        'sgu_ln_g': sgu_ln_g, 'sgu_ln_b': sgu_ln_b, 'sgu_ws': sgu_ws, 'sgu_bias': sgu_bias,
        'pool_w': pool_w, 'pool_scale': pool_scale,
        'conv_w': conv_w, 'conv_b': conv_b, 'dt_bias': dt_bias, 'a_log': a_log, 'd_skip': d_skip, 'ssd_norm': ssd_norm,
        'w_branch_a': w_branch_a, 'w_branch_b': w_branch_b, 'w_branch_c': w_branch_c, 'w_out': w_out,
        'xattn_norm': xattn_norm, 'mem_norm': mem_norm, 'xattn_wq': xattn_wq, 'xattn_wkv': xattn_wkv, 'xattn_wo': xattn_wo,
        'ffn2_norm': ffn2_norm, 'ffn2_w13': ffn2_w13, 'ffn2_w2': ffn2_w2,
        'final_norm': final_norm,
    }
    y_prompt = encoder_trunk(x_prompt, mem_prompt, p)
    y_sample = encoder_trunk(x_sample, mem_sample, p)
    return (y_prompt, y_sample)
```

```python
import numpy as np
import concourse.bass as bass
import concourse.mybir as mybir
from concourse.bass_utils import run_bass_kernel_spmd

F32 = mybir.dt.float32
BF16 = mybir.dt.bfloat16
AF = mybir.ActivationFunctionType
ALU = mybir.AluOpType
AX = mybir.AxisListType

D = 1024
DFF = 2816
NMEM = 256
NIN = 3872
EPS = 1e-6
T = 512
CH = 128
WSLOT = 4096
NWR = 4
SAME_ENG_SYNC = True
import os
BSTOP = int(os.environ.get('BSTOP', '0'))

WNAMES = ['ffn1_w13', 'ffn1_w2', 'w_in', 'w_gate', 'w_branch_a', 'w_branch_b', 'w_branch_c', 'w_out',
          'xattn_wq', 'xattn_wkv', 'xattn_wo', 'ffn2_w13', 'ffn2_w2']
WSHAPE = {'ffn1_w13': (D, 2 * DFF), 'ffn1_w2': (DFF, D), 'w_in': (D, NIN), 'w_gate': (D, 3 * D),
          'w_branch_a': (512, D), 'w_branch_b': (512, D), 'w_branch_c': (D, D), 'w_out': (D, D),
          'xattn_wq': (D, D), 'xattn_wkv': (D, 2 * D), 'xattn_wo': (D, D), 'ffn2_w13': (D, 2 * DFF),
          'ffn2_w2': (DFF, D)}
SMALL = ['ffn1_norm', 'mix_norm', 'b_gate', 'sgu_ln_g', 'sgu_ln_b', 'sgu_ws', 'sgu_bias', 'pool_w', 'pool_scale',
         'conv_w', 'conv_b', 'dt_bias', 'a_log', 'd_skip', 'ssd_norm', 'xattn_norm', 'mem_norm', 'ffn2_norm',
         'final_norm']
SMALL_SHAPE = {'ffn1_norm': (D,), 'mix_norm': (D,), 'b_gate': (3 * D,), 'sgu_ln_g': (512,), 'sgu_ln_b': (512,),
               'sgu_ws': (4, 128, 128), 'sgu_bias': (4, 128), 'pool_w': (4, 128, 128), 'pool_scale': (512,),
               'conv_w': (4, 1280), 'conv_b': (1280,), 'dt_bias': (2, 16), 'a_log': (2, 16), 'd_skip': (16,),
               'ssd_norm': (D,), 'xattn_norm': (D,), 'mem_norm': (D,), 'ffn2_norm': (D,)}


def wchunks(name):
    K, N = WSHAPE[name]
    KC = K // 128
    if name.endswith('w13'):
        return [(KC, 512, [(c * 256, 256, 0), (DFF + c * 256, 256, 256)]) for c in range(11)]
    if name.endswith('w2'):
        return [(KC, 128, [(c * 128, 128, 0)]) for c in range(8)]
    if name == 'w_in':
        out = [(KC, 512, [(c * 512, 512, 0)]) for c in range(7)]
        out.append((KC, 288, [(3584, 288, 0)]))
        return out
    if name in ('w_branch_a', 'w_branch_b'):
        return [(KC, 1024, [(0, 1024, 0)])]
    return [(KC, 512, [(c * 512, 512, 0)]) for c in range(N // 512)]


def host_consts():
    i = np.arange(128)
    k = i[:, None]
    q = i[None, :]
    blocks = {}
    blocks['ident'] = (k == q)
    blocks['tri_f'] = (k <= q)
    blocks['tri_b'] = (k >= q)
    blocks['su'] = (k > q)
    blocks['sl'] = (k < q)
    blocks['ones'] = np.ones((128, 128))
    wins = (2, 4, 8, 16)
    S3 = 3 * 128
    for g, w in enumerate(wins):
        def band(seq_len, t_off):
            pos = np.arange(seq_len)
            lo = np.clip(pos - w // 2, 0, seq_len)
            hi = np.clip(pos + w - w // 2, 0, seq_len)
            cnt = hi - lo
            out = {}
            for nb in (-1, 0, 1):
                m = np.zeros((128, 128))
                for tt in range(128):
                    tg = t_off + tt
                    for sg in range(lo[tg], hi[tg]):
                        sr = sg - (t_off + nb * 128)
                        if 0 <= sr < 128:
                            m[sr, tt] += 1.0
                    if nb == 0:
                        m[tt, tt] -= cnt[tg]
                out[nb] = m
            return out, 1.0 / cnt[t_off:t_off + 128]
        bm, rcm = band(S3, 128)
        bf, rcf = band(S3, 0)
        bl, rcl = band(S3, 256)
        blocks['band_%d_m-1' % g] = bm[-1]
        blocks['band_%d_m0' % g] = bm[0]
        blocks['band_%d_m1' % g] = bm[1]
        blocks['band_%d_f0' % g] = bf[0]
        blocks['band_%d_l0' % g] = bl[0]
        blocks['rc_%d_m' % g] = np.broadcast_to(rcm[None, :], (128, 128))
        blocks['rc_%d_f' % g] = np.broadcast_to(rcf[None, :], (128, 128))
        blocks['rc_%d_l' % g] = np.broadcast_to(rcl[None, :], (128, 128))
    fnames = ['ident', 'tri_f', 'tri_b', 'su', 'sl', 'ones']
    for g in range(4):
        fnames += ['rc_%d_m' % g, 'rc_%d_f' % g, 'rc_%d_l' % g]
    bnames = ['ident', 'ones', 'tri_f', 'tri_b', 'su', 'sl'] + [n for n in blocks if n.startswith('band_')]
    af = np.concatenate([np.asarray(blocks[n], dtype=np.float32) for n in fnames], axis=1)
    ab = np.concatenate([np.asarray(blocks[n], dtype=np.float32) for n in bnames], axis=1)
    return fnames, bnames, np.ascontiguousarray(af), np.ascontiguousarray(ab)


CONST_NAMES, CONSTB_NAMES, CONSTF_ARR, CONSTB_ARR = host_consts()
NCONST = CONSTF_ARR.shape[1]
NCONSTB = CONSTB_ARR.shape[1]


class Ctx:
    def __init__(self, nc):
        self.nc = nc
        self.E = {'pe': nc.tensor, 'act': nc.scalar, 'dve': nc.vector, 'pool': nc.gpsimd, 'sp': nc.sync}
        self.sem = {}
        self.cnt = {}
        self.seen = {e: {} for e in self.E}
        self.lastw = {}
        self.rd = {}
        self.nins = 0
        for e in self.E:
            self.newsem('E_' + e)

    def newsem(self, name):
        self.sem[name] = self.nc.alloc_semaphore(name)
        self.cnt[name] = 0

    def _need(self, r, w):
        need = {}

        def add(tok):
            if tok is None:
                return
            s, v = tok
            if need.get(s, 0) < v:
                need[s] = v
        for k in r:
            add(self.lastw.get(k))
        for k in w:
            add(self.lastw.get(k))
            for s, v in self.rd.get(k, {}).items():
                add((s, v))
        return need

    def _wait(self, e, need):
        seen = self.seen[e]
        own = 'E_' + e
        for s, v in need.items():
            if s == own and (e == 'pe' or not SAME_ENG_SYNC):
                continue
            if seen.get(s, 0) < v:
                self.E[e].wait_ge(self.sem[s], v)
                seen[s] = v
                self.nins += 1

    def _mark(self, tok, r, w):
        s, v = tok
        for k in r:
            d = self.rd.setdefault(k, {})
            if d.get(s, 0) < v:
                d[s] = v
        for k in w:
            self.lastw[k] = tok
            self.rd[k] = {}

    def op(self, e, fn, r=(), w=(), inc=True):
        self._wait(e, self._need(r, w))
        ins = fn()
        self.nins += 1
        s = 'E_' + e
        if inc:
            ins.then_inc(self.sem[s], 1)
            self.cnt[s] += 1
            tok = (s, self.cnt[s])
        else:
            tok = (s, self.cnt[s] + 1)
        self._mark(tok, r, w)
        return ins

    def dma(self, q, out, in_, sem, r=(), w=(), **kw):
        if sem not in self.sem:
            self.newsem(sem)
        self._wait(q, self._need(r, w))
        ins = self.E[q].dma_start(out=out, in_=in_, **kw)
        ins.then_inc(self.sem[sem], 16)
        self.nins += 1
        self.cnt[sem] += 16
        self._mark((sem, self.cnt[sem]), r, w)

    def final_wait(self, e):
        need = {}
        for s in self.sem:
            if self.cnt[s] > 0:
                need[s] = self.cnt[s]
        self._wait(e, need)


class Buf:
    def __init__(self, t, key):
        self.t = t
        self.k = key

    def __getitem__(self, idx):
        return self.t[idx]


def bcast_ap(ap, dims):
    return bass.AP(ap.tensor, ap.offset, dims)


class Builder:
    def __init__(self, depth, seqs, stop=None):
        self.stop = stop
        self.depth = depth
        self.seqs = list(seqs)
        self.NT = sum(seqs)
        self.NCHK = self.NT // CH
        self.tiles = []
        t0 = 0
        for si, L in enumerate(seqs):
            assert L % T == 0
            for j in range(L // T):
                self.tiles.append((si, t0 + j * T, j == 0, j == L // T - 1))
            t0 += L
        self.seq_start = [sum(seqs[:i]) for i in range(len(seqs))]

    def build(self):
        nc = bass.Bass("TRN2", target_bir_lowering=False)
        self.nc = nc
        c = Ctx(nc)
        self.c = c
        L = self.depth
        NT = self.NT
        nseq = len(self.seqs)
        dr = {}
        dr['x'] = nc.dram_tensor('x', [NT, D], F32, kind='ExternalInput').ap()
        dr['mem'] = nc.dram_tensor('mem', [nseq, NMEM, D], F32, kind='ExternalInput').ap()
        dr['consts'] = nc.dram_tensor('consts', [128, NCONST], F32, kind='ExternalInput').ap()
        dr['constsb'] = nc.dram_tensor('constsb', [128, NCONSTB], F32, kind='ExternalInput').ap()
        for n in WNAMES:
            dr[n] = nc.dram_tensor(n, [L] + list(WSHAPE[n]), F32, kind='ExternalInput').ap()
        for n in SMALL:
            shp = ([L] + list(SMALL_SHAPE[n])) if n != 'final_norm' else [D]
            dr[n] = nc.dram_tensor(n, shp, F32, kind='ExternalInput').ap()
        dr['y'] = nc.dram_tensor('y', [NT, D], F32, kind='ExternalOutput').ap()
        sc = {}
        for n in WNAMES:
            ch = wchunks(n)
            sc[n] = nc.dram_tensor('s_' + n, [L, len(ch), 128, WSLOT], BF16, kind='Internal').ap()
        sc['xT'] = nc.dram_tensor('s_xT', [8, 128, NT], F32, kind='Internal').ap()
        sc['mp'] = nc.dram_tensor('s_mp', [8, 128, NT], F32, kind='Internal').ap()
        sc['g12'] = nc.dram_tensor('s_g12', [16, 128, NT], BF16, kind='Internal').ap()
        sc['zs'] = nc.dram_tensor('s_zs', [NT, D], BF16, kind='Internal').ap()
        sc['pin'] = nc.dram_tensor('s_pin', [NT, 512], BF16, kind='Internal').ap()
        sc['xbc'] = nc.dram_tensor('s_xbc', [10, 128, NT], BF16, kind='Internal').ap()
        sc['dtr'] = nc.dram_tensor('s_dtr', [NT, 32], F32, kind='Internal').ap()
        sc['yl'] = nc.dram_tensor('s_yl', [NT, D], F32, kind='Internal').ap()
        sc['cm'] = nc.dram_tensor('s_cm', [2, 128, NT], BF16, kind='Internal').ap()
        sc['e4'] = nc.dram_tensor('s_e4', [NT, 64], F32, kind='Internal').ap()
        sc['S'] = nc.dram_tensor('s_S', [self.NCHK, 2, 128, 512], F32, kind='Internal').ap()
        sc['dec'] = nc.dram_tensor('s_dec', [self.NCHK, 2, 128, 8], F32, kind='Internal').ap()
        sc['hp'] = nc.dram_tensor('s_hp', [self.NCHK, 2, 128, 512], BF16, kind='Internal').ap()
        self.dr = dr
        self.sc = sc

        def sb(name, shape, dt):
            return Buf(nc.alloc_sbuf_tensor(name, shape, dt), name)
        self.sb = sb
        self.PS = nc.alloc_psum_tensor('psum', [128, 8, 512], F32)
        self.psn = 0
        self.CF = sb('CF', [128, NCONST], F32)
        self.CB = sb('CB', [128, NCONSTB], BF16)
        self.X = sb('X', [128, 8, T], F32)
        self.SQ = sb('SQ', [128, 8, T], BF16)
        self.XN = sb('XN', [128, 8, T], BF16)
        self.RS = sb('RS', [128, T], F32)
        self.H = sb('H', [128, 22, T], BF16)
        self.SA = [sb('SA%d' % i, [128, T], F32) for i in range(2)]
        self.WR = [sb('WR%d' % i, [128, WSLOT], BF16) for i in range(NWR)]
        self.BIG = sb('BIG', [128, 12800], F32)
        self.ST = [sb('ST%d' % i, [128, 512], BF16) for i in range(6)]
        self.sti = 0
        self.LC = sb('LC', [128, 4864], F32)
        self.LCB = sb('LCB', [128, 1024], BF16)
        self.SM = sb('SM', [128, 1024], F32)
        self.wplan = []
        self.wpos = 0
        self.wissued = 0

        self.KT = sb('KT', [128, 8, 256], BF16)
        self.VV = sb('VV', [128, 2, 1024], BF16)
        c.dma('sp', self.CF[:], dr['consts'][:, :], 'd_const', w=[self.CF.k])
        c.dma('sp', self.BIG[:, 0:NCONSTB], dr['constsb'][:, :], 'd_const', w=[self.BIG.k])
        c.op('dve', lambda: nc.vector.tensor_copy(out=self.CB[:], in_=self.BIG[:, 0:NCONSTB]), r=[self.BIG.k],
             w=[self.CB.k])
        c.op('dve', lambda: nc.vector.memset(self.SM[:, 1010:1011], EPS), w=[('SMc0',)])
        c.op('dve', lambda: nc.vector.memset(self.SM[:, 1011:1012], 1024.0 * EPS), w=[('SMc1',)])
        c.op('dve', lambda: nc.vector.memset(self.SM[:, 1012:1013], 1.0), w=[('SMc2',)])
        self.barrier()
        stop = self.stop

        def done(tag):
            if stop == tag:
                self.barrier()
                return True
            return False
        self.prep_weights()
        self.barrier()
        if done('prep'):
            return nc
        for l in range(L):
            self.layer_consts(l)
            self.barrier()
            if done('lc'):
                return nc
            self.plan_layer(l)
            for ti in range(len(self.tiles)):
                self.sweep_A(l, ti)
                if done('A0'):
                    return nc
            self.barrier()
            if done('A'):
                return nc
            for ti in range(len(self.tiles)):
                self.sweep_B(l, ti)
            self.barrier()
            if done('B'):
                return nc
            self.recurrence(l)
            self.barrier()
            if done('R'):
                return nc
            for si in range(len(self.seqs)):
                self.mem_kv(l, si)
                self.barrier()
                if done('KV'):
                    return nc
                for ti in range(len(self.tiles)):
                    if self.tiles[ti][0] == si:
                        self.sweep_C(l, ti)
            assert self.wpos == len(self.wplan), (self.wpos, len(self.wplan))
            self.barrier()
        c.final_wait('pool')
        return nc

    def cf(self, name):
        i = CONST_NAMES.index(name)
        return self.CF[:, i * 128:(i + 1) * 128]

    def barrier(self):
        c = self.c
        need = {s: v for s, v in c.cnt.items() if v > 0}
        for e in c.E:
            c._wait(e, dict(need))
        c.lastw = {k: v for k, v in c.lastw.items() if isinstance(k, tuple) and k[0] == 'wsc'}
        c.rd = {}

    def link(self, keys):
        c, nc = self.c, self.nc
        c.op('dve', lambda: nc.vector.memset(self.SM[:, 1000:1001], 0.0), w=list(keys) + [('SMz',)])

    def eps_ap(self, which):
        return self.SM[:, 1010 + which:1011 + which]

    def cb(self, name):
        i = CONSTB_NAMES.index(name)
        return self.CB[:, i * 128:(i + 1) * 128]

    def ps(self, n=1):
        if self.psn + n > 8:
            self.psn = 0
        b = self.psn
        self.psn = (self.psn + n) % 8
        keys = [('ps', b + i) for i in range(n)]
        return b, keys

    def stage(self):
        s = self.ST[self.sti % len(self.ST)]
        self.sti += 1
        return s

    def prep_weights(self):
        c, nc = self.c, self.nc
        n = 0
        for name in WNAMES:
            chs = wchunks(name)
            for l in range(self.depth):
                for ci, (KC, wc, ranges) in enumerate(chs):
                    dst = self.sc[name][l, ci]
                    for (c0, ncol, off) in ranges:
                        src = self.dr[name][l, :, c0:c0 + ncol].rearrange('(kc p) n -> p kc n', p=128)
                        d = dst[:, 0:KC * wc].rearrange('p (kc n) -> p kc n', kc=KC)[:, :, off:off + ncol]
                        sem = 'd_prep%d' % (n % 8)
                        n += 1
                        c.dma('pool', d, src, sem, w=[('wsc', name, l, ci)])

    def plan_layer(self, l):
        plan = []
        for ti in range(len(self.tiles)):
            for nm in ('ffn1_w13', 'ffn1_w2', 'w_gate', 'w_in', 'w_branch_a'):
                for ci in range(len(wchunks(nm))):
                    plan.append((nm, l, ci))
        for ti in range(len(self.tiles)):
            plan.append(('w_branch_b', l, 0))
        for si in range(len(self.seqs)):
            for ci in range(4):
                plan.append(('xattn_wkv', l, ci))
            for ti in range(len(self.tiles)):
                if self.tiles[ti][0] != si:
                    continue
                for nm in ('w_branch_c', 'w_out', 'xattn_wq', 'xattn_wo', 'ffn2_w13', 'ffn2_w2'):
                    for ci in range(len(wchunks(nm))):
                        plan.append((nm, l, ci))
        self.wplan.extend(plan)

    def _issue_w(self):
        i = self.wissued
        nm, l, ci = self.wplan[i]
        KC, wc, _ = wchunks(nm)[ci]
        slot = self.WR[i % NWR]
        n = KC * wc
        self.c.dma('sp', slot[:, 0:n], self.sc[nm][l, ci, :, 0:n], 'd_w%d' % (i % NWR),
                   r=[('wsc', nm, l, ci)], w=[slot.k])
        self.wissued += 1

    def wget(self, nm, l, ci):
        assert self.wplan[self.wpos] == (nm, l, ci), (self.wplan[self.wpos], (nm, l, ci))
        while self.wissued < min(len(self.wplan), self.wpos + NWR - 1) or self.wissued <= self.wpos:
            self._issue_w()
        slot = self.WR[self.wpos % NWR]
        self.wpos += 1
        KC, wc, _ = wchunks(nm)[ci]
        return slot, slot[:, 0:KC * wc].rearrange('p (kc n) -> p kc n', kc=KC)

    def layer_consts(self, l):
        c, nc, dr = self.c, self.nc, self.dr
        LC = self.LC
        lc = {}
        off = [0]

        def alloc(n):
            o = off[0]
            off[0] += n
            return o

        def load_bc(src_ap, n, key):
            o = alloc(n)
            src = bass.AP(src_ap.tensor, src_ap.offset, [[0, 128], [1, n]])
            c.dma('sp', LC[:, o:o + n], src, 'd_lc', w=[LC.k])
            lc[key] = o
        BG = self.BIG
        rows = 0
        pp = [('ffn1_norm', 8), ('mix_norm', 8), ('xattn_norm', 8), ('ffn2_norm', 8), ('b_gate', 24),
              ('pool_scale', 4), ('conv_b', 10)]
        for name, nk in pp:
            c.dma('sp', BG[rows:rows + nk, 2048:2176], dr[name][l].rearrange('(kc p) -> kc p', p=128), 'd_lcpp',
                  w=[('BGpp',)])
            lc[name] = alloc(nk)
            rows += nk
        c.dma('sp', BG[rows:rows + 40, 2048:2176], dr['conv_w'][l].rearrange('k (f p) -> (k f) p', p=128), 'd_lcpp',
              w=[('BGpp',)])
        lc['conv_w'] = alloc(40)
        rows += 40
        b0, keys0 = self.ps(1)
        c.op('pe', lambda: nc.tensor.transpose(out=self.PS[:, b0, 0:rows], in_=BG[0:rows, 2048:2176],
                                               identity=self.cf('ident')[0:rows, 0:rows]),
             r=[('BGpp',), self.CF.k], w=keys0)
        c.op('dve', lambda: nc.vector.tensor_copy(out=LC[:, 0:rows], in_=self.PS[:, b0, 0:rows]), r=keys0, w=[LC.k])
        load_bc(dr['sgu_ln_g'][l], 512, 'lg')
        load_bc(dr['sgu_ln_b'][l], 512, 'lb')
        load_bc(dr['sgu_bias'][l].rearrange('g t -> (g t)'), 512, 'sgu_bias')
        load_bc(dr['dt_bias'][l].rearrange('a h -> (a h)'), 32, 'dt_bias')
        load_bc(dr['a_log'][l].rearrange('a h -> (a h)'), 32, 'a_log')
        load_bc(dr['d_skip'][l], 16, 'd_skip')
        load_bc(dr['ssd_norm'][l], 1024, 'ssd_norm')
        load_bc(dr['mem_norm'][l], 1024, 'mem_norm')
        load_bc(dr['final_norm'], 1024, 'final_norm')
        for k in ('ffn1_norm', 'mix_norm', 'xattn_norm', 'ffn2_norm'):
            o = lc[k]
            c.op('dve', lambda o=o: nc.vector.tensor_scalar(out=LC[:, o:o + 8], in0=LC[:, o:o + 8], scalar1=32.0,
                                                            scalar2=None, op0=ALU.mult), r=[LC.k], w=[LC.k])
        o = lc['a_log']
        c.op('act', lambda: nc.scalar.activation(out=LC[:, o:o + 32], in_=LC[:, o:o + 32], func=AF.Exp),
             r=[LC.k], w=[LC.k])
        c.op('dve', lambda: nc.vector.tensor_scalar(out=LC[:, o:o + 32], in0=LC[:, o:o + 32], scalar1=-1.0,
                                                    scalar2=None, op0=ALU.mult), r=[LC.k], w=[LC.k])
        lc['a'] = o
        o = 0
        c.dma('sp', BG[:, o:o + 512].rearrange('p (g s) -> p g s', g=4),
              dr['sgu_ws'][l].rearrange('g t s -> t g s'), 'd_lc3', w=[BG.k])
        b, keys = self.ps(1)
        for g in range(4):
            c.op('pe', lambda g=g: nc.tensor.transpose(out=self.PS[:, b, g * 128:(g + 1) * 128],
                                                       in_=BG[:, o + g * 128:o + (g + 1) * 128],
                                                       identity=self.cf('ident')),
                 r=[BG.k, self.CF.k], w=keys, inc=(g == 3))
        c.op('dve', lambda: nc.vector.tensor_copy(out=self.LCB[:, 0:512], in_=self.PS[:, b, :]),
             r=keys, w=[self.LCB.k])
        o2 = 512
        c.dma('sp', BG[:, o2:o2 + 512].rearrange('p (g e) -> p g e', g=4),
              dr['pool_w'][l].rearrange('g d e -> d g e'), 'd_lc2', w=[('BG2',)])
        c.op('dve', lambda: nc.vector.tensor_copy(out=self.LCB[:, 512:1024], in_=BG[:, o2:o2 + 512]),
             r=[('BG2',)], w=[self.LCB.k])
        assert off[0] <= 4864
        self.lc = lc

    def rmsnorm_fm(self, gkey):
        c, nc = self.c, self.nc
        X, SQ, XN, RS = self.X, self.SQ, self.XN, self.RS
        for hh in range(2):
            c.op('act', lambda hh=hh: nc.scalar.activation(out=SQ[:, hh * 4:(hh + 1) * 4, :],
                                                           in_=X[:, hh * 4:(hh + 1) * 4, :], func=AF.Square),
                 r=[X.k], w=[SQ.k])
        b, keys = self.ps(1)
        for kc in range(8):
            c.op('pe', lambda kc=kc: nc.tensor.matmul(self.PS[:, b, :], self.cb('ones'), SQ[:, kc, :],
                                                      start=(kc == 0), stop=(kc == 7)),
                 r=[SQ.k, self.CB.k], w=keys, inc=(kc == 7))
        c.op('act', lambda: nc.scalar.activation(out=RS[:], in_=self.PS[:, b, :], func=AF.Sqrt, bias=self.eps_ap(1),
                                                 scale=1.0), r=keys, w=[RS.k])
        c.op('dve', lambda: nc.vector.reciprocal(out=RS[:], in_=RS[:]), r=[RS.k], w=[RS.k])
        o = self.lc[gkey]
        for kc in range(8):
            c.op('dve', lambda kc=kc: nc.vector.scalar_tensor_tensor(out=XN[:, kc, :], in0=X[:, kc, :],
                                                                     scalar=self.LC[:, o + kc:o + kc + 1],
                                                                     in1=RS[:], op0=ALU.mult, op1=ALU.mult),
                 r=[X.k, RS.k, self.LC.k], w=[XN.k])

    def ffn(self, l, pref):
        c, nc = self.c, self.nc
        X, XN, H = self.X, self.XN, self.H
        for ci in range(11):
            slot, W = self.wget(pref + '_w13', l, ci)
            for jj in range(2):
                j = ci * 2 + jj
                b, keys = self.ps(2)
                for half in range(2):
                    col = half * 256 + jj * 128
                    for kc in range(8):
                        c.op('pe', lambda kc=kc, col=col, half=half: nc.tensor.matmul(
                            self.PS[:, b + half, :], W[:, kc, col:col + 128], XN[:, kc, :],
                            start=(kc == 0), stop=(kc == 7)),
                            r=[slot.k, XN.k], w=[keys[half]], inc=(kc == 7))
                sa = self.SA[j % 2]
                c.op('act', lambda: nc.scalar.activation(out=sa[:], in_=self.PS[:, b, :], func=AF.Silu),
                     r=[keys[0]], w=[sa.k])
                c.op('dve', lambda j=j: nc.vector.tensor_tensor(out=H[:, j, :], in0=sa[:], in1=self.PS[:, b + 1, :],
                                                                op=ALU.mult), r=[sa.k, keys[1]], w=[H.k])
        for dd in range(8):
            slot, W = self.wget(pref + '_w2', l, dd)
            b, keys = self.ps(1)
            for kf in range(22):
                c.op('pe', lambda kf=kf: nc.tensor.matmul(self.PS[:, b, :], W[:, kf, :], H[:, kf, :],
                                                          start=(kf == 0), stop=(kf == 21)),
                     r=[slot.k, H.k], w=keys, inc=(kf == 21))
            c.op('dve', lambda dd=dd: nc.vector.scalar_tensor_tensor(out=X[:, dd, :], in0=self.PS[:, b, :], scalar=0.5,
                                                                     in1=X[:, dd, :], op0=ALU.mult, op1=ALU.add),
                 r=keys + [X.k], w=[X.k])

    def lin_fm(self, W, slot, kcs, rhs, rhs_key, ncol, evac):
        c, nc = self.c, self.nc
        for f in range(ncol // 128):
            b, keys = self.ps(1)
            for kc in range(kcs):
                c.op('pe', lambda kc=kc: nc.tensor.matmul(self.PS[:, b, :], W[:, kc, f * 128:(f + 1) * 128],
                                                          rhs[:, kc, :], start=(kc == 0), stop=(kc == kcs - 1)),
                     r=[slot.k, rhs_key], w=keys, inc=(kc == kcs - 1))
            evac(f, self.PS[:, b, :], keys)

    def store(self, dst, src_buf, src_ap, wkeys):
        self.c.dma('pool', dst, src_ap, 'd_st_' + src_buf.k, r=[src_buf.k], w=wkeys)

    def load_x(self, l, ti):
        c, nc = self.c, self.nc
        si, t0, first, last = self.tiles[ti]
        X = self.X
        if l == 0:
            XT = self.BIG
            for cc in range(4):
                c.dma('sp', XT[:, cc * 1024:(cc + 1) * 1024], self.dr['x'][t0 + cc * 128:t0 + (cc + 1) * 128, :],
                      'd_xin%d' % cc, w=[('BIGx', cc)])
            for cc in range(4):
                for half in range(2):
                    b, keys = self.ps(1)
                    for q in range(4):
                        kc = half * 4 + q
                        c.op('pe', lambda kc=kc, q=q: nc.tensor.transpose(
                            out=self.PS[:, b, q * 128:(q + 1) * 128],
                            in_=XT[:, cc * 1024 + kc * 128: cc * 1024 + (kc + 1) * 128], identity=self.cf('ident')),
                            r=[('BIGx', cc), self.CF.k], w=keys, inc=(q == 3))
                    c.op('act', lambda half=half, cc=cc: nc.scalar.activation(
                        out=X[:, half * 4:(half + 1) * 4, cc * 128:(cc + 1) * 128],
                        in_=self.PS[:, b, :].rearrange('p (q t) -> p q t', q=4), func=AF.Copy),
                        r=keys, w=[X.k])
        else:
            c.dma('sp', X[:], self.sc['xT'][:, :, t0:t0 + T].rearrange('k p t -> p k t'), 'd_x',
                  r=[('xT', ti)], w=[X.k])

    def sweep_A(self, l, ti):
        c, nc, sc = self.c, self.nc, self.sc
        si, t0, first, last = self.tiles[ti]
        X, XN, LC, lc = self.X, self.XN, self.LC, self.lc
        BIG = self.BIG
        G0 = Buf(BIG.t, 'A_G0')
        g0v = BIG[:, 4096:6144].bitcast(BF16).rearrange('p (k t) -> p k t', k=8)
        U = Buf(BIG.t, 'A_U')
        uv = BIG[:, 6144:8192].rearrange('p (k t) -> p k t', k=4)
        AOT = Buf(BIG.t, 'A_AOT')
        aov = BIG[:, 8192:9216].bitcast(BF16).rearrange('p (k t) -> p k t', k=4)
        VG = [Buf(BIG.t, 'A_VG%d' % i) for i in range(2)]
        vgv = [BIG[:, 9216 + i * 512: 9216 + (i + 1) * 512] for i in range(2)]
        VN = [Buf(BIG.t, 'A_VN%d' % i) for i in range(2)]
        vnv = [BIG[:, 10240 + i * 256: 10240 + (i + 1) * 256].bitcast(BF16) for i in range(2)]
        MPS = [Buf(BIG.t, 'A_MPS%d' % i) for i in range(2)]
        mpv = [BIG[:, 10752 + i * 512: 10752 + (i + 1) * 512] for i in range(2)]
        JK = Buf(BIG.t, 'A_JK')
        jkv = BIG[:, 11776:12288]
        SM = self.SM

        self.load_x(l, ti)
        self.rmsnorm_fm('ffn1_norm')
        self.ffn(l, 'ffn1')
        c.dma('pool', sc['xT'][:, :, t0:t0 + T].rearrange('k p t -> p k t'), X[:], 'd_st_X', r=[X.k], w=[('xT', ti)])
        self.rmsnorm_fm('mix_norm')
        ob = lc['b_gate']
        for ci in range(6):
            slot, W = self.wget('w_gate', l, ci)

            def evac(f, ps, keys, ci=ci):
                fo = ci * 4 + f
                if fo < 8:
                    c.op('act', lambda: nc.scalar.activation(out=g0v[:, fo, :], in_=ps, func=AF.Sigmoid,
                                                             bias=LC[:, ob + fo:ob + fo + 1]),
                         r=keys + [LC.k], w=[G0.k])
                else:
                    st = self.stage()
                    c.op('act', lambda: nc.scalar.activation(out=st[:, 0:512], in_=ps, func=AF.Sigmoid,
                                                             bias=LC[:, ob + fo:ob + fo + 1]),
                         r=keys + [LC.k], w=[st.k])
                    self.store(sc['g12'][fo - 8, :, t0:t0 + T], st, st[:, 0:512], [('g12', ti, fo - 8)])
            self.lin_fm(W, slot, 8, XN, XN.k, 512, evac)
        slot, W = self.wget('w_in', l, 0)

        def evac_u(f, ps, keys):
            c.op('act', lambda: nc.scalar.activation(out=uv[:, f, :], in_=ps, func=AF.Gelu_apprx_tanh),
                 r=keys, w=[U.k])
        self.lin_fm(W, slot, 8, XN, XN.k, 512, evac_u)

        def lin_tm(W, slot, ncol, evac):
            for cc in range(4):
                b, keys = self.ps(1)
                for kc in range(8):
                    c.op('pe', lambda kc=kc, cc=cc: nc.tensor.matmul(self.PS[:, b, 0:ncol],
                                                                     XN[:, kc, cc * 128:(cc + 1) * 128],
                                                                     W[:, kc, 0:ncol], start=(kc == 0), stop=(kc == 7)),
                         r=[slot.k, XN.k], w=keys, inc=(kc == 7))
                evac(cc, self.PS[:, b, 0:ncol], keys)
        slot, W = self.wget('w_in', l, 1)
        olg, olb, osb = lc['lg'], lc['lb'], lc['sgu_bias']

        def evac_v(cc, ps, keys):
            vg, vgk = vgv[cc % 2], VG[cc % 2].k
            vn, vnk = vnv[cc % 2], VN[cc % 2].k
            s0 = (cc % 2) * 8
            c.op('act', lambda: nc.scalar.activation(out=vg, in_=ps, func=AF.Gelu_apprx_tanh,
                                                     accum_out=SM[:, s0:s0 + 1]), r=keys, w=[vgk, ('SMa', cc % 2)])
            c.op('act', lambda: nc.scalar.activation(out=jkv, in_=vg, func=AF.Square,
                                                     accum_out=SM[:, s0 + 1:s0 + 2]),
                 r=[vgk], w=[JK.k, ('SMb', cc % 2)])
            c.op('dve', lambda: nc.vector.tensor_scalar(out=SM[:, s0 + 2:s0 + 3], in0=SM[:, s0:s0 + 1],
                                                        scalar1=1.0 / 512, scalar2=None, op0=ALU.mult),
                 r=[('SMa', cc % 2)], w=[('SMc', cc % 2)])
            c.op('dve', lambda: nc.vector.tensor_tensor(out=SM[:, s0 + 3:s0 + 4], in0=SM[:, s0 + 2:s0 + 3],
                                                        in1=SM[:, s0 + 2:s0 + 3], op=ALU.mult),
                 r=[('SMc', cc % 2)], w=[('SMd', cc % 2)])
            c.op('dve', lambda: nc.vector.scalar_tensor_tensor(out=SM[:, s0 + 4:s0 + 5], in0=SM[:, s0 + 1:s0 + 2],
                                                               scalar=1.0 / 512, in1=SM[:, s0 + 3:s0 + 4],
                                                               op0=ALU.mult, op1=ALU.subtract),
                 r=[('SMb', cc % 2), ('SMd', cc % 2)], w=[('SMe', cc % 2)])
            c.op('act', lambda: nc.scalar.activation(out=SM[:, s0 + 5:s0 + 6], in_=SM[:, s0 + 4:s0 + 5], func=AF.Sqrt,
                                                     bias=self.eps_ap(0), scale=1.0),
                 r=[('SMe', cc % 2)], w=[('SMf', cc % 2)])
            c.op('dve', lambda: nc.vector.reciprocal(out=SM[:, s0 + 5:s0 + 6], in_=SM[:, s0 + 5:s0 + 6]),
                 r=[('SMf', cc % 2)], w=[('SMf', cc % 2)])
            c.op('dve', lambda: nc.vector.tensor_scalar(out=vg, in0=vg, scalar1=SM[:, s0 + 2:s0 + 3],
                                                        scalar2=SM[:, s0 + 5:s0 + 6], op0=ALU.subtract, op1=ALU.mult),
                 r=[vgk, ('SMc', cc % 2), ('SMf', cc % 2)], w=[vgk])
            c.op('dve', lambda: nc.vector.tensor_tensor(out=vg, in0=vg, in1=LC[:, olg:olg + 512], op=ALU.mult),
                 r=[vgk, LC.k], w=[vgk])
            c.op('dve', lambda: nc.vector.tensor_tensor(out=vn, in0=vg, in1=LC[:, olb:olb + 512], op=ALU.add),
                 r=[vgk, LC.k], w=[vnk])
            b2, k2 = self.ps(1)
            for g in range(4):
                c.op('pe', lambda g=g: nc.tensor.matmul(self.PS[:, b2, g * 128:(g + 1) * 128],
                                                        vn[:, g * 128:(g + 1) * 128],
                                                        self.LCB[:, g * 128:(g + 1) * 128], start=True, stop=True),
                     r=[vnk, self.LCB.k], w=k2, inc=(g == 3))
            c.op('dve', lambda: nc.vector.tensor_tensor(out=jkv, in0=self.PS[:, b2, :], in1=LC[:, osb:osb + 512],
                                                        op=ALU.add), r=k2 + [LC.k], w=[JK.k])
            c.op('dve', lambda: nc.vector.tensor_tensor(out=aov[:, :, cc * 128:(cc + 1) * 128],
                                                        in0=jkv.rearrange('p (g t) -> p g t', g=4),
                                                        in1=uv[:, :, cc * 128:(cc + 1) * 128], op=ALU.mult),
                 r=[JK.k, U.k], w=[AOT.k])
        lin_tm(W, slot, 512, evac_v)
        slot, W = self.wget('w_in', l, 2)

        def evac_pin(cc, ps, keys):
            st = self.stage()
            c.op('act', lambda: nc.scalar.activation(out=st[:, 0:512], in_=ps, func=AF.Copy), r=keys, w=[st.k])
            self.store(sc['pin'][t0 + cc * 128:t0 + (cc + 1) * 128, :], st, st[:, 0:512], [('pin', ti)])
        lin_tm(W, slot, 512, evac_pin)
        for zi in range(2):
            slot, W = self.wget('w_in', l, 3 + zi)

            def evac_z(cc, ps, keys, zi=zi):
                st = self.stage()
                c.op('act', lambda: nc.scalar.activation(out=st[:, 0:512], in_=ps, func=AF.Silu), r=keys, w=[st.k])
                self.store(sc['zs'][t0 + cc * 128:t0 + (cc + 1) * 128, zi * 512:(zi + 1) * 512], st, st[:, 0:512],
                           [('zs', ti)])
            lin_tm(W, slot, 512, evac_z)
        for xi in range(2):
            slot, W = self.wget('w_in', l, 5 + xi)

            def evac_x(f, ps, keys, xi=xi):
                st = self.stage()
                c.op('dve', lambda: nc.vector.tensor_copy(out=st[:, 0:512], in_=ps), r=keys, w=[st.k])
                self.store(sc['xbc'][xi * 4 + f, :, t0:t0 + T], st, st[:, 0:512], [('xbc', ti)])
            self.lin_fm(W, slot, 8, XN, XN.k, 512, evac_x)
        slot, W = self.wget('w_in', l, 7)

        def evac_x2(f, ps, keys):
            st = self.stage()
            c.op('dve', lambda: nc.vector.tensor_copy(out=st[:, 0:512], in_=ps), r=keys, w=[st.k])
            self.store(sc['xbc'][8 + f, :, t0:t0 + T], st, st[:, 0:512], [('xbc', ti)])
        self.lin_fm(W, slot, 8, XN, XN.k, 256, evac_x2)
        for cc in range(4):
            b, keys = self.ps(1)
            for kc in range(8):
                c.op('pe', lambda kc=kc, cc=cc: nc.tensor.matmul(self.PS[:, b, 0:32], XN[:, kc, cc * 128:(cc + 1) * 128],
                                                                 W[:, kc, 256:288], start=(kc == 0), stop=(kc == 7)),
                     r=[slot.k, XN.k], w=keys, inc=(kc == 7))
            c.op('act', lambda cc=cc: nc.scalar.activation(out=SM[:, 64 + cc * 32:64 + (cc + 1) * 32],
                                                           in_=self.PS[:, b, 0:32], func=AF.Copy),
                 r=keys, w=[('SMdt', cc)])
            c.dma('pool', sc['dtr'][t0 + cc * 128:t0 + (cc + 1) * 128, :], SM[:, 64 + cc * 32:64 + (cc + 1) * 32],
                  'd_st_dt%d' % cc, r=[('SMdt', cc)], w=[('dtr', ti)])
        slot, W = self.wget('w_branch_a', l, 0)

        def evac_a(f, ps, keys):
            mv, mk = mpv[f % 2], MPS[f % 2].k
            c.op('dve', lambda: nc.vector.tensor_tensor(out=mv, in0=ps, in1=g0v[:, f, :], op=ALU.mult),
                 r=keys + [G0.k], w=[mk])
            c.dma('pool', sc['mp'][f, :, t0:t0 + T], mv, 'd_st_' + mk, r=[mk], w=[('mp', ti)])
        self.lin_fm(W, slot, 4, aov, AOT.k, 1024, evac_a)

    def sweep_B(self, l, ti):
        c, nc, sc = self.c, self.nc, self.sc
        si, t0, first, last = self.tiles[ti]
        LC, lc, SM, BIG = self.LC, self.lc, self.SM, self.BIG
        XR = Buf(BIG.t, 'B_XR')
        xrv = BIG[:, 0:2580].bitcast(BF16).rearrange('p (f t) -> p f t', f=10)
        XC = Buf(BIG.t, 'B_XC')
        xcv = BIG[:, 2580:5140].bitcast(BF16).rearrange('p (f t) -> p f t', f=10)
        ACC = Buf(BIG.t, 'B_ACC')
        accv = BIG[:, 5140:5652]
        PINB = Buf(BIG.t, 'B_PIN')
        pinv = BIG[:, 5652:7188].bitcast(BF16).rearrange('p (c n) -> p c n', c=6)
        G1 = Buf(BIG.t, 'B_G1')
        g1v = BIG[:, 7188:9236].bitcast(BF16).rearrange('p (k t) -> p k t', k=8)
        POOLED = Buf(BIG.t, 'B_PO')
        pov = BIG[:, 9236:10260].bitcast(BF16).rearrange('p (g t) -> p g t', g=4)
        BO = Buf(BIG.t, 'B_BO')
        bov = BIG[:, 10260:11284].bitcast(BF16).rearrange('p (g t) -> p g t', g=4)
        DTR = Buf(BIG.t, 'B_DTR')
        dtrv = BIG[:, 11284:11412].rearrange('p (c n) -> p c n', c=4)
        HB = self.H
        hflat = HB[:, :, :].rearrange('p a t -> p (a t)')
        XS = Buf(HB.t, 'B_XS')
        xsv = hflat[:, 0:5120].rearrange('p (a n) -> p a n', a=5)
        BT = Buf(HB.t, 'B_BT')
        btv = hflat[:, 5120:5248]
        MM_ = Buf(HB.t, 'B_M')
        mv_ = hflat[:, 5248:9344].rearrange('p (a h q) -> p a h q', a=2, h=16)
        MCB = Buf(HB.t, 'B_MCB')
        mcbv = hflat[:, 9344:9856].rearrange('p (a g q) -> p a g q', a=2, g=2)
        XF = self.X
        xflat = XF[:, :, :].rearrange('p a t -> p (a t)')
        RF = Buf(XF.t, 'B_RF')
        rfv = xflat[:, 0:1536].bitcast(BF16).rearrange('p (j h q) -> p j h q', j=3, h=8)
        EB = Buf(XF.t, 'B_E')
        ebv = xflat[:, 2048:4096].rearrange('p (h q) -> p h q', h=16)
        YL = Buf(self.XN.t, 'B_YL')
        ylv = self.XN[:, :, :].rearrange('p a t -> p (a t)').bitcast(F32)[:, 0:1024]
        SST = [Buf(self.SQ.t, 'B_SST%d' % i) for i in range(2)]
        sstv = [self.SQ[:, :, :].rearrange('p a t -> p (a t)').bitcast(F32)[:, i * 512:(i + 1) * 512] for i in range(2)]
        MPB = Buf(self.RS.t, 'B_MPB')
        mpbv = self.RS[:]
        TMP = self.SA[0]
        TMP2 = self.SA[1]

        seq0 = self.seq_start[si]
        seqL = self.seqs[si]
        if first:
            c.op('pool', lambda: nc.gpsimd.memset(xrv[:, :, 0:2], 0.0), w=[XR.k])
        if last:
            c.op('pool', lambda: nc.gpsimd.memset(xrv[:, :, 514:516], 0.0), w=[XR.k])
        lo = t0 - (0 if first else 2)
        hi = t0 + T + (0 if last else 2)
        tis = [j for j in (ti - 1, ti, ti + 1) if 0 <= j < len(self.tiles)]
        c.dma('sp', xrv[:, :, 2 - (t0 - lo): 514 + (hi - t0 - T)],
              sc['xbc'][:, :, lo:hi].rearrange('f p t -> p f t'), 'd_xr', r=[('xbc', j) for j in tis], w=[XR.k])
        c0 = t0 // CH
        plo = c0 - (0 if first else 1)
        phi = c0 + 4 + (0 if last else 1)
        c.dma('sp', pinv[:, (plo - c0 + 1):(phi - c0 + 1), :],
              sc['pin'][plo * CH:phi * CH, :].rearrange('(c p) n -> p c n', p=128), 'd_pin',
              r=[('pin', j) for j in tis], w=[PINB.k])
        c.dma('sp', dtrv, sc['dtr'][t0:t0 + T, :].rearrange('(c p) n -> p c n', p=128), 'd_dtr',
              r=[('dtr', ti)], w=[DTR.k])
        c.dma('sp', g1v, sc['g12'][0:8, :, t0:t0 + T].rearrange('k p t -> p k t'), 'd_g1',
              r=[('g12', ti, k) for k in range(8)], w=[G1.k])
        if BSTOP == 1:
            return
        ocw, ocb = lc['conv_w'], lc['conv_b']
        self.link([EB.k, ('B_CT', 0), ('B_CT', 1)])
        for f in range(10):
            ctk = ('B_CT', f % 2)
            ct = xflat[:, 2048 + (f % 2) * 516: 2048 + (f % 2 + 1) * 516]
            c.op('act', lambda f=f, ct=ct: nc.scalar.activation(out=ct, in_=xrv[:, f, :], func=AF.Copy),
                 r=[XR.k], w=[ctk])
            c.op('dve', lambda f=f, ct=ct: nc.vector.tensor_scalar(out=accv, in0=ct[:, 0:512],
                                                                   scalar1=LC[:, ocw + f:ocw + f + 1],
                                                                   scalar2=None, op0=ALU.mult),
                 r=[ctk, LC.k], w=[ACC.k])
            for k in range(1, 4):
                c.op('dve', lambda f=f, k=k, ct=ct: nc.vector.scalar_tensor_tensor(
                    out=accv, in0=ct[:, k:k + 512], scalar=LC[:, ocw + k * 10 + f:ocw + k * 10 + f + 1],
                    in1=accv, op0=ALU.mult, op1=ALU.add), r=[ctk, LC.k, ACC.k], w=[ACC.k])
            c.op('act', lambda f=f: nc.scalar.activation(out=xcv[:, f, :], in_=accv, func=AF.Silu,
                                                         bias=LC[:, ocb + f:ocb + f + 1]),
                 r=[ACC.k, LC.k], w=[XC.k])
        if BSTOP == 2:
            return
        CMZ = Buf(BIG.t, 'B_CMZ')
        cmz = BIG[:, 11412:11924].bitcast(BF16).rearrange('p (g t) -> p g t', g=2)
        c.op('pool', lambda: nc.gpsimd.memset(cmz, 0.0), w=[CMZ.k])
        for g in range(2):
            c.op('act', lambda g=g: nc.scalar.activation(out=cmz[g * 64:(g + 1) * 64, g, :],
                                                         in_=xcv[g * 64:(g + 1) * 64, 9, :], func=AF.Copy),
                 r=[XC.k, CMZ.k], w=[CMZ.k])
        c.dma('pool', sc['cm'][:, :, t0:t0 + T].rearrange('g p t -> p g t'), cmz, 'd_st_cm', r=[CMZ.k],
              w=[('cm', ti)])
        odb, oa, ods = lc['dt_bias'], lc['a'], lc['d_skip']
        if BSTOP == 21:
            return
        for cc in range(4):
            chk = c0 + cc
            tsl = slice(cc * 128, (cc + 1) * 128)
            s_dt, s_da, s_t1, s_t2 = 192, 224, 256, 288
            c.op('dve', lambda: nc.vector.tensor_tensor(out=SM[:, s_t1:s_t1 + 32], in0=dtrv[:, cc, :],
                                                        in1=LC[:, odb:odb + 32], op=ALU.add),
                 r=[DTR.k, LC.k], w=[('SM', 't1')])
            c.op('dve', lambda: nc.vector.tensor_scalar(out=SM[:, s_t2:s_t2 + 32], in0=SM[:, s_t1:s_t1 + 32],
                                                        scalar1=-1.0, scalar2=None, op0=ALU.mult),
                 r=[('SM', 't1')], w=[('SM', 't2')])
            c.op('dve', lambda: nc.vector.tensor_tensor(out=SM[:, s_t2:s_t2 + 32], in0=SM[:, s_t2:s_t2 + 32],
                                                        in1=SM[:, s_t1:s_t1 + 32], op=ALU.min),
                 r=[('SM', 't1'), ('SM', 't2')], w=[('SM', 't2')])
            c.op('act', lambda: nc.scalar.activation(out=SM[:, s_t2:s_t2 + 32], in_=SM[:, s_t2:s_t2 + 32],
                                                     func=AF.Exp), r=[('SM', 't2')], w=[('SM', 't2')])
            c.op('act', lambda: nc.scalar.activation(out=SM[:, s_t2:s_t2 + 32], in_=SM[:, s_t2:s_t2 + 32],
                                                     func=AF.Ln, bias=self.eps_ap(2)), r=[('SM', 't2')], w=[('SM', 't2')])
            c.op('dve', lambda: nc.vector.scalar_tensor_tensor(out=SM[:, s_dt:s_dt + 32], in0=SM[:, s_t1:s_t1 + 32],
                                                               scalar=0.0, in1=SM[:, s_t2:s_t2 + 32],
                                                               op0=ALU.max, op1=ALU.add),
                 r=[('SM', 't1'), ('SM', 't2')], w=[('SM', 'dt')])
            c.op('dve', lambda: nc.vector.tensor_tensor(out=SM[:, s_da:s_da + 32], in0=SM[:, s_dt:s_dt + 32],
                                                        in1=LC[:, oa:oa + 32], op=ALU.mult),
                 r=[('SM', 'dt'), LC.k], w=[('SM', 'da')])
            if BSTOP == 22:
                continue
            da3 = SM[:, 480:528].bitcast(BF16).rearrange('p (j n) -> p j n', j=3)
            da_f = SM[:, s_da:s_da + 32]
            r1 = SM[:, 528:560]
            r2 = SM[:, 560:592]
            c.op('dve', lambda: nc.vector.tensor_copy(out=da3[:, 0, :], in_=da_f), r=[('SM', 'da')], w=[('SM', 'd3', 0)])
            c.op('dve', lambda: nc.vector.tensor_tensor(out=r1, in0=da_f, in1=da3[:, 0, :], op=ALU.subtract),
                 r=[('SM', 'da'), ('SM', 'd3', 0)], w=[('SM', 'r1')])
            c.op('dve', lambda: nc.vector.tensor_copy(out=da3[:, 1, :], in_=r1), r=[('SM', 'r1')], w=[('SM', 'd3', 1)])
            c.op('dve', lambda: nc.vector.tensor_tensor(out=r2, in0=r1, in1=da3[:, 1, :], op=ALU.subtract),
                 r=[('SM', 'r1'), ('SM', 'd3', 1)], w=[('SM', 'r2')])
            c.op('dve', lambda: nc.vector.tensor_copy(out=da3[:, 2, :], in_=r2), r=[('SM', 'r2')], w=[('SM', 'd3', 2)])
            d3k = [('SM', 'd3', j) for j in range(3)]
            if BSTOP == 23:
                continue
            b4, k4 = self.ps(1)
            for i, (nm, dcol) in enumerate((('tri_f', 0), ('su', 0), ('tri_b', 16), ('sl', 16))):
                for j in range(3):
                    c.op('pe', lambda i=i, nm=nm, dcol=dcol, j=j: nc.tensor.matmul(
                        self.PS[:, b4, i * 16:(i + 1) * 16], self.cb(nm), da3[:, j, dcol:dcol + 16],
                        start=(j == 0), stop=(j == 2)), r=d3k + [self.CB.k], w=k4, inc=(i == 3 and j == 2))
            if BSTOP == 24:
                continue
            s_e4, s_ac = 320, 384
            BSK = int(os.environ.get('BSK', '0'))
            if BSK != 1:
                c.op('act', lambda: nc.scalar.activation(out=SM[:, s_e4:s_e4 + 64], in_=self.PS[:, b4, 0:64], func=AF.Exp),
                     r=k4, w=[('SM', 'e4')])
            if BSK != 2:
                c.op('act', lambda: nc.scalar.activation(out=SM[:, s_ac:s_ac + 64], in_=self.PS[:, b4, 0:64],
                                                         func=AF.Copy), r=k4, w=[('SM', 'ac')])
            if BSK != 3:
                c.dma('pool', sc['e4'][t0 + cc * 128:t0 + (cc + 1) * 128, :], SM[:, s_e4:s_e4 + 64], 'd_st_e4',
                      r=[('SM', 'e4')], w=[('e4', ti)])
            if BSTOP == 25:
                continue
            s_dtd = 448
            for di in range(2):
                c.op('dve', lambda di=di: nc.vector.tensor_tensor(
                    out=SM[:, s_dtd + di * 16:s_dtd + (di + 1) * 16], in0=SM[:, s_dt + di * 16:s_dt + (di + 1) * 16],
                    in1=SM[:, s_e4 + 16 + di * 32:s_e4 + 32 + di * 32], op=ALU.mult),
                    r=[('SM', 'dt'), ('SM', 'e4')], w=[('SM', 'dtd')])
            if BSTOP == 3:
                continue
            bx, kx = self.ps(1)
            pxb = self.PS[:, bx, :].bitcast(BF16)
            for f in range(8):
                c.op('pe', lambda f=f: nc.tensor.transpose(out=pxb[:, f * 128:(f + 1) * 128], in_=xcv[:, f, tsl],
                                                           identity=self.cb('ident')),
                     r=[XC.k, self.CB.k], w=kx, inc=(f == 7))
            bb, kb = self.ps(1)
            pbb = self.PS[:, bb, :].bitcast(BF16)
            c.op('pe', lambda: nc.tensor.transpose(out=pbb[:, 0:128], in_=xcv[:, 8, tsl], identity=self.cb('ident')),
                 r=[XC.k, self.CB.k], w=kb)
            c.op('act', lambda: nc.scalar.activation(out=btv, in_=pbb[:, 0:128], func=AF.Copy), r=kb, w=[BT.k])
            XCT = Buf(HB.t, 'B_XCT')
            xctv = hflat[:, 9856:10880]
            c.op('act', lambda: nc.scalar.activation(out=xctv, in_=pxb, func=AF.Copy), r=kx, w=[XCT.k])
            kx = [XCT.k]
            px3 = xctv.rearrange('p (h d) -> p h d', h=16)

            def bc16(col):
                a = SM[:, col:col + 16]
                return bass.AP(a.tensor, a.offset, [list(a.ap[0]), [1, 16], [0, 64]])
            srcs = [(s_dt, 'dt'), (s_dt + 16, 'dt'), (s_dtd, 'dtd'), (s_dtd + 16, 'dtd')]
            for i, (col, kk) in enumerate(srcs):
                c.op('dve', lambda i=i, col=col: nc.vector.tensor_tensor(
                    out=xsv[:, i, :].rearrange('p (h d) -> p h d', h=16), in0=px3, in1=bc16(col), op=ALU.mult),
                    r=kx + [('SM', kk)], w=[(XS.k, i)])
            dsk = LC[:, ods:ods + 16]
            c.op('dve', lambda: nc.vector.tensor_tensor(
                out=xsv[:, 4, :].rearrange('p (h d) -> p h d', h=16), in0=px3,
                in1=bass.AP(dsk.tensor, dsk.offset, [list(dsk.ap[0]), [1, 16], [0, 64]]), op=ALU.mult),
                r=kx + [LC.k], w=[(XS.k, 4)])
            if BSTOP == 4:
                continue
            bc_, kc_ = self.ps(1)
            for g in range(2):
                c.op('pe', lambda g=g: nc.tensor.matmul(self.PS[:, bc_, g * 128:(g + 1) * 128],
                                                        xcv[:, 8, tsl], cmz[:, g, tsl],
                                                        start=True, stop=True), r=[XC.k, CMZ.k], w=kc_, inc=(g == 1))
            cbs = SM[:, 600:856]
            c.op('act', lambda: nc.scalar.activation(out=cbs, in_=self.PS[:, bc_, 0:256], func=AF.Copy),
                 r=kc_, w=[('SM', 'cb')])
            for di, nm in enumerate(('tri_f', 'tri_b')):
                tri = self.cf(nm)
                c.op('dve', lambda di=di, tri=tri: nc.vector.tensor_tensor(
                    out=mcbv[:, di, :, :], in0=cbs.rearrange('p (g q) -> p g q', g=2),
                    in1=bass.AP(tri.tensor, tri.offset, [list(tri.ap[0]), [0, 2], [1, 128]]), op=ALU.mult),
                    r=[('SM', 'cb'), self.CF.k], w=[(MCB.k, di)])
            if BSTOP == 5:
                continue
            for di, nm in enumerate(('tri_f', 'tri_b')):
                tri = self.cf(nm)
                trib = self.cb(nm)
                bd, kd = self.ps(4)
                for half in range(2):
                    dsrc = da3[:, :, di * 16 + half * 8: di * 16 + half * 8 + 8]
                    c.op('pool', lambda trib=trib, dsrc=dsrc: nc.gpsimd.tensor_tensor(
                        out=rfv, in0=bass.AP(dsrc.tensor, dsrc.offset, [list(dsrc.ap[0]), [32, 3], [1, 8], [0, 128]]),
                        in1=bass.AP(trib.tensor, trib.offset, [list(trib.ap[0]), [0, 3], [0, 8], [1, 128]]),
                        op=ALU.mult), r=d3k + [self.CB.k], w=[RF.k])
                    for jb in range(2):
                        bank = half * 2 + jb
                        for j in range(3):
                            c.op('pe', lambda j=j, jb=jb, bank=bank: nc.tensor.matmul(
                                self.PS[:, bd + bank, :], self.cb('ones'),
                                rfv[:, j, jb * 4:(jb + 1) * 4, :].rearrange('p h q -> p (h q)'),
                                start=(j == 0), stop=(j == 2)), r=[RF.k, self.CB.k], w=[kd[bank]], inc=(j == 2))
                acol = s_ac + (0 if di == 0 else 32)
                for h in range(16):
                    c.op('act', lambda h=h: nc.scalar.activation(
                        out=ebv[:, h, :], in_=self.PS[:, bd + h // 4, (h % 4) * 128:(h % 4 + 1) * 128], func=AF.Relu,
                        scale=-1.0, bias=SM[:, acol + h:acol + h + 1]),
                        r=[kd[h // 4], ('SM', 'ac')], w=[EB.k])
                qe = 127 if di == 0 else 0
                dst = self.stage()
                dv = dst[:, 0:16].bitcast(F32)
                for g in range(2):
                    src = self.PS[g * 64:(g + 1) * 64, bd + g * 2:bd + g * 2 + 2, :].rearrange(
                        'p b (h q) -> p (b h) q', h=4)[:, :, qe:qe + 1]
                    c.op('act', lambda g=g, src=src: nc.scalar.activation(
                        out=dv[g * 64:(g + 1) * 64, :].rearrange('p (h o) -> p h o', o=1), in_=src, func=AF.Exp),
                        r=[kd[g * 2], kd[g * 2 + 1]], w=[dst.k])
                self.store(sc['dec'][chk, di], dst, dv, [('dec', chk, di)])
                c.op('act', lambda: nc.scalar.activation(out=ebv, in_=ebv, func=AF.Exp, scale=-1.0), r=[EB.k], w=[EB.k])
                for g in range(2):
                    m = mcbv[:, di, g, :]
                    c.op('dve', lambda g=g, m=m, di=di: nc.vector.tensor_tensor(
                        out=mv_[:, di, g * 8:(g + 1) * 8, :], in0=ebv[:, g * 8:(g + 1) * 8, :],
                        in1=bass.AP(m.tensor, m.offset, [list(m.ap[0]), [0, 8], [1, 128]]), op=ALU.mult),
                        r=[EB.k, (MCB.k, di)], w=[(MM_.k, di)])
            if BSTOP == 6:
                continue
            by, ky = self.ps(2)
            for half in range(2):
                c.op('pe', lambda half=half: nc.tensor.matmul(self.PS[:, by + half, :], self.cb('ident'),
                                                              xsv[:, 4, half * 512:(half + 1) * 512],
                                                              start=True, stop=False),
                     r=[(XS.k, 4), self.CB.k], w=[ky[half]], inc=False)
            for h in range(16):
                for di in range(2):
                    lastmm = (h % 8 == 7 and di == 1)
                    c.op('pe', lambda h=h, di=di: nc.tensor.matmul(
                        self.PS[:, by + h // 8, (h % 8) * 64:(h % 8 + 1) * 64], mv_[:, di, h, :],
                        xsv[:, di, h * 64:(h + 1) * 64], start=False, stop=(di == 1)),
                        r=[(MM_.k, di), (XS.k, di)], w=[ky[h // 8]], inc=lastmm)
            c.op('act', lambda: nc.scalar.activation(out=ylv, in_=self.PS[:, by:by + 2, :].rearrange('p b n -> p (b n)'),
                                                     func=AF.Copy), r=ky, w=[YL.k])
            c.dma('pool', sc['yl'][t0 + cc * 128:t0 + (cc + 1) * 128, :], ylv, 'd_st_yl', r=[YL.k], w=[('yl', ti)])
            if BSTOP == 7:
                continue
            for di in range(2):
                bs, ks = self.ps(2)
                for g in range(2):
                    c.op('pe', lambda g=g, di=di: nc.tensor.matmul(
                        self.PS[:, bs + g, :], btv, xsv[:, 2 + di, g * 512:(g + 1) * 512], start=True, stop=True),
                        r=[BT.k, (XS.k, 2 + di)], w=[ks[g]])
                sv, sk = sstv[di], SST[di].k
                for g in range(2):
                    c.op('act', lambda sv=sv, g=g: nc.scalar.activation(out=sv[g * 64:(g + 1) * 64, :],
                                                                        in_=self.PS[g * 64:(g + 1) * 64, bs + g, :],
                                                                        func=AF.Copy), r=[ks[g]], w=[sk])
                c.dma('pool', sc['S'][chk, di], sv, 'd_st_' + sk, r=[sk], w=[('S', chk, di)])
            if BSTOP == 8:
                continue
            bp, kp = self.ps(1)
            cfirst = first and cc == 0
            clast = last and cc == 3
            var = 'f' if cfirst else ('l' if clast else 'm')
            for g in range(4):
                nbs = [nb for nb in (-1, 0, 1) if not ((nb == -1 and cfirst) or (nb == 1 and clast))]
                for ii, nb in enumerate(nbs):
                    bn = 'band_%d_%s0' % (g, var) if nb == 0 else 'band_%d_m%d' % (g, nb)
                    c.op('pe', lambda g=g, nb=nb, bn=bn, ii=ii: nc.tensor.matmul(
                        self.PS[:, bp, g * 128:(g + 1) * 128], pinv[:, cc + 1 + nb, g * 128:(g + 1) * 128],
                        self.cb(bn), start=(ii == 0), stop=(ii == len(nbs) - 1)),
                        r=[PINB.k, self.CB.k], w=kp, inc=(g == 3 and ii == len(nbs) - 1))
            irc = CONST_NAMES.index('rc_0_%s' % var)
            assert all(CONST_NAMES.index('rc_%d_%s' % (g, var)) == irc + 3 * g for g in range(4))
            rca = self.CF[:, irc * 128:(irc + 1) * 128]
            c.op('act', lambda: nc.scalar.activation(out=accv, in_=self.PS[:, bp, :], func=AF.Copy), r=kp, w=[ACC.k])
            c.op('dve', lambda: nc.vector.tensor_tensor(
                out=pov[:, :, tsl], in0=accv.rearrange('p (g t) -> p g t', g=4),
                in1=bass.AP(rca.tensor, rca.offset, [list(rca.ap[0]), [3 * 128, 4], [1, 128]]), op=ALU.mult),
                r=[ACC.k, self.CF.k], w=[POOLED.k])
        if BSTOP == 9:
            return
        ops_ = lc['pool_scale']
        for g in range(4):
            b, keys = self.ps(1)
            c.op('pe', lambda g=g: nc.tensor.matmul(self.PS[:, b, :], self.LCB[:, 512 + g * 128:512 + (g + 1) * 128],
                                                    pov[:, g, :], start=True, stop=True),
                 r=[POOLED.k, self.LCB.k], w=keys)
            c.op('act', lambda g=g: nc.scalar.activation(out=bov[:, g, :], in_=self.PS[:, b, :], func=AF.Copy,
                                                         scale=LC[:, ops_ + g:ops_ + g + 1]),
                 r=keys + [LC.k], w=[BO.k])
        slot, W = self.wget('w_branch_b', l, 0)

        def evac_b(f, ps, keys):
            c.dma('sp', mpbv, sc['mp'][f, :, t0:t0 + T], 'd_mpb', r=[('mp', ti)], w=[MPB.k])
            tm = TMP if f % 2 == 0 else TMP2
            c.op('dve', lambda: nc.vector.tensor_tensor(out=tm[:], in0=ps, in1=g1v[:, f, :], op=ALU.mult),
                 r=keys + [G1.k], w=[tm.k])
            c.op('pool', lambda: nc.gpsimd.tensor_tensor(out=tm[:], in0=tm[:], in1=mpbv, op=ALU.add),
                 r=[tm.k, MPB.k], w=[tm.k])
            c.dma('pool', sc['mp'][f, :, t0:t0 + T], tm[:], 'd_st_' + tm.k, r=[tm.k], w=[('mp', ti)])
        self.lin_fm(W, slot, 4, bov, BO.k, 1024, evac_b)

    def recurrence(self, l):
        c, nc, sc = self.c, self.nc, self.sc
        BIG = self.BIG
        Hs = Buf(BIG.t, 'R_H')
        hv = BIG[:, 0:512]
        Tm = Buf(BIG.t, 'R_T')
        tv = BIG[:, 512:1024]
        SL = [Buf(BIG.t, 'R_S%d' % i) for i in range(4)]
        slv = [BIG[:, 1024 + i * 512:1024 + (i + 1) * 512] for i in range(4)]
        DL = [Buf(BIG.t, 'R_D%d' % i) for i in range(4)]
        dlv = [BIG[:, 3072 + i * 8:3072 + (i + 1) * 8] for i in range(4)]
        HO = [Buf(BIG.t, 'R_HO%d' % i) for i in range(2)]
        hov = [BIG[:, 3200 + i * 256:3200 + (i + 1) * 256].bitcast(BF16) for i in range(2)]
        n = 0
        for si, Ls in enumerate(self.seqs):
            c0 = self.seq_start[si] // CH
            nch = Ls // CH
            for di in range(2):
                order = list(range(c0, c0 + nch)) if di == 0 else list(range(c0 + nch - 1, c0 - 1, -1))
                c.op('dve', lambda: nc.vector.memset(hv, 0.0), w=[Hs.k])
                for chk in order:
                    i4 = n % 4
                    i2 = n % 2
                    n += 1
                    c.dma('sp', slv[i4], sc['S'][chk, di], 'd_rs%d' % i4, r=[('S', chk, di)], w=[SL[i4].k])
                    c.dma('sp', dlv[i4], sc['dec'][chk, di], 'd_rd%d' % i4, r=[('dec', chk, di)], w=[DL[i4].k])
                    c.op('act', lambda i2=i2: nc.scalar.activation(out=hov[i2], in_=hv, func=AF.Copy),
                         r=[Hs.k], w=[HO[i2].k])
                    c.dma('pool', sc['hp'][chk, di], hov[i2], 'd_st_ho%d' % i2, r=[HO[i2].k], w=[('hp', chk, di)])
                    d_ = dlv[i4]
                    c.op('dve', lambda d_=d_: nc.vector.tensor_tensor(
                        out=tv.rearrange('p (e d) -> p e d', e=8), in0=hv.rearrange('p (e d) -> p e d', e=8),
                        in1=bass.AP(d_.tensor, d_.offset, [list(d_.ap[0]), [1, 8], [0, 64]]), op=ALU.mult),
                        r=[Hs.k, DL[i4].k], w=[Tm.k])
                    c.op('dve', lambda i4=i4: nc.vector.tensor_tensor(out=hv, in0=tv, in1=slv[i4], op=ALU.add),
                         r=[Tm.k, SL[i4].k], w=[Hs.k])

    def mem_kv(self, l, si):
        c, nc = self.c, self.nc
        KT, VV, LC, lc, SM = self.KT, self.VV, self.LC, self.lc, self.SM
        BIG = self.BIG
        MT = Buf(BIG.t, 'K_MT')
        mtv = BIG[:, 0:2048].rearrange('p (c n) -> p c n', c=2)
        MNB = Buf(BIG.t, 'K_MN')
        mnv = BIG[:, 2048:3072].bitcast(BF16).rearrange('p (c n) -> p c n', c=2)
        MNT = Buf(BIG.t, 'K_MNT')
        mntv = BIG[:, 3072:4096].bitcast(BF16).rearrange('p (k m) -> p k m', k=8)
        JK = Buf(BIG.t, 'K_JK')
        jkv = BIG[:, 4096:5120]
        c.dma('sp', mtv, self.dr['mem'][si].rearrange('(c p) n -> p c n', p=128), 'd_mem', w=[MT.k])
        omn = lc['mem_norm']
        for mc in range(2):
            c.op('act', lambda mc=mc: nc.scalar.activation(out=jkv, in_=mtv[:, mc, :], func=AF.Square,
                                                           accum_out=SM[:, 16 + mc:17 + mc]),
                 r=[MT.k], w=[JK.k, ('SMk', mc)])
            c.op('act', lambda mc=mc: nc.scalar.activation(out=SM[:, 18 + mc:19 + mc], in_=SM[:, 16 + mc:17 + mc],
                                                           func=AF.Sqrt, bias=self.eps_ap(0), scale=1.0 / 1024),
                 r=[('SMk', mc)], w=[('SMk2', mc)])
            c.op('dve', lambda mc=mc: nc.vector.reciprocal(out=SM[:, 20 + mc:21 + mc], in_=SM[:, 18 + mc:19 + mc]),
                 r=[('SMk2', mc)], w=[('SMk3', mc)])
            c.op('dve', lambda mc=mc: nc.vector.scalar_tensor_tensor(
                out=mnv[:, mc, :], in0=mtv[:, mc, :], scalar=SM[:, 20 + mc:21 + mc], in1=LC[:, omn:omn + 1024],
                op0=ALU.mult, op1=ALU.mult), r=[MT.k, ('SMk3', mc), LC.k], w=[MNB.k])
            b, keys = self.ps(1)
            pb = self.PS[:, b, :].bitcast(BF16)
            for kc in range(8):
                c.op('pe', lambda kc=kc, mc=mc: nc.tensor.transpose(out=pb[:, kc * 128:(kc + 1) * 128],
                                                                    in_=mnv[:, mc, kc * 128:(kc + 1) * 128],
                                                                    identity=self.cb('ident')),
                     r=[MNB.k, self.CB.k], w=keys, inc=(kc == 7))
            c.op('act', lambda mc=mc: nc.scalar.activation(out=mntv[:, :, mc * 128:(mc + 1) * 128],
                                                           in_=pb.rearrange('p (k m) -> p k m', k=8), func=AF.Copy),
                 r=keys, w=[MNT.k])
        for ci in range(2):
            slot, W = self.wget('xattn_wkv', l, ci)
            for f in range(4):
                b, keys = self.ps(1)
                for kc in range(8):
                    c.op('pe', lambda kc=kc, f=f: nc.tensor.matmul(self.PS[:, b, 0:256], W[:, kc, f * 128:(f + 1) * 128],
                                                                   mntv[:, kc, :], start=(kc == 0), stop=(kc == 7)),
                         r=[slot.k, MNT.k], w=keys, inc=(kc == 7))
                c.op('act', lambda f=f, ci=ci: nc.scalar.activation(out=KT[:, ci * 4 + f, :], in_=self.PS[:, b, 0:256],
                                                                    func=AF.Copy), r=keys, w=[KT.k])
        for ci in range(2):
            slot, W = self.wget('xattn_wkv', l, 2 + ci)
            for mc in range(2):
                b, keys = self.ps(1)
                for kc in range(8):
                    c.op('pe', lambda kc=kc, mc=mc: nc.tensor.matmul(self.PS[:, b, :], mntv[:, kc, mc * 128:(mc + 1) * 128],
                                                                     W[:, kc, :], start=(kc == 0), stop=(kc == 7)),
                         r=[slot.k, MNT.k], w=keys, inc=(kc == 7))
                c.op('act', lambda mc=mc, ci=ci: nc.scalar.activation(out=VV[:, mc, ci * 512:(ci + 1) * 512],
                                                                      in_=self.PS[:, b, :], func=AF.Copy),
                     r=keys, w=[VV.k])

    def sweep_C(self, l, ti):
        c, nc, sc = self.c, self.nc, self.sc
        si, t0, first, last = self.tiles[ti]
        X, XN, LC, lc, SM, BIG = self.X, self.XN, self.LC, self.lc, self.SM, self.BIG
        c0 = t0 // CH
        MP = Buf(BIG.t, 'C_MP')
        mpv = BIG[:, 0:4096].rearrange('p (k t) -> p k t', k=8)
        G2 = Buf(BIG.t, 'C_G2')
        g2v = BIG[:, 4096:6144].bitcast(BF16).rearrange('p (k t) -> p k t', k=8)
        YNT = Buf(BIG.t, 'C_YNT')
        yntv = BIG[:, 6144:8192].bitcast(BF16).rearrange('p (k t) -> p k t', k=8)
        CM = Buf(BIG.t, 'C_CM')
        cmv = self.RS[:].bitcast(BF16).rearrange('p (g t) -> p g t', g=2)
        E4 = Buf(BIG.t, 'C_E4')
        e4v = BIG[:, 8448:8704].rearrange('p (c n) -> p c n', c=4)
        YL = [Buf(BIG.t, 'C_YL%d' % i) for i in range(2)]
        ylv = [BIG[:, 8704 + i * 1024:8704 + (i + 1) * 1024] for i in range(2)]
        ZS = [Buf(BIG.t, 'C_ZS%d' % i) for i in range(2)]
        zsv = [BIG[:, 10752 + i * 512:10752 + (i + 1) * 512].bitcast(BF16) for i in range(2)]
        HP = [Buf(BIG.t, 'C_HP%d' % i) for i in range(2)]
        hpv = [BIG[:, 11776 + i * 512:11776 + (i + 1) * 512].bitcast(BF16) for i in range(2)]
        HB = self.H
        hflat = HB[:, :, :].rearrange('p a t -> p (a t)')
        YT = Buf(HB.t, 'C_YT')
        ytv = hflat[:, 0:2048].bitcast(F32)
        YN = Buf(HB.t, 'C_YN')
        ynv = hflat[:, 2048:3072]
        MGB = Buf(HB.t, 'C_MGB')
        mgv = hflat[:, 3072:7168].rearrange('p (k t) -> p k t', k=8)
        QT = Buf(HB.t, 'C_QT')
        qtv = hflat[:, 7168:11264].rearrange('p (k t) -> p k t', k=8)
        OT = MGB
        otv = mgv
        PRB = Buf(self.SQ.t, 'C_PRB')
        sqflat = self.SQ[:, :, :].rearrange('p a t -> p (a t)')
        prv = sqflat[:, 0:2048].bitcast(F32).rearrange('p (h m) -> p h m', h=4)
        PNB = Buf(self.SQ.t, 'C_PNB')
        pnv = sqflat[:, 2048:3072].rearrange('p (h m) -> p h m', h=4)
        PT = Buf(self.SQ.t, 'C_PT')
        ptv = sqflat[:, 3072:4096].rearrange('p (a s) -> p a s', a=8)

        self.link([self.H.k, YT.k, YN.k, MGB.k, QT.k])
        self.link([self.SQ.k, PRB.k, PNB.k, PT.k])
        c.dma('sp', X[:], sc['xT'][:, :, t0:t0 + T].rearrange('k p t -> p k t'), 'd_x', r=[('xT', ti)], w=[X.k])
        c.dma('sp', mpv, sc['mp'][:, :, t0:t0 + T].rearrange('k p t -> p k t'), 'd_mp', r=[('mp', ti)], w=[MP.k])
        c.dma('sp', g2v, sc['g12'][8:16, :, t0:t0 + T].rearrange('k p t -> p k t'), 'd_g2',
              r=[('g12', ti, k) for k in range(8, 16)], w=[G2.k])
        self.link([self.RS.k, CM.k])
        c.dma('sp', cmv, sc['cm'][:, :, t0:t0 + T].rearrange('g p t -> p g t'), 'd_cm', r=[('cm', ti)], w=[CM.k])
        c.dma('sp', e4v, sc['e4'][t0:t0 + T, :].rearrange('(c p) n -> p c n', p=128), 'd_e4', r=[('e4', ti)], w=[E4.k])
        osn = lc['ssd_norm']
        for cc in range(4):
            chk = c0 + cc
            i2 = cc % 2
            tsl = slice(cc * 128, (cc + 1) * 128)
            c.dma('sp', ylv[i2], sc['yl'][t0 + cc * 128:t0 + (cc + 1) * 128, :], 'd_yl%d' % i2, r=[('yl', ti)],
                  w=[YL[i2].k])
            c.dma('sp', zsv[i2], sc['zs'][t0 + cc * 128:t0 + (cc + 1) * 128, :], 'd_zs%d' % i2, r=[('zs', ti)],
                  w=[ZS[i2].k])
            c.dma('sp', hpv[i2].rearrange('p (a n) -> p a n', a=2), sc['hp'][chk].rearrange('a p n -> p a n'),
                  'd_hp%d' % i2, r=[('hp', chk, 0), ('hp', chk, 1)], w=[HP[i2].k])
            hp3 = hpv[i2].rearrange('p (a n) -> p a n', a=2)
            yv = ylv[i2]
            for di in range(2):
                b, keys = self.ps(2)
                for g in range(2):
                    c.op('pe', lambda g=g, di=di: nc.tensor.matmul(self.PS[:, b + g, :], cmv[:, g, tsl],
                                                                   hp3[:, di, :], start=True, stop=True),
                         r=[CM.k, HP[i2].k], w=[keys[g]])
                ecol = 0 if di == 0 else 32
                ea = e4v[:, cc, ecol:ecol + 16]
                c.op('act', lambda b=b: nc.scalar.activation(
                    out=ytv, in_=self.PS[:, b:b + 2, :].rearrange('p b n -> p (b n)'), func=AF.Copy),
                    r=keys, w=[YT.k])
                c.op('dve', lambda ea=ea, b=b: nc.vector.tensor_tensor(
                    out=ytv.rearrange('p (h d) -> p h d', h=16),
                    in0=ytv.rearrange('p (h d) -> p h d', h=16),
                    in1=bass.AP(ea.tensor, ea.offset, [list(ea.ap[0]), [1, 16], [0, 64]]), op=ALU.mult),
                    r=[YT.k, E4.k], w=[YT.k])
                c.op('pool', lambda: nc.gpsimd.tensor_tensor(out=yv, in0=yv, in1=ytv, op=ALU.add),
                     r=[YL[i2].k, YT.k], w=[YL[i2].k])
            c.op('dve', lambda: nc.vector.tensor_tensor(out=yv, in0=yv, in1=zsv[i2], op=ALU.mult),
                 r=[YL[i2].k, ZS[i2].k], w=[YL[i2].k])
            for g in range(2):
                c.op('act', lambda g=g: nc.scalar.activation(out=ytv[:, g * 512:(g + 1) * 512],
                                                             in_=yv[:, g * 512:(g + 1) * 512], func=AF.Square,
                                                             accum_out=SM[:, 32 + g:33 + g]),
                     r=[YL[i2].k], w=[YT.k, ('SMg', g)])
            c.op('act', lambda: nc.scalar.activation(out=SM[:, 34:36], in_=SM[:, 32:34], func=AF.Sqrt,
                                                     bias=self.eps_ap(0), scale=1.0 / 512),
                 r=[('SMg', 0), ('SMg', 1)], w=[('SMg2',)])
            c.op('dve', lambda: nc.vector.reciprocal(out=SM[:, 36:38], in_=SM[:, 34:36]), r=[('SMg2',)], w=[('SMg3',)])
            for g in range(2):
                c.op('dve', lambda g=g: nc.vector.scalar_tensor_tensor(
                    out=ynv[:, g * 512:(g + 1) * 512], in0=yv[:, g * 512:(g + 1) * 512], scalar=SM[:, 36 + g:37 + g],
                    in1=LC[:, osn + g * 512:osn + (g + 1) * 512], op0=ALU.mult, op1=ALU.mult),
                    r=[YL[i2].k, ('SMg3',), LC.k], w=[YN.k])
            b, keys = self.ps(1)
            pb = self.PS[:, b, :].bitcast(BF16)
            for kc in range(8):
                c.op('pe', lambda kc=kc: nc.tensor.transpose(out=pb[:, kc * 128:(kc + 1) * 128],
                                                             in_=ynv[:, kc * 128:(kc + 1) * 128],
                                                             identity=self.cb('ident')),
                     r=[YN.k, self.CB.k], w=keys, inc=(kc == 7))
            c.op('act', lambda: nc.scalar.activation(out=yntv[:, :, tsl], in_=pb.rearrange('p (k t) -> p k t', k=8),
                                                     func=AF.Copy), r=keys, w=[YNT.k])
        for ci in range(2):
            slot, W = self.wget('w_branch_c', l, ci)

            def evac_c(f, ps, keys, ci=ci):
                dd = ci * 4 + f
                tm = self.SA[dd % 2]
                c.op('dve', lambda: nc.vector.tensor_tensor(out=tm[:], in0=ps, in1=g2v[:, dd, :], op=ALU.mult),
                     r=keys + [G2.k], w=[tm.k])
                c.op('pool', lambda: nc.gpsimd.tensor_tensor(out=mgv[:, dd, :], in0=tm[:], in1=mpv[:, dd, :],
                                                             op=ALU.add), r=[tm.k, MP.k], w=[MGB.k])
            self.lin_fm(W, slot, 8, yntv, YNT.k, 512, evac_c)
        for ci in range(2):
            slot, W = self.wget('w_out', l, ci)

            def evac_o(f, ps, keys, ci=ci):
                dd = ci * 4 + f
                c.op('dve', lambda: nc.vector.tensor_tensor(out=X[:, dd, :], in0=ps, in1=X[:, dd, :], op=ALU.add),
                     r=keys + [X.k], w=[X.k])
            self.lin_fm(W, slot, 8, mgv, MGB.k, 512, evac_o)
        self.link([self.RS.k, CM.k])
        self.link([self.SQ.k, PRB.k, PNB.k, PT.k])
        self.rmsnorm_fm('xattn_norm')
        self.link([self.SQ.k, PRB.k, PNB.k, PT.k])
        for ci in range(2):
            slot, W = self.wget('xattn_wq', l, ci)

            def evac_q(f, ps, keys, ci=ci):
                dd = ci * 4 + f
                c.op('act', lambda: nc.scalar.activation(out=qtv[:, dd, :], in_=ps, func=AF.Copy, scale=1.0 / 16.0),
                     r=keys, w=[QT.k])
            self.lin_fm(W, slot, 8, XN, XN.k, 512, evac_q)
        KT, VV = self.KT, self.VV
        for cc in range(4):
            tsl = slice(cc * 128, (cc + 1) * 128)
            b, keys = self.ps(2)
            for hh in range(4):
                for j in range(2):
                    c.op('pe', lambda hh=hh, j=j: nc.tensor.matmul(
                        self.PS[:, b + hh // 2, (hh % 2) * 256:(hh % 2 + 1) * 256], qtv[:, hh * 2 + j, tsl],
                        KT[:, hh * 2 + j, :], start=(j == 0), stop=(j == 1)),
                        r=[QT.k, KT.k], w=[keys[hh // 2]], inc=(j == 1 and hh % 2 == 1))
            sc4 = self.PS[:, b:b + 2, :].rearrange('p b (h m) -> p (b h) m', h=2)
            c.op('act', lambda: nc.scalar.activation(out=prv, in_=sc4, func=AF.Copy), r=keys, w=[PRB.k])
            sc4 = prv
            keys = [PRB.k]
            c.op('dve', lambda: nc.vector.tensor_reduce(out=SM[:, 40:44], in_=sc4, axis=AX.X, op=ALU.max),
                 r=keys, w=[('SMx', 0)])
            c.op('dve', lambda: nc.vector.tensor_scalar(out=SM[:, 44:48], in0=SM[:, 40:44], scalar1=-1.0, scalar2=None,
                                                        op0=ALU.mult), r=[('SMx', 0)], w=[('SMx', 1)])
            for hh in range(4):
                c.op('act', lambda hh=hh: nc.scalar.activation(out=prv[:, hh, :], in_=sc4[:, hh, :], func=AF.Exp,
                                                               bias=SM[:, 44 + hh:45 + hh],
                                                               accum_out=SM[:, 48 + hh:49 + hh]),
                     r=keys + [('SMx', 1)], w=[PRB.k, ('SMx', 2)])
            c.op('dve', lambda: nc.vector.reciprocal(out=SM[:, 52:56], in_=SM[:, 48:52]), r=[('SMx', 2)],
                 w=[('SMx', 3)])
            ri = SM[:, 52:56]
            c.op('dve', lambda: nc.vector.tensor_tensor(
                out=pnv, in0=prv, in1=bass.AP(ri.tensor, ri.offset, [list(ri.ap[0]), [1, 4], [0, 256]]), op=ALU.mult),
                r=[PRB.k, ('SMx', 3)], w=[PNB.k])
            b2, k2 = self.ps(1)
            pb = self.PS[:, b2, :].bitcast(BF16)
            for hh in range(4):
                for mc in range(2):
                    a = hh * 2 + mc
                    c.op('pe', lambda hh=hh, mc=mc, a=a: nc.tensor.transpose(
                        out=pb[:, a * 128:(a + 1) * 128], in_=pnv[:, hh, mc * 128:(mc + 1) * 128],
                        identity=self.cb('ident')), r=[PNB.k, self.CB.k], w=k2, inc=(a == 7))
            c.op('act', lambda: nc.scalar.activation(out=ptv, in_=pb.rearrange('p (a s) -> p a s', a=8), func=AF.Copy),
                 r=k2, w=[PT.k])
            b3, k3 = self.ps(2)
            for hh in range(4):
                for j in range(2):
                    dd = hh * 2 + j
                    for mc in range(2):
                        c.op('pe', lambda hh=hh, j=j, mc=mc, dd=dd: nc.tensor.matmul(
                            self.PS[:, b3 + dd // 4, (dd % 4) * 128:(dd % 4 + 1) * 128],
                            VV[:, mc, hh * 256 + j * 128:hh * 256 + (j + 1) * 128], ptv[:, hh * 2 + mc, :],
                            start=(mc == 0), stop=(mc == 1)),
                            r=[VV.k, PT.k], w=[k3[dd // 4]], inc=(mc == 1 and dd % 4 == 3))
            c.op('act', lambda: nc.scalar.activation(
                out=otv[:, :, tsl], in_=self.PS[:, b3:b3 + 2, :].rearrange('p b (k s) -> p (b k) s', k=4),
                func=AF.Copy), r=k3, w=[MGB.k])
        for ci in range(2):
            slot, W = self.wget('xattn_wo', l, ci)

            def evac_wo(f, ps, keys, ci=ci):
                dd = ci * 4 + f
                c.op('dve', lambda: nc.vector.tensor_tensor(out=X[:, dd, :], in0=ps, in1=X[:, dd, :], op=ALU.add),
                     r=keys + [X.k], w=[X.k])
            self.lin_fm(W, slot, 8, otv, MGB.k, 512, evac_wo)
        self.link([self.SQ.k, PRB.k, PNB.k, PT.k])
        self.rmsnorm_fm('ffn2_norm')
        self.link([self.H.k, YT.k, YN.k, MGB.k, QT.k])
        self.ffn(l, 'ffn2')
        if l < self.depth - 1:
            c.dma('pool', sc['xT'][:, :, t0:t0 + T].rearrange('k p t -> p k t'), X[:], 'd_st_X', r=[X.k],
                  w=[('xT', ti)])
        else:
            ofn = lc['final_norm']
            for cc in range(4):
                i2 = cc % 2
                xo, xok = ylv[i2], YL[i2].k
                for half in range(2):
                    b, keys = self.ps(1)
                    for q in range(4):
                        kc = half * 4 + q
                        c.op('pe', lambda kc=kc, q=q, cc=cc: nc.tensor.transpose(
                            out=self.PS[:, b, q * 128:(q + 1) * 128], in_=X[:, kc, cc * 128:(cc + 1) * 128],
                            identity=self.cf('ident')), r=[X.k, self.CF.k], w=keys, inc=(q == 3))
                    c.op('act', lambda half=half: nc.scalar.activation(out=xo[:, half * 512:(half + 1) * 512],
                                                                       in_=self.PS[:, b, :], func=AF.Copy),
                         r=keys, w=[xok])
                c.op('act', lambda: nc.scalar.activation(out=ytv, in_=xo, func=AF.Square, accum_out=SM[:, 56:57]),
                     r=[xok], w=[self.H.k, YT.k, ('SMf', 0)])
                c.op('act', lambda: nc.scalar.activation(out=SM[:, 57:58], in_=SM[:, 56:57], func=AF.Sqrt,
                                                         bias=self.eps_ap(0), scale=1.0 / 1024),
                     r=[('SMf', 0)], w=[('SMf', 1)])
                c.op('dve', lambda: nc.vector.reciprocal(out=SM[:, 58:59], in_=SM[:, 57:58]), r=[('SMf', 1)],
                     w=[('SMf', 2)])
                c.op('dve', lambda: nc.vector.scalar_tensor_tensor(out=xo, in0=xo, scalar=SM[:, 58:59],
                                                                   in1=LC[:, ofn:ofn + 1024], op0=ALU.mult,
                                                                   op1=ALU.mult), r=[xok, ('SMf', 2), LC.k], w=[xok])
                c.dma('pool', self.dr['y'][t0 + cc * 128:t0 + (cc + 1) * 128, :], xo, 'd_st_y%d' % i2, r=[xok],
                      w=[('y', ti, cc)])


def run(depth, seqs, per_core_inputs, n_cores, trace=False, stop=None):
    bld = Builder(depth, seqs, stop)
    nc = bld.build()
    res = run_bass_kernel_spmd(nc, per_core_inputs, core_ids=list(range(n_cores)), trace=trace)
    return res, bld


def kernel(**inp):
    depth = 4
    seqs = [8192, 4096]
    n = 8
    xp, xs = np.asarray(inp['x_prompt']), np.asarray(inp['x_sample'])
    mp, ms = np.asarray(inp['mem_prompt']), np.asarray(inp['mem_sample'])
    shared = {k: np.ascontiguousarray(np.asarray(inp[k], dtype=np.float32)) for k in WNAMES + SMALL}
    shared['consts'] = CONSTF_ARR
    shared['constsb'] = CONSTB_ARR
    in_maps = []
    for i in range(n):
        d = dict(shared)
        d['x'] = np.ascontiguousarray(np.concatenate([xp[i], xs[i % 4]], axis=0))
        d['mem'] = np.ascontiguousarray(np.stack([mp[i], ms[i % 4]], axis=0))
        in_maps.append(d)
    res, _ = run(depth, seqs, in_maps, n)
    yp = np.stack([res.results[i]['y'][0:8192] for i in range(8)], axis=0)
    ysm = np.stack([res.results[i]['y'][8192:] for i in range(4)], axis=0)
    return (yp.astype(np.float32), ysm.astype(np.float32))
```

```python
import numpy as np
import concourse.bass as bass
import concourse.mybir as mybir
from concourse.bass_utils import run_bass_kernel_spmd

F32 = mybir.dt.float32
BF16 = mybir.dt.bfloat16
AF = mybir.ActivationFunctionType
ALU = mybir.AluOpType
AX = mybir.AxisListType

D = 1024
DFF = 2816
NMEM = 256
NIN = 3872
EPS = 1e-6
T = 512
CH = 128
WSLOT = 4096
import os
NWR = int(os.environ.get("NWR", "3"))
SAME_ENG_SYNC = bool(int(os.environ.get("SES", "1")))
BSTOP = int(os.environ.get('BSTOP', '0'))

WNAMES = ['ffn1_w13', 'ffn1_w2', 'w_in', 'w_gate', 'w_branch_a', 'w_branch_b', 'w_branch_c', 'w_out',
          'xattn_wq', 'xattn_wkv', 'xattn_wo', 'ffn2_w13', 'ffn2_w2']
WSHAPE = {'ffn1_w13': (D, 2 * DFF), 'ffn1_w2': (DFF, D), 'w_in': (D, NIN), 'w_gate': (D, 3 * D),
          'w_branch_a': (512, D), 'w_branch_b': (512, D), 'w_branch_c': (D, D), 'w_out': (D, D),
          'xattn_wq': (D, D), 'xattn_wkv': (D, 2 * D), 'xattn_wo': (D, D), 'ffn2_w13': (D, 2 * DFF),
          'ffn2_w2': (DFF, D)}
SMALL = ['ffn1_norm', 'mix_norm', 'b_gate', 'sgu_ln_g', 'sgu_ln_b', 'sgu_ws', 'sgu_bias', 'pool_w', 'pool_scale',
         'conv_w', 'conv_b', 'dt_bias', 'a_log', 'd_skip', 'ssd_norm', 'xattn_norm', 'mem_norm', 'ffn2_norm',
         'final_norm']
SMALL_SHAPE = {'ffn1_norm': (D,), 'mix_norm': (D,), 'b_gate': (3 * D,), 'sgu_ln_g': (512,), 'sgu_ln_b': (512,),
               'sgu_ws': (4, 128, 128), 'sgu_bias': (4, 128), 'pool_w': (4, 128, 128), 'pool_scale': (512,),
               'conv_w': (4, 1280), 'conv_b': (1280,), 'dt_bias': (2, 16), 'a_log': (2, 16), 'd_skip': (16,),
               'ssd_norm': (D,), 'xattn_norm': (D,), 'mem_norm': (D,), 'ffn2_norm': (D,)}


def wchunks(name):
    K, N = WSHAPE[name]
    KC = K // 128
    if name.endswith('w13'):
        return [(KC, 512, [(c * 256, 256, 0), (DFF + c * 256, 256, 256)]) for c in range(11)]
    if name.endswith('w2'):
        return [(KC, 128, [(c * 128, 128, 0)]) for c in range(8)]
    if name == 'w_in':
        out = [(KC, 512, [(c * 512, 512, 0)]) for c in range(7)]
        out.append((KC, 288, [(3584, 288, 0)]))
        return out
    if name in ('w_branch_a', 'w_branch_b'):
        return [(KC, 1024, [(0, 1024, 0)])]
    return [(KC, 512, [(c * 512, 512, 0)]) for c in range(N // 512)]


def host_consts():
    i = np.arange(128)
    k = i[:, None]
    q = i[None, :]
    blocks = {}
    blocks['ident'] = (k == q)
    blocks['tri_f'] = (k <= q)
    blocks['tri_b'] = (k >= q)
    blocks['su'] = (k > q)
    blocks['sl'] = (k < q)
    blocks['ones'] = np.ones((128, 128))
    wins = (2, 4, 8, 16)
    S3 = 3 * 128
    for g, w in enumerate(wins):
        def band(seq_len, t_off):
            pos = np.arange(seq_len)
            lo = np.clip(pos - w // 2, 0, seq_len)
            hi = np.clip(pos + w - w // 2, 0, seq_len)
            cnt = hi - lo
            out = {}
            for nb in (-1, 0, 1):
                m = np.zeros((128, 128))
                for tt in range(128):
                    tg = t_off + tt
                    for sg in range(lo[tg], hi[tg]):
                        sr = sg - (t_off + nb * 128)
                        if 0 <= sr < 128:
                            m[sr, tt] += 1.0
                    if nb == 0:
                        m[tt, tt] -= cnt[tg]
                out[nb] = m
            return out, 1.0 / cnt[t_off:t_off + 128]
        bm, rcm = band(S3, 128)
        bf, rcf = band(S3, 0)
        bl, rcl = band(S3, 256)
        blocks['band_%d_m-1' % g] = bm[-1]
        blocks['band_%d_m0' % g] = bm[0]
        blocks['band_%d_m1' % g] = bm[1]
        blocks['band_%d_f0' % g] = bf[0]
        blocks['band_%d_l0' % g] = bl[0]
        blocks['rc_%d_m' % g] = np.broadcast_to(rcm[None, :], (128, 128))
        blocks['rc_%d_f' % g] = np.broadcast_to(rcf[None, :], (128, 128))
        blocks['rc_%d_l' % g] = np.broadcast_to(rcl[None, :], (128, 128))
    fnames = ['ident', 'tri_f', 'tri_b', 'su', 'sl', 'ones']
    for g in range(4):
        fnames += ['rc_%d_m' % g, 'rc_%d_f' % g, 'rc_%d_l' % g]
    bnames = ['ident', 'ones', 'tri_f', 'tri_b', 'su', 'sl'] + [n for n in blocks if n.startswith('band_')]
    af = np.concatenate([np.asarray(blocks[n], dtype=np.float32) for n in fnames], axis=1)
    ab = np.concatenate([np.asarray(blocks[n], dtype=np.float32) for n in bnames], axis=1)
    return fnames, bnames, np.ascontiguousarray(af), np.ascontiguousarray(ab)


CONST_NAMES, CONSTB_NAMES, CONSTF_ARR, CONSTB_ARR = host_consts()
NCONST = CONSTF_ARR.shape[1]
NCONSTB = CONSTB_ARR.shape[1]


class Ctx:
    def __init__(self, nc):
        self.nc = nc
        self.E = {'pe': nc.tensor, 'act': nc.scalar, 'dve': nc.vector, 'pool': nc.gpsimd, 'sp': nc.sync}
        self.sem = {}
        self.cnt = {}
        self.seen = {e: {} for e in self.E}
        self.lastw = {}
        self.rd = {}
        self.nins = 0
        for e in self.E:
            self.newsem('E_' + e)

    def newsem(self, name):
        self.sem[name] = self.nc.alloc_semaphore(name)
        self.cnt[name] = 0

    def _need(self, r, w):
        need = {}

        def add(tok):
            if tok is None:
                return
            s, v = tok
            if need.get(s, 0) < v:
                need[s] = v
        for k in r:
            add(self.lastw.get(k))
        for k in w:
            add(self.lastw.get(k))
            for s, v in self.rd.get(k, {}).items():
                add((s, v))
        return need

    def _wait(self, e, need):
        seen = self.seen[e]
        own = 'E_' + e
        for s, v in need.items():
            if s == own and (e == 'pe' or not SAME_ENG_SYNC):
                continue
            if seen.get(s, 0) < v:
                self.E[e].wait_ge(self.sem[s], v)
                seen[s] = v
                self.nins += 1

    def _mark(self, tok, r, w):
        s, v = tok
        for k in r:
            d = self.rd.setdefault(k, {})
            if d.get(s, 0) < v:
                d[s] = v
        for k in w:
            self.lastw[k] = tok
            self.rd[k] = {}

    def op(self, e, fn, r=(), w=(), inc=True):
        self._wait(e, self._need(r, w))
        ins = fn()
        self.nins += 1
        s = 'E_' + e
        if inc:
            ins.then_inc(self.sem[s], 1)
            self.cnt[s] += 1
            tok = (s, self.cnt[s])
        else:
            tok = (s, self.cnt[s] + 1)
        self._mark(tok, r, w)
        return ins

    def dma(self, q, out, in_, sem, r=(), w=(), **kw):
        if sem not in self.sem:
            self.newsem(sem)
        self._wait(q, self._need(r, w))
        ins = self.E[q].dma_start(out=out, in_=in_, **kw)
        ins.then_inc(self.sem[sem], 16)
        self.nins += 1
        self.cnt[sem] += 16
        self._mark((sem, self.cnt[sem]), r, w)

    def final_wait(self, e):
        need = {}
        for s in self.sem:
            if self.cnt[s] > 0:
                need[s] = self.cnt[s]
        self._wait(e, need)


class Buf:
    def __init__(self, t, key):
        self.t = t
        self.k = key

    def __getitem__(self, idx):
        return self.t[idx]


def bcast_ap(ap, dims):
    return bass.AP(ap.tensor, ap.offset, dims)


class Builder:
    def __init__(self, depth, seqs, stop=None):
        self.stop = stop
        self.depth = depth
        self.seqs = list(seqs)
        self.NT = sum(seqs)
        self.NCHK = self.NT // CH
        self.tiles = []
        t0 = 0
        for si, L in enumerate(seqs):
            assert L % T == 0
            for j in range(L // T):
                self.tiles.append((si, t0 + j * T, j == 0, j == L // T - 1))
            t0 += L
        self.seq_start = [sum(seqs[:i]) for i in range(len(seqs))]

    def build(self):
        nc = bass.Bass("TRN2", target_bir_lowering=False)
        self.nc = nc
        c = Ctx(nc)
        self.c = c
        L = self.depth
        NT = self.NT
        nseq = len(self.seqs)
        dr = {}
        dr['x'] = nc.dram_tensor('x', [NT, D], F32, kind='ExternalInput').ap()
        dr['mem'] = nc.dram_tensor('mem', [nseq, NMEM, D], F32, kind='ExternalInput').ap()
        dr['consts'] = nc.dram_tensor('consts', [128, NCONST], F32, kind='ExternalInput').ap()
        dr['constsb'] = nc.dram_tensor('constsb', [128, NCONSTB], F32, kind='ExternalInput').ap()
        for n in WNAMES:
            dr[n] = nc.dram_tensor(n, [L] + list(WSHAPE[n]), F32, kind='ExternalInput').ap()
        for n in SMALL:
            shp = ([L] + list(SMALL_SHAPE[n])) if n != 'final_norm' else [D]
            dr[n] = nc.dram_tensor(n, shp, F32, kind='ExternalInput').ap()
        dr['y'] = nc.dram_tensor('y', [NT, D], F32, kind='ExternalOutput').ap()
        sc = {}
        for n in WNAMES:
            ch = wchunks(n)
            sc[n] = nc.dram_tensor('s_' + n, [L, len(ch), 128, WSLOT], BF16, kind='Internal').ap()
        sc['xT'] = nc.dram_tensor('s_xT', [8, 128, NT], F32, kind='Internal').ap()
        sc['mp'] = nc.dram_tensor('s_mp', [8, 128, NT], F32, kind='Internal').ap()
        sc['g12'] = nc.dram_tensor('s_g12', [16, 128, NT], BF16, kind='Internal').ap()
        sc['zs'] = nc.dram_tensor('s_zs', [NT, D], BF16, kind='Internal').ap()
        sc['pin'] = nc.dram_tensor('s_pin', [NT, 512], BF16, kind='Internal').ap()
        sc['xbc'] = nc.dram_tensor('s_xbc', [10, 128, NT], BF16, kind='Internal').ap()
        sc['dtr'] = nc.dram_tensor('s_dtr', [NT, 32], F32, kind='Internal').ap()
        sc['yl'] = nc.dram_tensor('s_yl', [NT, D], F32, kind='Internal').ap()
        sc['cm'] = nc.dram_tensor('s_cm', [2, 128, NT], BF16, kind='Internal').ap()
        sc['e4'] = nc.dram_tensor('s_e4', [NT, 64], F32, kind='Internal').ap()
        sc['S'] = nc.dram_tensor('s_S', [self.NCHK, 2, 128, 512], F32, kind='Internal').ap()
        sc['dec'] = nc.dram_tensor('s_dec', [self.NCHK, 2, 128, 8], F32, kind='Internal').ap()
        sc['hp'] = nc.dram_tensor('s_hp', [self.NCHK, 2, 128, 512], BF16, kind='Internal').ap()
        self.dr = dr
        self.sc = sc

        def sb(name, shape, dt):
            return Buf(nc.alloc_sbuf_tensor(name, shape, dt), name)
        self.sb = sb
        self.PS = nc.alloc_psum_tensor('psum', [128, 8, 512], F32)
        self.psn = 0
        self.CF = sb('CF', [128, NCONST], F32)
        self.CB = sb('CB', [128, NCONSTB], BF16)
        self.X = sb('X', [128, 8, T], F32)
        self.SQ = sb('SQ', [128, 8, T], BF16)
        self.XN = sb('XN', [128, 8, T], BF16)
        self.RS = sb('RS', [128, T], F32)
        self.H = sb('H', [128, 22, T], BF16)
        self.SA = [sb('SA%d' % i, [128, T], F32) for i in range(2)]
        self.WR = [sb('WR%d' % i, [128, WSLOT], BF16) for i in range(NWR)]
        self.WX = sb('WX', [128, 2048], F32)
        self.BIG = sb('BIG', [128, 12800], F32)
        self.ST = [sb('ST%d' % i, [128, 512], BF16) for i in range(6)]
        self.sti = 0
        self.LC = sb('LC', [128, 4864], F32)
        self.LCB = sb('LCB', [128, 1024], BF16)
        self.SM = sb('SM', [128, 1024], F32)
        self.wplan = []
        self.wpos = 0
        self.wissued = 0

        self.KT = sb('KT', [128, 8, 256], BF16)
        self.VV = sb('VV', [128, 2, 1024], BF16)
        c.dma('sp', self.CF[:], dr['consts'][:, :], 'd_const', w=[self.CF.k])
        c.dma('sp', self.BIG[:, 0:NCONSTB], dr['constsb'][:, :], 'd_const', w=[self.BIG.k])
        c.op('dve', lambda: nc.vector.tensor_copy(out=self.CB[:], in_=self.BIG[:, 0:NCONSTB]), r=[self.BIG.k],
             w=[self.CB.k])
        c.op('dve', lambda: nc.vector.memset(self.SM[:, 1010:1011], EPS), w=[('SMc0',)])
        c.op('dve', lambda: nc.vector.memset(self.SM[:, 1011:1012], 1024.0 * EPS), w=[('SMc1',)])
        c.op('dve', lambda: nc.vector.memset(self.SM[:, 1012:1013], 1.0), w=[('SMc2',)])
        self.barrier()
        stop = self.stop

        def done(tag):
            if stop == tag:
                self.barrier()
                return True
            return False
        self.prep_weights()
        self.barrier()
        if done('prep'):
            return nc
        for l in range(L):
            self.layer_consts(l)
            self.barrier()
            if done('lc'):
                return nc
            self.plan_layer(l)
            for ti in range(len(self.tiles)):
                self.sweep_A(l, ti)
                if done('A0'):
                    return nc
            self.barrier()
            if done('A'):
                return nc
            for ti in range(len(self.tiles)):
                self.sweep_B(l, ti)
            self.barrier()
            if done('B'):
                return nc
            self.recurrence(l)
            self.barrier()
            if done('R'):
                return nc
            for si in range(len(self.seqs)):
                self.mem_kv(l, si)
                self.barrier()
                if done('KV'):
                    return nc
                for ti in range(len(self.tiles)):
                    if self.tiles[ti][0] == si:
                        self.sweep_C(l, ti)
            assert self.wpos == len(self.wplan), (self.wpos, len(self.wplan))
            self.barrier()
        c.final_wait('pool')
        return nc

    def cf(self, name):
        i = CONST_NAMES.index(name)
        return self.CF[:, i * 128:(i + 1) * 128]

    def barrier(self):
        c = self.c
        need = {s: v for s, v in c.cnt.items() if v > 0}
        for e in c.E:
            c._wait(e, dict(need))
        c.lastw = {k: v for k, v in c.lastw.items() if isinstance(k, tuple) and k[0] == 'wsc'}
        c.rd = {}

    def link(self, keys):
        c, nc = self.c, self.nc
        c.op('dve', lambda: nc.vector.memset(self.SM[:, 1000:1001], 0.0), w=list(keys) + [('SMz',)])

    def xnk(self):
        return [(self.XN.k, kc) for kc in range(8)]

    def eps_ap(self, which):
        return self.SM[:, 1010 + which:1011 + which]

    def cb(self, name):
        i = CONSTB_NAMES.index(name)
        return self.CB[:, i * 128:(i + 1) * 128]

    def ps(self, n=1):
        if self.psn + n > 8:
            self.psn = 0
        b = self.psn
        self.psn = (self.psn + n) % 8
        keys = [('ps', b + i) for i in range(n)]
        return b, keys

    def stage(self):
        s = self.ST[self.sti % len(self.ST)]
        self.sti += 1
        return s

    def prep_weights(self):
        c, nc = self.c, self.nc
        n = 0
        for name in WNAMES:
            chs = wchunks(name)
            for l in range(self.depth):
                for ci, (KC, wc, ranges) in enumerate(chs):
                    dst = self.sc[name][l, ci]
                    for (c0, ncol, off) in ranges:
                        src = self.dr[name][l, :, c0:c0 + ncol].rearrange('(kc p) n -> p kc n', p=128)
                        d = dst[:, 0:KC * wc].rearrange('p (kc n) -> p kc n', kc=KC)[:, :, off:off + ncol]
                        sem = 'd_prep%d' % (n % 8)
                        n += 1
                        c.dma('pool', d, src, sem, w=[('wsc', name, l, ci)])

    def plan_layer(self, l):
        plan = []
        for ti in range(len(self.tiles)):
            for nm in ('ffn1_w13', 'ffn1_w2', 'w_gate', 'w_in', 'w_branch_a'):
                for ci in range(len(wchunks(nm))):
                    plan.append((nm, l, ci))
        for ti in range(len(self.tiles)):
            plan.append(('w_branch_b', l, 0))
        for si in range(len(self.seqs)):
            for ci in range(4):
                plan.append(('xattn_wkv', l, ci))
            for ti in range(len(self.tiles)):
                if self.tiles[ti][0] != si:
                    continue
                for nm in ('w_branch_c', 'w_out', 'xattn_wq', 'xattn_wo', 'ffn2_w13', 'ffn2_w2'):
                    for ci in range(len(wchunks(nm))):
                        plan.append((nm, l, ci))
        self.wplan.extend(plan)

    def _issue_w(self):
        i = self.wissued
        nm, l, ci = self.wplan[i]
        KC, wc, _ = wchunks(nm)[ci]
        slot = self.WR[i % NWR]
        n = KC * wc
        self.c.dma('sp', slot[:, 0:n], self.sc[nm][l, ci, :, 0:n], 'd_w%d' % (i % NWR),
                   r=[('wsc', nm, l, ci)], w=[slot.k])
        self.wissued += 1

    def wget(self, nm, l, ci):
        assert self.wplan[self.wpos] == (nm, l, ci), (self.wplan[self.wpos], (nm, l, ci))
        while self.wissued < min(len(self.wplan), self.wpos + NWR - 1) or self.wissued <= self.wpos:
            self._issue_w()
        slot = self.WR[self.wpos % NWR]
        self.wpos += 1
        KC, wc, _ = wchunks(nm)[ci]
        return slot, slot[:, 0:KC * wc].rearrange('p (kc n) -> p kc n', kc=KC)

    def layer_consts(self, l):
        c, nc, dr = self.c, self.nc, self.dr
        LC = self.LC
        lc = {}
        off = [0]

        def alloc(n):
            o = off[0]
            off[0] += n
            return o

        def load_bc(src_ap, n, key):
            o = alloc(n)
            src = bass.AP(src_ap.tensor, src_ap.offset, [[0, 128], [1, n]])
            c.dma('sp', LC[:, o:o + n], src, 'd_lc', w=[LC.k])
            lc[key] = o
        BG = self.BIG
        rows = 0
        pp = [('ffn1_norm', 8), ('mix_norm', 8), ('xattn_norm', 8), ('ffn2_norm', 8), ('b_gate', 24),
              ('pool_scale', 4), ('conv_b', 10)]
        for name, nk in pp:
            c.dma('sp', BG[rows:rows + nk, 2048:2176], dr[name][l].rearrange('(kc p) -> kc p', p=128), 'd_lcpp',
                  w=[('BGpp',)])
            lc[name] = alloc(nk)
            rows += nk
        c.dma('sp', BG[rows:rows + 40, 2048:2176], dr['conv_w'][l].rearrange('k (f p) -> (k f) p', p=128), 'd_lcpp',
              w=[('BGpp',)])
        lc['conv_w'] = alloc(40)
        rows += 40
        b0, keys0 = self.ps(1)
        c.op('pe', lambda: nc.tensor.transpose(out=self.PS[:, b0, 0:rows], in_=BG[0:rows, 2048:2176],
                                               identity=self.cf('ident')[0:rows, 0:rows]),
             r=[('BGpp',), self.CF.k], w=keys0)
        c.op('dve', lambda: nc.vector.tensor_copy(out=LC[:, 0:rows], in_=self.PS[:, b0, 0:rows]), r=keys0, w=[LC.k])
        load_bc(dr['sgu_ln_g'][l], 512, 'lg')
        load_bc(dr['sgu_ln_b'][l], 512, 'lb')
        load_bc(dr['sgu_bias'][l].rearrange('g t -> (g t)'), 512, 'sgu_bias')
        load_bc(dr['dt_bias'][l].rearrange('a h -> (a h)'), 32, 'dt_bias')
        load_bc(dr['a_log'][l].rearrange('a h -> (a h)'), 32, 'a_log')
        load_bc(dr['d_skip'][l], 16, 'd_skip')
        load_bc(dr['ssd_norm'][l], 1024, 'ssd_norm')
        load_bc(dr['mem_norm'][l], 1024, 'mem_norm')
        load_bc(dr['final_norm'], 1024, 'final_norm')
        for k in ('ffn1_norm', 'mix_norm', 'xattn_norm', 'ffn2_norm'):
            o = lc[k]
            c.op('dve', lambda o=o: nc.vector.tensor_scalar(out=LC[:, o:o + 8], in0=LC[:, o:o + 8], scalar1=32.0,
                                                            scalar2=None, op0=ALU.mult), r=[LC.k], w=[LC.k])
        o = lc['a_log']
        c.op('act', lambda: nc.scalar.activation(out=LC[:, o:o + 32], in_=LC[:, o:o + 32], func=AF.Exp),
             r=[LC.k], w=[LC.k])
        c.op('dve', lambda: nc.vector.tensor_scalar(out=LC[:, o:o + 32], in0=LC[:, o:o + 32], scalar1=-1.0,
                                                    scalar2=None, op0=ALU.mult), r=[LC.k], w=[LC.k])
        lc['a'] = o
        o = 0
        c.dma('sp', BG[:, o:o + 512].rearrange('p (g s) -> p g s', g=4),
              dr['sgu_ws'][l].rearrange('g t s -> t g s'), 'd_lc3', w=[BG.k])
        b, keys = self.ps(1)
        for g in range(4):
            c.op('pe', lambda g=g: nc.tensor.transpose(out=self.PS[:, b, g * 128:(g + 1) * 128],
                                                       in_=BG[:, o + g * 128:o + (g + 1) * 128],
                                                       identity=self.cf('ident')),
                 r=[BG.k, self.CF.k], w=keys, inc=(g == 3))
        c.op('dve', lambda: nc.vector.tensor_copy(out=self.LCB[:, 0:512], in_=self.PS[:, b, :]),
             r=keys, w=[self.LCB.k])
        o2 = 512
        c.dma('sp', BG[:, o2:o2 + 512].rearrange('p (g e) -> p g e', g=4),
              dr['pool_w'][l].rearrange('g d e -> d g e'), 'd_lc2', w=[('BG2',)])
        c.op('dve', lambda: nc.vector.tensor_copy(out=self.LCB[:, 512:1024], in_=BG[:, o2:o2 + 512]),
             r=[('BG2',)], w=[self.LCB.k])
        assert off[0] <= 4864
        self.lc = lc

    def rmsnorm_fm(self, gkey):
        c, nc = self.c, self.nc
        X, SQ, XN, RS = self.X, self.SQ, self.XN, self.RS
        for hh in range(2):
            c.op('act', lambda hh=hh: nc.scalar.activation(out=SQ[:, hh * 4:(hh + 1) * 4, :],
                                                           in_=X[:, hh * 4:(hh + 1) * 4, :], func=AF.Square),
                 r=[X.k], w=[(SQ.k, hh)])
        b, keys = self.ps(1)
        for kc in range(8):
            c.op('pe', lambda kc=kc: nc.tensor.matmul(self.PS[:, b, :], self.cb('ones'), SQ[:, kc, :],
                                                      start=(kc == 0), stop=(kc == 7)),
                 r=[(SQ.k, kc // 4), self.CB.k], w=keys, inc=(kc == 7))
        c.op('act', lambda: nc.scalar.activation(out=RS[:], in_=self.PS[:, b, :], func=AF.Sqrt, bias=self.eps_ap(1),
                                                 scale=1.0), r=keys, w=[RS.k])
        c.op('dve', lambda: nc.vector.reciprocal(out=RS[:], in_=RS[:]), r=[RS.k], w=[RS.k])
        o = self.lc[gkey]
        for kc in range(8):
            c.op('dve', lambda kc=kc: nc.vector.scalar_tensor_tensor(out=XN[:, kc, :], in0=X[:, kc, :],
                                                                     scalar=self.LC[:, o + kc:o + kc + 1],
                                                                     in1=RS[:], op0=ALU.mult, op1=ALU.mult),
                 r=[X.k, RS.k, self.LC.k], w=[(XN.k, kc)])

    def ffn(self, l, pref):
        c, nc = self.c, self.nc
        X, XN, H = self.X, self.XN, self.H
        for ci in range(11):
            slot, W = self.wget(pref + '_w13', l, ci)
            for jj in range(2):
                j = ci * 2 + jj
                b, keys = self.ps(2)
                for half in range(2):
                    col = half * 256 + jj * 128
                    for kc in range(8):
                        c.op('pe', lambda kc=kc, col=col, half=half: nc.tensor.matmul(
                            self.PS[:, b + half, :], W[:, kc, col:col + 128], XN[:, kc, :],
                            start=(kc == 0), stop=(kc == 7)),
                            r=[slot.k] + self.xnk(), w=[keys[half]], inc=(kc == 7))
                sa = self.SA[j % 2]
                c.op('act', lambda: nc.scalar.activation(out=sa[:], in_=self.PS[:, b, :], func=AF.Silu),
                     r=[keys[0]], w=[sa.k])
                c.op('dve', lambda j=j: nc.vector.tensor_tensor(out=H[:, j, :], in0=sa[:], in1=self.PS[:, b + 1, :],
                                                                op=ALU.mult), r=[sa.k, keys[1]], w=[H.k])
        for dd in range(8):
            slot, W = self.wget(pref + '_w2', l, dd)
            b, keys = self.ps(1)
            for kf in range(22):
                c.op('pe', lambda kf=kf: nc.tensor.matmul(self.PS[:, b, :], W[:, kf, :], H[:, kf, :],
                                                          start=(kf == 0), stop=(kf == 21)),
                     r=[slot.k, H.k], w=keys, inc=(kf == 21))
            c.op('dve', lambda dd=dd: nc.vector.scalar_tensor_tensor(out=X[:, dd, :], in0=self.PS[:, b, :], scalar=0.5,
                                                                     in1=X[:, dd, :], op0=ALU.mult, op1=ALU.add),
                 r=keys + [X.k], w=[X.k])

    def lin_fm(self, W, slot, kcs, rhs, rhs_key, ncol, evac):
        c, nc = self.c, self.nc
        for f in range(ncol // 128):
            b, keys = self.ps(1)
            for kc in range(kcs):
                c.op('pe', lambda kc=kc: nc.tensor.matmul(self.PS[:, b, :], W[:, kc, f * 128:(f + 1) * 128],
                                                          rhs[:, kc, :], start=(kc == 0), stop=(kc == kcs - 1)),
                     r=[slot.k] + (list(rhs_key) if isinstance(rhs_key, list) else [rhs_key]), w=keys,
                     inc=(kc == kcs - 1))
            evac(f, self.PS[:, b, :], keys)

    def store(self, dst, src_buf, src_ap, wkeys):
        self.c.dma('pool', dst, src_ap, 'd_st_' + src_buf.k, r=[src_buf.k], w=wkeys)

    def load_x(self, l, ti):
        c, nc = self.c, self.nc
        si, t0, first, last = self.tiles[ti]
        X = self.X
        if l == 0:
            XT = self.BIG
            for cc in range(4):
                c.dma('sp', XT[:, cc * 1024:(cc + 1) * 1024], self.dr['x'][t0 + cc * 128:t0 + (cc + 1) * 128, :],
                      'd_xin%d' % cc, w=[('BIGx', cc)])
            for cc in range(4):
                for half in range(2):
                    b, keys = self.ps(1)
                    for q in range(4):
                        kc = half * 4 + q
                        c.op('pe', lambda kc=kc, q=q: nc.tensor.transpose(
                            out=self.PS[:, b, q * 128:(q + 1) * 128],
                            in_=XT[:, cc * 1024 + kc * 128: cc * 1024 + (kc + 1) * 128], identity=self.cf('ident')),
                            r=[('BIGx', cc), self.CF.k], w=keys, inc=(q == 3))
                    c.op('act', lambda half=half, cc=cc: nc.scalar.activation(
                        out=X[:, half * 4:(half + 1) * 4, cc * 128:(cc + 1) * 128],
                        in_=self.PS[:, b, :].rearrange('p (q t) -> p q t', q=4), func=AF.Copy),
                        r=keys, w=[X.k])
        else:
            c.dma('sp', X[:], self.sc['xT'][:, :, t0:t0 + T].rearrange('k p t -> p k t'), 'd_x',
                  r=[('xT', ti)], w=[X.k])

    def sweep_A(self, l, ti):
        c, nc, sc = self.c, self.nc, self.sc
        si, t0, first, last = self.tiles[ti]
        X, XN, LC, lc = self.X, self.XN, self.LC, self.lc
        BIG = self.BIG
        G0 = Buf(BIG.t, 'A_G0')
        g0v = BIG[:, 4096:6144].bitcast(BF16).rearrange('p (k t) -> p k t', k=8)
        U = Buf(BIG.t, 'A_U')
        uv = BIG[:, 6144:8192].rearrange('p (k t) -> p k t', k=4)
        AOT = Buf(BIG.t, 'A_AOT')
        aov = BIG[:, 8192:9216].bitcast(BF16).rearrange('p (k t) -> p k t', k=4)
        VG = [Buf(BIG.t, 'A_VG%d' % i) for i in range(2)]
        vgv = [BIG[:, 9216 + i * 512: 9216 + (i + 1) * 512] for i in range(2)]
        VN = [Buf(BIG.t, 'A_VN%d' % i) for i in range(2)]
        vnv = [BIG[:, 10240 + i * 256: 10240 + (i + 1) * 256].bitcast(BF16) for i in range(2)]
        MPS = [Buf(BIG.t, 'A_MPS%d' % i) for i in range(2)]
        mpv = [BIG[:, 10752 + i * 512: 10752 + (i + 1) * 512] for i in range(2)]
        JK = Buf(BIG.t, 'A_JK')
        jkv = BIG[:, 11776:12288]
        SM = self.SM

        self.load_x(l, ti)
        self.rmsnorm_fm('ffn1_norm')
        self.ffn(l, 'ffn1')
        c.dma('pool', sc['xT'][:, :, t0:t0 + T].rearrange('k p t -> p k t'), X[:], 'd_st_X', r=[X.k], w=[('xT', ti)])
        self.rmsnorm_fm('mix_norm')
        ob = lc['b_gate']
        for ci in range(6):
            slot, W = self.wget('w_gate', l, ci)

            def evac(f, ps, keys, ci=ci):
                fo = ci * 4 + f
                if fo < 8:
                    c.op('act', lambda: nc.scalar.activation(out=g0v[:, fo, :], in_=ps, func=AF.Sigmoid,
                                                             bias=LC[:, ob + fo:ob + fo + 1]),
                         r=keys + [LC.k], w=[G0.k])
                else:
                    st = self.stage()
                    c.op('act', lambda: nc.scalar.activation(out=st[:, 0:512], in_=ps, func=AF.Sigmoid,
                                                             bias=LC[:, ob + fo:ob + fo + 1]),
                         r=keys + [LC.k], w=[st.k])
                    self.store(sc['g12'][fo - 8, :, t0:t0 + T], st, st[:, 0:512], [('g12', ti, fo - 8)])
            self.lin_fm(W, slot, 8, XN, self.xnk(), 512, evac)
        slot, W = self.wget('w_in', l, 0)

        def evac_u(f, ps, keys):
            c.op('act', lambda: nc.scalar.activation(out=uv[:, f, :], in_=ps, func=AF.Gelu_apprx_tanh),
                 r=keys, w=[U.k])
        self.lin_fm(W, slot, 8, XN, self.xnk(), 512, evac_u)

        def lin_tm(W, slot, ncol, evac):
            for cc in range(4):
                b, keys = self.ps(1)
                for kc in range(8):
                    c.op('pe', lambda kc=kc, cc=cc: nc.tensor.matmul(self.PS[:, b, 0:ncol],
                                                                     XN[:, kc, cc * 128:(cc + 1) * 128],
                                                                     W[:, kc, 0:ncol], start=(kc == 0), stop=(kc == 7)),
                         r=[slot.k] + self.xnk(), w=keys, inc=(kc == 7))
                evac(cc, self.PS[:, b, 0:ncol], keys)
        slot, W = self.wget('w_in', l, 1)
        olg, olb, osb = lc['lg'], lc['lb'], lc['sgu_bias']

        def evac_v(cc, ps, keys):
            vg, vgk = vgv[cc % 2], VG[cc % 2].k
            vn, vnk = vnv[cc % 2], VN[cc % 2].k
            s0 = (cc % 2) * 8
            c.op('act', lambda: nc.scalar.activation(out=vg, in_=ps, func=AF.Gelu_apprx_tanh,
                                                     accum_out=SM[:, s0:s0 + 1]), r=keys, w=[vgk, ('SMa', cc % 2)])
            c.op('act', lambda: nc.scalar.activation(out=jkv, in_=vg, func=AF.Square,
                                                     accum_out=SM[:, s0 + 1:s0 + 2]),
                 r=[vgk], w=[JK.k, ('SMb', cc % 2)])
            c.op('dve', lambda: nc.vector.tensor_scalar(out=SM[:, s0 + 2:s0 + 3], in0=SM[:, s0:s0 + 1],
                                                        scalar1=1.0 / 512, scalar2=None, op0=ALU.mult),
                 r=[('SMa', cc % 2)], w=[('SMc', cc % 2)])
            c.op('dve', lambda: nc.vector.tensor_tensor(out=SM[:, s0 + 3:s0 + 4], in0=SM[:, s0 + 2:s0 + 3],
                                                        in1=SM[:, s0 + 2:s0 + 3], op=ALU.mult),
                 r=[('SMc', cc % 2)], w=[('SMd', cc % 2)])
            c.op('dve', lambda: nc.vector.scalar_tensor_tensor(out=SM[:, s0 + 4:s0 + 5], in0=SM[:, s0 + 1:s0 + 2],
                                                               scalar=1.0 / 512, in1=SM[:, s0 + 3:s0 + 4],
                                                               op0=ALU.mult, op1=ALU.subtract),
                 r=[('SMb', cc % 2), ('SMd', cc % 2)], w=[('SMe', cc % 2)])
            c.op('act', lambda: nc.scalar.activation(out=SM[:, s0 + 5:s0 + 6], in_=SM[:, s0 + 4:s0 + 5], func=AF.Sqrt,
                                                     bias=self.eps_ap(0), scale=1.0),
                 r=[('SMe', cc % 2)], w=[('SMf', cc % 2)])
            c.op('dve', lambda: nc.vector.reciprocal(out=SM[:, s0 + 5:s0 + 6], in_=SM[:, s0 + 5:s0 + 6]),
                 r=[('SMf', cc % 2)], w=[('SMf', cc % 2)])
            c.op('dve', lambda: nc.vector.tensor_scalar(out=vg, in0=vg, scalar1=SM[:, s0 + 2:s0 + 3],
                                                        scalar2=SM[:, s0 + 5:s0 + 6], op0=ALU.subtract, op1=ALU.mult),
                 r=[vgk, ('SMc', cc % 2), ('SMf', cc % 2)], w=[vgk])
            c.op('dve', lambda: nc.vector.tensor_tensor(out=vg, in0=vg, in1=LC[:, olg:olg + 512], op=ALU.mult),
                 r=[vgk, LC.k], w=[vgk])
            c.op('dve', lambda: nc.vector.tensor_tensor(out=vn, in0=vg, in1=LC[:, olb:olb + 512], op=ALU.add),
                 r=[vgk, LC.k], w=[vnk])
            def tail(cc=cc, vn=vn, vnk=vnk):
              b2, k2 = self.ps(1)
              for g in range(4):
                c.op('pe', lambda g=g: nc.tensor.matmul(self.PS[:, b2, g * 128:(g + 1) * 128],
                                                        vn[:, g * 128:(g + 1) * 128],
                                                        self.LCB[:, g * 128:(g + 1) * 128], start=True, stop=True),
                     r=[vnk, self.LCB.k], w=k2, inc=(g == 3))
              c.op('dve', lambda: nc.vector.tensor_tensor(out=jkv, in0=self.PS[:, b2, :], in1=LC[:, osb:osb + 512],
                                                          op=ALU.add), r=k2 + [LC.k], w=[JK.k])
              c.op('dve', lambda: nc.vector.tensor_tensor(out=aov[:, :, cc * 128:(cc + 1) * 128],
                                                          in0=jkv.rearrange('p (g t) -> p g t', g=4),
                                                          in1=uv[:, :, cc * 128:(cc + 1) * 128], op=ALU.mult),
                   r=[JK.k, U.k], w=[AOT.k])
            if pending:
                pending.pop(0)()
            pending.append(tail)
        pending = []
        lin_tm(W, slot, 512, evac_v)
        slot, W = self.wget('w_in', l, 2)

        def evac_pin(cc, ps, keys):
            st = self.stage()
            c.op('act', lambda: nc.scalar.activation(out=st[:, 0:512], in_=ps, func=AF.Copy), r=keys, w=[st.k])
            self.store(sc['pin'][t0 + cc * 128:t0 + (cc + 1) * 128, :], st, st[:, 0:512], [('pin', ti)])
        lin_tm(W, slot, 512, evac_pin)
        while pending:
            pending.pop(0)()
        for zi in range(2):
            slot, W = self.wget('w_in', l, 3 + zi)

            def evac_z(cc, ps, keys, zi=zi):
                st = self.stage()
                c.op('act', lambda: nc.scalar.activation(out=st[:, 0:512], in_=ps, func=AF.Silu), r=keys, w=[st.k])
                self.store(sc['zs'][t0 + cc * 128:t0 + (cc + 1) * 128, zi * 512:(zi + 1) * 512], st, st[:, 0:512],
                           [('zs', ti)])
            lin_tm(W, slot, 512, evac_z)
        for xi in range(2):
            slot, W = self.wget('w_in', l, 5 + xi)

            def evac_x(f, ps, keys, xi=xi):
                st = self.stage()
                c.op('dve', lambda: nc.vector.tensor_copy(out=st[:, 0:512], in_=ps), r=keys, w=[st.k])
                self.store(sc['xbc'][xi * 4 + f, :, t0:t0 + T], st, st[:, 0:512], [('xbc', ti)])
            self.lin_fm(W, slot, 8, XN, self.xnk(), 512, evac_x)
        slot, W = self.wget('w_in', l, 7)

        def evac_x2(f, ps, keys):
            st = self.stage()
            c.op('dve', lambda: nc.vector.tensor_copy(out=st[:, 0:512], in_=ps), r=keys, w=[st.k])
            self.store(sc['xbc'][8 + f, :, t0:t0 + T], st, st[:, 0:512], [('xbc', ti)])
        self.lin_fm(W, slot, 8, XN, self.xnk(), 256, evac_x2)
        for cc in range(4):
            b, keys = self.ps(1)
            for kc in range(8):
                c.op('pe', lambda kc=kc, cc=cc: nc.tensor.matmul(self.PS[:, b, 0:32], XN[:, kc, cc * 128:(cc + 1) * 128],
                                                                 W[:, kc, 256:288], start=(kc == 0), stop=(kc == 7)),
                     r=[slot.k] + self.xnk(), w=keys, inc=(kc == 7))
            c.op('act', lambda cc=cc: nc.scalar.activation(out=SM[:, 64 + cc * 32:64 + (cc + 1) * 32],
                                                           in_=self.PS[:, b, 0:32], func=AF.Copy),
                 r=keys, w=[('SMdt', cc)])
            c.dma('pool', sc['dtr'][t0 + cc * 128:t0 + (cc + 1) * 128, :], SM[:, 64 + cc * 32:64 + (cc + 1) * 32],
                  'd_st_dt%d' % cc, r=[('SMdt', cc)], w=[('dtr', ti)])
        slot, W = self.wget('w_branch_a', l, 0)

        def evac_a(f, ps, keys):
            mv, mk = mpv[f % 2], MPS[f % 2].k
            c.op('dve', lambda: nc.vector.tensor_tensor(out=mv, in0=ps, in1=g0v[:, f, :], op=ALU.mult),
                 r=keys + [G0.k], w=[mk])
            c.dma('pool', sc['mp'][f, :, t0:t0 + T], mv, 'd_st_' + mk, r=[mk], w=[('mp', ti)])
        self.lin_fm(W, slot, 4, aov, AOT.k, 1024, evac_a)

    def sweep_B(self, l, ti):
        c, nc, sc = self.c, self.nc, self.sc
        si, t0, first, last = self.tiles[ti]
        LC, lc, SM, BIG = self.LC, self.lc, self.SM, self.BIG
        XR = Buf(BIG.t, 'B_XR')
        xrv = BIG[:, 0:2580].bitcast(BF16).rearrange('p (f t) -> p f t', f=10)
        XC = Buf(BIG.t, 'B_XC')
        xcv = BIG[:, 2580:5140].bitcast(BF16).rearrange('p (f t) -> p f t', f=10)
        ACC = Buf(BIG.t, 'B_ACC')
        accv = BIG[:, 5140:5652]
        PINB = Buf(BIG.t, 'B_PIN')
        pinv = BIG[:, 5652:7188].bitcast(BF16).rearrange('p (c n) -> p c n', c=6)
        G1 = Buf(BIG.t, 'B_G1')
        g1v = BIG[:, 7188:9236].bitcast(BF16).rearrange('p (k t) -> p k t', k=8)
        POOLED = Buf(BIG.t, 'B_PO')
        pov = BIG[:, 9236:10260].bitcast(BF16).rearrange('p (g t) -> p g t', g=4)
        BO = Buf(BIG.t, 'B_BO')
        bov = BIG[:, 10260:11284].bitcast(BF16).rearrange('p (g t) -> p g t', g=4)
        DTR = Buf(BIG.t, 'B_DTR')
        dtrv = BIG[:, 11284:11412].rearrange('p (c n) -> p c n', c=4)
        HB = self.H
        hflat = HB[:, :, :].rearrange('p a t -> p (a t)')
        XS = Buf(HB.t, 'B_XS')
        xsv = hflat[:, 0:5120].rearrange('p (a n) -> p a n', a=5)
        BT = Buf(HB.t, 'B_BT')
        btv = hflat[:, 5120:5248]
        MM_ = Buf(HB.t, 'B_M')
        mv_ = hflat[:, 5248:9344].rearrange('p (a h q) -> p a h q', a=2, h=16)
        MCB = Buf(HB.t, 'B_MCB')
        mcbv = hflat[:, 9344:9856].rearrange('p (a g q) -> p a g q', a=2, g=2)
        XF = self.X
        xflat = XF[:, :, :].rearrange('p a t -> p (a t)')
        RF = Buf(XF.t, 'B_RF')
        rfv = xflat[:, 0:1536].bitcast(BF16).rearrange('p (j h q) -> p j h q', j=3, h=8)
        EB = Buf(XF.t, 'B_E')
        ebv = xflat[:, 2048:4096].rearrange('p (h q) -> p h q', h=16)
        YL = Buf(self.XN.t, 'B_YL')
        ylv = self.XN[:, :, :].rearrange('p a t -> p (a t)').bitcast(F32)[:, 0:1024]
        SST = [Buf(self.SQ.t, 'B_SST%d' % i) for i in range(2)]
        sstv = [self.SQ[:, :, :].rearrange('p a t -> p (a t)').bitcast(F32)[:, i * 512:(i + 1) * 512] for i in range(2)]
        MPB = Buf(self.RS.t, 'B_MPB')
        mpbv = self.RS[:]
        TMP = self.SA[0]
        TMP2 = self.SA[1]

        seq0 = self.seq_start[si]
        seqL = self.seqs[si]
        if first:
            c.op('pool', lambda: nc.gpsimd.memset(xrv[:, :, 0:2], 0.0), w=[XR.k])
        if last:
            c.op('pool', lambda: nc.gpsimd.memset(xrv[:, :, 514:516], 0.0), w=[XR.k])
        lo = t0 - (0 if first else 2)
        hi = t0 + T + (0 if last else 2)
        tis = [j for j in (ti - 1, ti, ti + 1) if 0 <= j < len(self.tiles)]
        c.dma('sp', xrv[:, :, 2 - (t0 - lo): 514 + (hi - t0 - T)],
              sc['xbc'][:, :, lo:hi].rearrange('f p t -> p f t'), 'd_xr', r=[('xbc', j) for j in tis], w=[XR.k])
        c0 = t0 // CH
        plo = c0 - (0 if first else 1)
        phi = c0 + 4 + (0 if last else 1)
        c.dma('sp', pinv[:, (plo - c0 + 1):(phi - c0 + 1), :],
              sc['pin'][plo * CH:phi * CH, :].rearrange('(c p) n -> p c n', p=128), 'd_pin',
              r=[('pin', j) for j in tis], w=[PINB.k])
        c.dma('sp', dtrv, sc['dtr'][t0:t0 + T, :].rearrange('(c p) n -> p c n', p=128), 'd_dtr',
              r=[('dtr', ti)], w=[DTR.k])
        c.dma('sp', g1v, sc['g12'][0:8, :, t0:t0 + T].rearrange('k p t -> p k t'), 'd_g1',
              r=[('g12', ti, k) for k in range(8)], w=[G1.k])
        if BSTOP == 1:
            return
        ocw, ocb = lc['conv_w'], lc['conv_b']
        self.link([EB.k] + [(EB.k, h) for h in range(16)] + [('B_CT', j) for j in range(4)])
        WX = self.WX
        cts = [xflat[:, 2048:2564], xflat[:, 2564:3080], WX[:, 0:516], WX[:, 516:1032]]
        ktf = self.KT[:, :, :].rearrange('p a t -> p (a t)').bitcast(F32)
        vvf = self.VV[:, :, :].rearrange('p a t -> p (a t)').bitcast(F32)
        accs = [ktf[:, 0:512], ktf[:, 512:1024], vvf[:, 0:512], vvf[:, 512:1024]]
        for g0 in range(0, 10, 4):
            fs = list(range(g0, min(g0 + 4, 10)))
            for j, f in enumerate(fs):
                c.op('act', lambda f=f, j=j: nc.scalar.activation(out=cts[j], in_=xrv[:, f, :], func=AF.Copy),
                     r=[XR.k], w=[('B_CT', j)])
            for k in range(4):
                for j, f in enumerate(fs):
                    if k == 0:
                        c.op('dve', lambda f=f, j=j: nc.vector.tensor_scalar(
                            out=accs[j], in0=cts[j][:, 0:512], scalar1=LC[:, ocw + f:ocw + f + 1], scalar2=None,
                            op0=ALU.mult), r=[('B_CT', j), LC.k], w=[('B_AC', j)])
                    else:
                        c.op('dve', lambda f=f, j=j, k=k: nc.vector.scalar_tensor_tensor(
                            out=accs[j], in0=cts[j][:, k:k + 512],
                            scalar=LC[:, ocw + k * 10 + f:ocw + k * 10 + f + 1], in1=accs[j], op0=ALU.mult,
                            op1=ALU.add), r=[('B_CT', j), LC.k, ('B_AC', j)], w=[('B_AC', j)])
            for j, f in enumerate(fs):
                c.op('act', lambda f=f, j=j: nc.scalar.activation(out=xcv[:, f, :], in_=accs[j], func=AF.Silu,
                                                                  bias=LC[:, ocb + f:ocb + f + 1]),
                     r=[('B_AC', j), LC.k], w=[(XC.k, f)])
        c.op('dve', lambda: nc.vector.memset(SM[:, 1001:1002], 0.0), r=[(XC.k, f) for f in range(10)],
             w=[XC.k, ('SMz2',)])
        if BSTOP == 2:
            return
        CMZ = Buf(BIG.t, 'B_CMZ')
        cmz = BIG[:, 11412:11924].bitcast(BF16).rearrange('p (g t) -> p g t', g=2)
        c.op('pool', lambda: nc.gpsimd.memset(cmz, 0.0), w=[CMZ.k])
        for g in range(2):
            c.op('act', lambda g=g: nc.scalar.activation(out=cmz[g * 64:(g + 1) * 64, g, :],
                                                         in_=xcv[g * 64:(g + 1) * 64, 9, :], func=AF.Copy),
                 r=[XC.k, CMZ.k], w=[CMZ.k])
        c.dma('pool', sc['cm'][:, :, t0:t0 + T].rearrange('g p t -> p g t'), cmz, 'd_st_cm', r=[CMZ.k],
              w=[('cm', ti)])
        odb, oa, ods = lc['dt_bias'], lc['a'], lc['d_skip']
        if BSTOP == 21:
            return
        for cc in range(4):
            chk = c0 + cc
            tsl = slice(cc * 128, (cc + 1) * 128)
            s_dt, s_da, s_t1, s_t2 = 192, 224, 256, 288
            c.op('dve', lambda: nc.vector.tensor_tensor(out=SM[:, s_t1:s_t1 + 32], in0=dtrv[:, cc, :],
                                                        in1=LC[:, odb:odb + 32], op=ALU.add),
                 r=[DTR.k, LC.k], w=[('SM', 't1')])
            c.op('dve', lambda: nc.vector.tensor_scalar(out=SM[:, s_t2:s_t2 + 32], in0=SM[:, s_t1:s_t1 + 32],
                                                        scalar1=-1.0, scalar2=None, op0=ALU.mult),
                 r=[('SM', 't1')], w=[('SM', 't2')])
            c.op('dve', lambda: nc.vector.tensor_tensor(out=SM[:, s_t2:s_t2 + 32], in0=SM[:, s_t2:s_t2 + 32],
                                                        in1=SM[:, s_t1:s_t1 + 32], op=ALU.min),
                 r=[('SM', 't1'), ('SM', 't2')], w=[('SM', 't2')])
            c.op('act', lambda: nc.scalar.activation(out=SM[:, s_t2:s_t2 + 32], in_=SM[:, s_t2:s_t2 + 32],
                                                     func=AF.Exp), r=[('SM', 't2')], w=[('SM', 't2')])
            c.op('act', lambda: nc.scalar.activation(out=SM[:, s_t2:s_t2 + 32], in_=SM[:, s_t2:s_t2 + 32],
                                                     func=AF.Ln, bias=self.eps_ap(2)), r=[('SM', 't2')], w=[('SM', 't2')])
            c.op('dve', lambda: nc.vector.scalar_tensor_tensor(out=SM[:, s_dt:s_dt + 32], in0=SM[:, s_t1:s_t1 + 32],
                                                               scalar=0.0, in1=SM[:, s_t2:s_t2 + 32],
                                                               op0=ALU.max, op1=ALU.add),
                 r=[('SM', 't1'), ('SM', 't2')], w=[('SM', 'dt')])
            c.op('dve', lambda: nc.vector.tensor_tensor(out=SM[:, s_da:s_da + 32], in0=SM[:, s_dt:s_dt + 32],
                                                        in1=LC[:, oa:oa + 32], op=ALU.mult),
                 r=[('SM', 'dt'), LC.k], w=[('SM', 'da')])
            if BSTOP == 22:
                continue
            da3 = SM[:, 480:528].bitcast(BF16).rearrange('p (j n) -> p j n', j=3)
            da_f = SM[:, s_da:s_da + 32]
            r1 = SM[:, 528:560]
            r2 = SM[:, 560:592]
            c.op('dve', lambda: nc.vector.tensor_copy(out=da3[:, 0, :], in_=da_f), r=[('SM', 'da')], w=[('SM', 'd3', 0)])
            c.op('dve', lambda: nc.vector.tensor_tensor(out=r1, in0=da_f, in1=da3[:, 0, :], op=ALU.subtract),
                 r=[('SM', 'da'), ('SM', 'd3', 0)], w=[('SM', 'r1')])
            c.op('dve', lambda: nc.vector.tensor_copy(out=da3[:, 1, :], in_=r1), r=[('SM', 'r1')], w=[('SM', 'd3', 1)])
            c.op('dve', lambda: nc.vector.tensor_tensor(out=r2, in0=r1, in1=da3[:, 1, :], op=ALU.subtract),
                 r=[('SM', 'r1'), ('SM', 'd3', 1)], w=[('SM', 'r2')])
            c.op('dve', lambda: nc.vector.tensor_copy(out=da3[:, 2, :], in_=r2), r=[('SM', 'r2')], w=[('SM', 'd3', 2)])
            d3k = [('SM', 'd3', j) for j in range(3)]
            if BSTOP == 23:
                continue
            b4, k4 = self.ps(1)
            for i, (nm, dcol) in enumerate((('tri_f', 0), ('su', 0), ('tri_b', 16), ('sl', 16))):
                for j in range(3):
                    c.op('pe', lambda i=i, nm=nm, dcol=dcol, j=j: nc.tensor.matmul(
                        self.PS[:, b4, i * 16:(i + 1) * 16], self.cb(nm), da3[:, j, dcol:dcol + 16],
                        start=(j == 0), stop=(j == 2)), r=d3k + [self.CB.k], w=k4, inc=(i == 3 and j == 2))
            if BSTOP == 24:
                continue
            s_e4, s_ac = 320, 384
            BSK = int(os.environ.get('BSK', '0'))
            if BSK != 1:
                c.op('act', lambda: nc.scalar.activation(out=SM[:, s_e4:s_e4 + 64], in_=self.PS[:, b4, 0:64], func=AF.Exp),
                     r=k4, w=[('SM', 'e4')])
            if BSK != 2:
                c.op('act', lambda: nc.scalar.activation(out=SM[:, s_ac:s_ac + 64], in_=self.PS[:, b4, 0:64],
                                                         func=AF.Copy), r=k4, w=[('SM', 'ac')])
            if BSK != 3:
                c.dma('pool', sc['e4'][t0 + cc * 128:t0 + (cc + 1) * 128, :], SM[:, s_e4:s_e4 + 64], 'd_st_e4',
                      r=[('SM', 'e4')], w=[('e4', ti)])
            if BSTOP == 25:
                continue
            s_dtd = 448
            for di in range(2):
                c.op('dve', lambda di=di: nc.vector.tensor_tensor(
                    out=SM[:, s_dtd + di * 16:s_dtd + (di + 1) * 16], in0=SM[:, s_dt + di * 16:s_dt + (di + 1) * 16],
                    in1=SM[:, s_e4 + 16 + di * 32:s_e4 + 32 + di * 32], op=ALU.mult),
                    r=[('SM', 'dt'), ('SM', 'e4')], w=[('SM', 'dtd')])
            if BSTOP == 3:
                continue
            bx, kx = self.ps(1)
            pxb = self.PS[:, bx, :].bitcast(BF16)
            for f in range(8):
                c.op('pe', lambda f=f: nc.tensor.transpose(out=pxb[:, f * 128:(f + 1) * 128], in_=xcv[:, f, tsl],
                                                           identity=self.cb('ident')),
                     r=[XC.k, self.CB.k], w=kx, inc=(f == 7))
            bb, kb = self.ps(1)
            pbb = self.PS[:, bb, :].bitcast(BF16)
            c.op('pe', lambda: nc.tensor.transpose(out=pbb[:, 0:128], in_=xcv[:, 8, tsl], identity=self.cb('ident')),
                 r=[XC.k, self.CB.k], w=kb)
            c.op('act', lambda: nc.scalar.activation(out=btv, in_=pbb[:, 0:128], func=AF.Copy), r=kb, w=[BT.k])
            XCT = Buf(HB.t, 'B_XCT')
            xctv = hflat[:, 9856:10880]
            c.op('act', lambda: nc.scalar.activation(out=xctv, in_=pxb, func=AF.Copy), r=kx, w=[XCT.k])
            kx = [XCT.k]
            px3 = xctv.rearrange('p (h d) -> p h d', h=16)

            def bc16(col):
                a = SM[:, col:col + 16]
                return bass.AP(a.tensor, a.offset, [list(a.ap[0]), [1, 16], [0, 64]])
            srcs = [(s_dt, 'dt'), (s_dt + 16, 'dt'), (s_dtd, 'dtd'), (s_dtd + 16, 'dtd')]
            for i, (col, kk) in enumerate(srcs):
                c.op('dve', lambda i=i, col=col: nc.vector.tensor_tensor(
                    out=xsv[:, i, :].rearrange('p (h d) -> p h d', h=16), in0=px3, in1=bc16(col), op=ALU.mult),
                    r=kx + [('SM', kk)], w=[(XS.k, i)])
            dsk = LC[:, ods:ods + 16]
            c.op('dve', lambda: nc.vector.tensor_tensor(
                out=xsv[:, 4, :].rearrange('p (h d) -> p h d', h=16), in0=px3,
                in1=bass.AP(dsk.tensor, dsk.offset, [list(dsk.ap[0]), [1, 16], [0, 64]]), op=ALU.mult),
                r=kx + [LC.k], w=[(XS.k, 4)])
            if BSTOP == 4:
                continue
            bc_, kc_ = self.ps(1)
            for g in range(2):
                c.op('pe', lambda g=g: nc.tensor.matmul(self.PS[:, bc_, g * 128:(g + 1) * 128],
                                                        xcv[:, 8, tsl], cmz[:, g, tsl],
                                                        start=True, stop=True), r=[XC.k, CMZ.k], w=kc_, inc=(g == 1))
            cbs = SM[:, 600:856]
            c.op('act', lambda: nc.scalar.activation(out=cbs, in_=self.PS[:, bc_, 0:256], func=AF.Copy),
                 r=kc_, w=[('SM', 'cb')])
            for di, nm in enumerate(('tri_f', 'tri_b')):
                tri = self.cf(nm)
                c.op('dve', lambda di=di, tri=tri: nc.vector.tensor_tensor(
                    out=mcbv[:, di, :, :], in0=cbs.rearrange('p (g q) -> p g q', g=2),
                    in1=bass.AP(tri.tensor, tri.offset, [list(tri.ap[0]), [0, 2], [1, 128]]), op=ALU.mult),
                    r=[('SM', 'cb'), self.CF.k], w=[(MCB.k, di)])
            if BSTOP == 5:
                continue
            for di, nm in enumerate(('tri_f', 'tri_b')):
                tri = self.cf(nm)
                trib = self.cb(nm)
                bd, kd = self.ps(4)
                for half in range(2):
                    dsrc = da3[:, :, di * 16 + half * 8: di * 16 + half * 8 + 8]
                    c.op('pool', lambda trib=trib, dsrc=dsrc: nc.gpsimd.tensor_tensor(
                        out=rfv, in0=bass.AP(dsrc.tensor, dsrc.offset, [list(dsrc.ap[0]), [32, 3], [1, 8], [0, 128]]),
                        in1=bass.AP(trib.tensor, trib.offset, [list(trib.ap[0]), [0, 3], [0, 8], [1, 128]]),
                        op=ALU.mult), r=d3k + [self.CB.k], w=[RF.k])
                    for jb in range(2):
                        bank = half * 2 + jb
                        for j in range(3):
                            c.op('pe', lambda j=j, jb=jb, bank=bank: nc.tensor.matmul(
                                self.PS[:, bd + bank, :], self.cb('ones'),
                                rfv[:, j, jb * 4:(jb + 1) * 4, :].rearrange('p h q -> p (h q)'),
                                start=(j == 0), stop=(j == 2)), r=[RF.k, self.CB.k], w=[kd[bank]], inc=(j == 2))
                acol = s_ac + (0 if di == 0 else 32)
                for h in range(16):
                    c.op('act', lambda h=h: nc.scalar.activation(
                        out=ebv[:, h, :], in_=self.PS[:, bd + h // 4, (h % 4) * 128:(h % 4 + 1) * 128], func=AF.Relu,
                        scale=-1.0, bias=SM[:, acol + h:acol + h + 1]),
                        r=[kd[h // 4], ('SM', 'ac')], w=[(EB.k, h)])
                qe = 127 if di == 0 else 0
                dst = self.stage()
                dv = dst[:, 0:16].bitcast(F32)
                for g in range(2):
                    src = self.PS[g * 64:(g + 1) * 64, bd + g * 2:bd + g * 2 + 2, :].rearrange(
                        'p b (h q) -> p (b h) q', h=4)[:, :, qe:qe + 1]
                    c.op('act', lambda g=g, src=src: nc.scalar.activation(
                        out=dv[g * 64:(g + 1) * 64, :].rearrange('p (h o) -> p h o', o=1), in_=src, func=AF.Exp),
                        r=[kd[g * 2], kd[g * 2 + 1]], w=[dst.k])
                self.store(sc['dec'][chk, di], dst, dv, [('dec', chk, di)])
                c.op('act', lambda: nc.scalar.activation(out=ebv, in_=ebv, func=AF.Exp, scale=-1.0),
                     r=[EB.k], w=[EB.k] + [(EB.k, h) for h in range(16)])
                for g in range(2):
                    m = mcbv[:, di, g, :]
                    c.op('dve', lambda g=g, m=m, di=di: nc.vector.tensor_tensor(
                        out=mv_[:, di, g * 8:(g + 1) * 8, :], in0=ebv[:, g * 8:(g + 1) * 8, :],
                        in1=bass.AP(m.tensor, m.offset, [list(m.ap[0]), [0, 8], [1, 128]]), op=ALU.mult),
                        r=[EB.k, (MCB.k, di)] + [(EB.k, h) for h in range(g * 8, g * 8 + 8)], w=[(MM_.k, di, g)])
            if BSTOP == 6:
                continue
            by, ky = self.ps(2)
            for half in range(2):
                c.op('pe', lambda half=half: nc.tensor.matmul(self.PS[:, by + half, :], self.cb('ident'),
                                                              xsv[:, 4, half * 512:(half + 1) * 512],
                                                              start=True, stop=False),
                     r=[(XS.k, 4), self.CB.k], w=[ky[half]], inc=False)
            for h in range(16):
                for di in range(2):
                    lastmm = (h % 8 == 7 and di == 1)
                    c.op('pe', lambda h=h, di=di: nc.tensor.matmul(
                        self.PS[:, by + h // 8, (h % 8) * 64:(h % 8 + 1) * 64], mv_[:, di, h, :],
                        xsv[:, di, h * 64:(h + 1) * 64], start=False, stop=(di == 1)),
                        r=[(MM_.k, di, h // 8), (XS.k, di)], w=[ky[h // 8]], inc=lastmm)
            c.op('act', lambda: nc.scalar.activation(out=ylv, in_=self.PS[:, by:by + 2, :].rearrange('p b n -> p (b n)'),
                                                     func=AF.Copy), r=ky, w=[YL.k])
            c.dma('pool', sc['yl'][t0 + cc * 128:t0 + (cc + 1) * 128, :], ylv, 'd_st_yl', r=[YL.k], w=[('yl', ti)])
            if BSTOP == 7:
                continue
            for di in range(2):
                bs, ks = self.ps(2)
                for g in range(2):
                    c.op('pe', lambda g=g, di=di: nc.tensor.matmul(
                        self.PS[:, bs + g, :], btv, xsv[:, 2 + di, g * 512:(g + 1) * 512], start=True, stop=True),
                        r=[BT.k, (XS.k, 2 + di)], w=[ks[g]])
                sv, sk = sstv[di], SST[di].k
                for g in range(2):
                    c.op('act', lambda sv=sv, g=g: nc.scalar.activation(out=sv[g * 64:(g + 1) * 64, :],
                                                                        in_=self.PS[g * 64:(g + 1) * 64, bs + g, :],
                                                                        func=AF.Copy), r=[ks[g]], w=[sk])
                c.dma('pool', sc['S'][chk, di], sv, 'd_st_' + sk, r=[sk], w=[('S', chk, di)])
            if BSTOP == 8:
                continue
            bp, kp = self.ps(1)
            cfirst = first and cc == 0
            clast = last and cc == 3
            var = 'f' if cfirst else ('l' if clast else 'm')
            for g in range(4):
                nbs = [nb for nb in (-1, 0, 1) if not ((nb == -1 and cfirst) or (nb == 1 and clast))]
                for ii, nb in enumerate(nbs):
                    bn = 'band_%d_%s0' % (g, var) if nb == 0 else 'band_%d_m%d' % (g, nb)
                    c.op('pe', lambda g=g, nb=nb, bn=bn, ii=ii: nc.tensor.matmul(
                        self.PS[:, bp, g * 128:(g + 1) * 128], pinv[:, cc + 1 + nb, g * 128:(g + 1) * 128],
                        self.cb(bn), start=(ii == 0), stop=(ii == len(nbs) - 1)),
                        r=[PINB.k, self.CB.k], w=kp, inc=(g == 3 and ii == len(nbs) - 1))
            irc = CONST_NAMES.index('rc_0_%s' % var)
            assert all(CONST_NAMES.index('rc_%d_%s' % (g, var)) == irc + 3 * g for g in range(4))
            rca = self.CF[:, irc * 128:(irc + 1) * 128]
            c.op('act', lambda: nc.scalar.activation(out=accv, in_=self.PS[:, bp, :], func=AF.Copy), r=kp, w=[ACC.k])
            c.op('dve', lambda: nc.vector.tensor_tensor(
                out=pov[:, :, tsl], in0=accv.rearrange('p (g t) -> p g t', g=4),
                in1=bass.AP(rca.tensor, rca.offset, [list(rca.ap[0]), [3 * 128, 4], [1, 128]]), op=ALU.mult),
                r=[ACC.k, self.CF.k], w=[POOLED.k])
        if BSTOP == 9:
            return
        ops_ = lc['pool_scale']
        for g in range(4):
            b, keys = self.ps(1)
            c.op('pe', lambda g=g: nc.tensor.matmul(self.PS[:, b, :], self.LCB[:, 512 + g * 128:512 + (g + 1) * 128],
                                                    pov[:, g, :], start=True, stop=True),
                 r=[POOLED.k, self.LCB.k], w=keys)
            c.op('act', lambda g=g: nc.scalar.activation(out=bov[:, g, :], in_=self.PS[:, b, :], func=AF.Copy,
                                                         scale=LC[:, ops_ + g:ops_ + g + 1]),
                 r=keys + [LC.k], w=[BO.k])
        slot, W = self.wget('w_branch_b', l, 0)

        sqf = self.SQ[:, :, :].rearrange('p a t -> p (a t)').bitcast(F32)
        xnf = self.XN[:, :, :].rearrange('p a t -> p (a t)').bitcast(F32)
        mpbs = [mpbv, sqf[:, 1024:1536], sqf[:, 1536:2048], xnf[:, 1024:1536]]

        def evac_b(f, ps, keys):
            mpb = mpbs[f % 4]
            mk = ('B_MPB', f % 4)
            c.dma('sp', mpb, sc['mp'][f, :, t0:t0 + T], 'd_mpb%d' % (f % 4), r=[('mp', ti, f)], w=[mk])
            tmv = [TMP[:], TMP2[:], xnf[:, 1536:2048], self.WX[:, 1032:1544]][f % 4]
            tk = ('B_TM', f % 4)
            c.op('dve', lambda: nc.vector.tensor_tensor(out=tmv, in0=ps, in1=g1v[:, f, :], op=ALU.mult),
                 r=keys + [G1.k], w=[tk])
            c.op('pool', lambda: nc.gpsimd.tensor_tensor(out=tmv, in0=tmv, in1=mpb, op=ALU.add),
                 r=[tk, mk], w=[tk])
            c.dma('pool', sc['mp'][f, :, t0:t0 + T], tmv, 'd_st_btm%d' % (f % 4), r=[tk], w=[('mp', ti, f)])
        self.lin_fm(W, slot, 4, bov, BO.k, 1024, evac_b)

    def recurrence(self, l):
        c, nc, sc = self.c, self.nc, self.sc
        BIG = self.BIG
        Hs = Buf(BIG.t, 'R_H')
        hv = BIG[:, 0:512]
        Tm = Buf(BIG.t, 'R_T')
        tv = BIG[:, 512:1024]
        SL = [Buf(BIG.t, 'R_S%d' % i) for i in range(4)]
        slv = [BIG[:, 1024 + i * 512:1024 + (i + 1) * 512] for i in range(4)]
        DL = [Buf(BIG.t, 'R_D%d' % i) for i in range(4)]
        dlv = [BIG[:, 3072 + i * 8:3072 + (i + 1) * 8] for i in range(4)]
        HO = [Buf(BIG.t, 'R_HO%d' % i) for i in range(2)]
        hov = [BIG[:, 3200 + i * 256:3200 + (i + 1) * 256].bitcast(BF16) for i in range(2)]
        n = 0
        for si, Ls in enumerate(self.seqs):
            c0 = self.seq_start[si] // CH
            nch = Ls // CH
            for di in range(2):
                order = list(range(c0, c0 + nch)) if di == 0 else list(range(c0 + nch - 1, c0 - 1, -1))
                c.op('dve', lambda: nc.vector.memset(hv, 0.0), w=[Hs.k])
                for chk in order:
                    i4 = n % 4
                    i2 = n % 2
                    n += 1
                    c.dma('sp', slv[i4], sc['S'][chk, di], 'd_rs%d' % i4, r=[('S', chk, di)], w=[SL[i4].k])
                    c.dma('sp', dlv[i4], sc['dec'][chk, di], 'd_rd%d' % i4, r=[('dec', chk, di)], w=[DL[i4].k])
                    c.op('act', lambda i2=i2: nc.scalar.activation(out=hov[i2], in_=hv, func=AF.Copy),
                         r=[Hs.k], w=[HO[i2].k])
                    c.dma('pool', sc['hp'][chk, di], hov[i2], 'd_st_ho%d' % i2, r=[HO[i2].k], w=[('hp', chk, di)])
                    d_ = dlv[i4]
                    c.op('dve', lambda d_=d_: nc.vector.tensor_tensor(
                        out=tv.rearrange('p (e d) -> p e d', e=8), in0=hv.rearrange('p (e d) -> p e d', e=8),
                        in1=bass.AP(d_.tensor, d_.offset, [list(d_.ap[0]), [1, 8], [0, 64]]), op=ALU.mult),
                        r=[Hs.k, DL[i4].k], w=[Tm.k])
                    c.op('dve', lambda i4=i4: nc.vector.tensor_tensor(out=hv, in0=tv, in1=slv[i4], op=ALU.add),
                         r=[Tm.k, SL[i4].k], w=[Hs.k])

    def mem_kv(self, l, si):
        c, nc = self.c, self.nc
        KT, VV, LC, lc, SM = self.KT, self.VV, self.LC, self.lc, self.SM
        BIG = self.BIG
        MT = Buf(BIG.t, 'K_MT')
        mtv = BIG[:, 0:2048].rearrange('p (c n) -> p c n', c=2)
        MNB = Buf(BIG.t, 'K_MN')
        mnv = BIG[:, 2048:3072].bitcast(BF16).rearrange('p (c n) -> p c n', c=2)
        MNT = Buf(BIG.t, 'K_MNT')
        mntv = BIG[:, 3072:4096].bitcast(BF16).rearrange('p (k m) -> p k m', k=8)
        JK = Buf(BIG.t, 'K_JK')
        jkv = BIG[:, 4096:5120]
        c.dma('sp', mtv, self.dr['mem'][si].rearrange('(c p) n -> p c n', p=128), 'd_mem', w=[MT.k])
        omn = lc['mem_norm']
        for mc in range(2):
            c.op('act', lambda mc=mc: nc.scalar.activation(out=jkv, in_=mtv[:, mc, :], func=AF.Square,
                                                           accum_out=SM[:, 16 + mc:17 + mc]),
                 r=[MT.k], w=[JK.k, ('SMk', mc)])
            c.op('act', lambda mc=mc: nc.scalar.activation(out=SM[:, 18 + mc:19 + mc], in_=SM[:, 16 + mc:17 + mc],
                                                           func=AF.Sqrt, bias=self.eps_ap(0), scale=1.0 / 1024),
                 r=[('SMk', mc)], w=[('SMk2', mc)])
            c.op('dve', lambda mc=mc: nc.vector.reciprocal(out=SM[:, 20 + mc:21 + mc], in_=SM[:, 18 + mc:19 + mc]),
                 r=[('SMk2', mc)], w=[('SMk3', mc)])
            c.op('dve', lambda mc=mc: nc.vector.scalar_tensor_tensor(
                out=mnv[:, mc, :], in0=mtv[:, mc, :], scalar=SM[:, 20 + mc:21 + mc], in1=LC[:, omn:omn + 1024],
                op0=ALU.mult, op1=ALU.mult), r=[MT.k, ('SMk3', mc), LC.k], w=[MNB.k])
            b, keys = self.ps(1)
            pb = self.PS[:, b, :].bitcast(BF16)
            for kc in range(8):
                c.op('pe', lambda kc=kc, mc=mc: nc.tensor.transpose(out=pb[:, kc * 128:(kc + 1) * 128],
                                                                    in_=mnv[:, mc, kc * 128:(kc + 1) * 128],
                                                                    identity=self.cb('ident')),
                     r=[MNB.k, self.CB.k], w=keys, inc=(kc == 7))
            c.op('act', lambda mc=mc: nc.scalar.activation(out=mntv[:, :, mc * 128:(mc + 1) * 128],
                                                           in_=pb.rearrange('p (k m) -> p k m', k=8), func=AF.Copy),
                 r=keys, w=[MNT.k])
        for ci in range(2):
            slot, W = self.wget('xattn_wkv', l, ci)
            for f in range(4):
                b, keys = self.ps(1)
                for kc in range(8):
                    c.op('pe', lambda kc=kc, f=f: nc.tensor.matmul(self.PS[:, b, 0:256], W[:, kc, f * 128:(f + 1) * 128],
                                                                   mntv[:, kc, :], start=(kc == 0), stop=(kc == 7)),
                         r=[slot.k, MNT.k], w=keys, inc=(kc == 7))
                c.op('act', lambda f=f, ci=ci: nc.scalar.activation(out=KT[:, ci * 4 + f, :], in_=self.PS[:, b, 0:256],
                                                                    func=AF.Copy), r=keys, w=[KT.k])
        for ci in range(2):
            slot, W = self.wget('xattn_wkv', l, 2 + ci)
            for mc in range(2):
                b, keys = self.ps(1)
                for kc in range(8):
                    c.op('pe', lambda kc=kc, mc=mc: nc.tensor.matmul(self.PS[:, b, :], mntv[:, kc, mc * 128:(mc + 1) * 128],
                                                                     W[:, kc, :], start=(kc == 0), stop=(kc == 7)),
                         r=[slot.k, MNT.k], w=keys, inc=(kc == 7))
                c.op('act', lambda mc=mc, ci=ci: nc.scalar.activation(out=VV[:, mc, ci * 512:(ci + 1) * 512],
                                                                      in_=self.PS[:, b, :], func=AF.Copy),
                     r=keys, w=[VV.k])

    def sweep_C(self, l, ti):
        c, nc, sc = self.c, self.nc, self.sc
        si, t0, first, last = self.tiles[ti]
        X, XN, LC, lc, SM, BIG = self.X, self.XN, self.LC, self.lc, self.SM, self.BIG
        c0 = t0 // CH
        MP = Buf(BIG.t, 'C_MP')
        mpv = BIG[:, 0:4096].rearrange('p (k t) -> p k t', k=8)
        G2 = Buf(BIG.t, 'C_G2')
        g2v = BIG[:, 4096:6144].bitcast(BF16).rearrange('p (k t) -> p k t', k=8)
        YNT = Buf(BIG.t, 'C_YNT')
        yntv = BIG[:, 6144:8192].bitcast(BF16).rearrange('p (k t) -> p k t', k=8)
        CM = Buf(BIG.t, 'C_CM')
        cmv = self.RS[:].bitcast(BF16).rearrange('p (g t) -> p g t', g=2)
        E4 = Buf(BIG.t, 'C_E4')
        e4v = BIG[:, 8448:8704].rearrange('p (c n) -> p c n', c=4)
        YL = [Buf(BIG.t, 'C_YL%d' % i) for i in range(2)]
        ylv = [BIG[:, 8704 + i * 1024:8704 + (i + 1) * 1024] for i in range(2)]
        ZS = [Buf(BIG.t, 'C_ZS%d' % i) for i in range(2)]
        zsv = [BIG[:, 10752 + i * 512:10752 + (i + 1) * 512].bitcast(BF16) for i in range(2)]
        HP = [Buf(BIG.t, 'C_HP%d' % i) for i in range(2)]
        hpv = [BIG[:, 11776 + i * 512:11776 + (i + 1) * 512].bitcast(BF16) for i in range(2)]
        HB = self.H
        hflat = HB[:, :, :].rearrange('p a t -> p (a t)')
        YT = Buf(HB.t, 'C_YT')
        ytv = hflat[:, 0:2048].bitcast(F32)
        YN = Buf(HB.t, 'C_YN')
        ynv = hflat[:, 2048:3072]
        MGB = Buf(HB.t, 'C_MGB')
        mgv = hflat[:, 3072:7168].rearrange('p (k t) -> p k t', k=8)
        QT = Buf(HB.t, 'C_QT')
        qtv = hflat[:, 7168:11264].rearrange('p (k t) -> p k t', k=8)
        OT = MGB
        otv = mgv
        PRB = Buf(self.SQ.t, 'C_PRB')
        sqflat = self.SQ[:, :, :].rearrange('p a t -> p (a t)')
        prv = sqflat[:, 0:2048].bitcast(F32).rearrange('p (h m) -> p h m', h=4)
        PNB = Buf(self.SQ.t, 'C_PNB')
        pnv = sqflat[:, 2048:3072].rearrange('p (h m) -> p h m', h=4)
        PT = Buf(self.SQ.t, 'C_PT')
        ptv = sqflat[:, 3072:4096].rearrange('p (a s) -> p a s', a=8)

        self.link([self.H.k, YT.k, YN.k, MGB.k, QT.k])
        self.link([self.SQ.k, (self.SQ.k, 0), (self.SQ.k, 1), PRB.k, PNB.k, PT.k] + [(PRB.k, hh) for hh in range(4)])
        c.dma('sp', X[:], sc['xT'][:, :, t0:t0 + T].rearrange('k p t -> p k t'), 'd_x', r=[('xT', ti)], w=[X.k])
        c.dma('sp', mpv, sc['mp'][:, :, t0:t0 + T].rearrange('k p t -> p k t'), 'd_mp', r=[('mp', ti)], w=[MP.k])
        c.dma('sp', g2v, sc['g12'][8:16, :, t0:t0 + T].rearrange('k p t -> p k t'), 'd_g2',
              r=[('g12', ti, k) for k in range(8, 16)], w=[G2.k])
        self.link([self.RS.k, CM.k])
        c.dma('sp', cmv, sc['cm'][:, :, t0:t0 + T].rearrange('g p t -> p g t'), 'd_cm', r=[('cm', ti)], w=[CM.k])
        c.dma('sp', e4v, sc['e4'][t0:t0 + T, :].rearrange('(c p) n -> p c n', p=128), 'd_e4', r=[('e4', ti)], w=[E4.k])
        osn = lc['ssd_norm']
        for cc in range(4):
            chk = c0 + cc
            i2 = cc % 2
            tsl = slice(cc * 128, (cc + 1) * 128)
            c.dma('sp', ylv[i2], sc['yl'][t0 + cc * 128:t0 + (cc + 1) * 128, :], 'd_yl%d' % i2, r=[('yl', ti)],
                  w=[YL[i2].k])
            c.dma('sp', zsv[i2], sc['zs'][t0 + cc * 128:t0 + (cc + 1) * 128, :], 'd_zs%d' % i2, r=[('zs', ti)],
                  w=[ZS[i2].k])
            c.dma('sp', hpv[i2].rearrange('p (a n) -> p a n', a=2), sc['hp'][chk].rearrange('a p n -> p a n'),
                  'd_hp%d' % i2, r=[('hp', chk, 0), ('hp', chk, 1)], w=[HP[i2].k])
            hp3 = hpv[i2].rearrange('p (a n) -> p a n', a=2)
            yv = ylv[i2]
            for di in range(2):
                b, keys = self.ps(2)
                for g in range(2):
                    c.op('pe', lambda g=g, di=di: nc.tensor.matmul(self.PS[:, b + g, :], cmv[:, g, tsl],
                                                                   hp3[:, di, :], start=True, stop=True),
                         r=[CM.k, HP[i2].k], w=[keys[g]])
                ecol = 0 if di == 0 else 32
                ea = e4v[:, cc, ecol:ecol + 16]
                c.op('act', lambda b=b: nc.scalar.activation(
                    out=ytv, in_=self.PS[:, b:b + 2, :].rearrange('p b n -> p (b n)'), func=AF.Copy),
                    r=keys, w=[YT.k])
                c.op('dve', lambda ea=ea, b=b: nc.vector.tensor_tensor(
                    out=ytv.rearrange('p (h d) -> p h d', h=16),
                    in0=ytv.rearrange('p (h d) -> p h d', h=16),
                    in1=bass.AP(ea.tensor, ea.offset, [list(ea.ap[0]), [1, 16], [0, 64]]), op=ALU.mult),
                    r=[YT.k, E4.k], w=[YT.k])
                c.op('pool', lambda: nc.gpsimd.tensor_tensor(out=yv, in0=yv, in1=ytv, op=ALU.add),
                     r=[YL[i2].k, YT.k], w=[YL[i2].k])
            c.op('dve', lambda: nc.vector.tensor_tensor(out=yv, in0=yv, in1=zsv[i2], op=ALU.mult),
                 r=[YL[i2].k, ZS[i2].k], w=[YL[i2].k])
            for g in range(2):
                c.op('act', lambda g=g: nc.scalar.activation(out=ytv[:, g * 512:(g + 1) * 512],
                                                             in_=yv[:, g * 512:(g + 1) * 512], func=AF.Square,
                                                             accum_out=SM[:, 32 + g:33 + g]),
                     r=[YL[i2].k], w=[YT.k, ('SMg', g)])
            c.op('act', lambda: nc.scalar.activation(out=SM[:, 34:36], in_=SM[:, 32:34], func=AF.Sqrt,
                                                     bias=self.eps_ap(0), scale=1.0 / 512),
                 r=[('SMg', 0), ('SMg', 1)], w=[('SMg2',)])
            c.op('dve', lambda: nc.vector.reciprocal(out=SM[:, 36:38], in_=SM[:, 34:36]), r=[('SMg2',)], w=[('SMg3',)])
            for g in range(2):
                c.op('dve', lambda g=g: nc.vector.scalar_tensor_tensor(
                    out=ynv[:, g * 512:(g + 1) * 512], in0=yv[:, g * 512:(g + 1) * 512], scalar=SM[:, 36 + g:37 + g],
                    in1=LC[:, osn + g * 512:osn + (g + 1) * 512], op0=ALU.mult, op1=ALU.mult),
                    r=[YL[i2].k, ('SMg3',), LC.k], w=[YN.k])
            b, keys = self.ps(1)
            pb = self.PS[:, b, :].bitcast(BF16)
            for kc in range(8):
                c.op('pe', lambda kc=kc: nc.tensor.transpose(out=pb[:, kc * 128:(kc + 1) * 128],
                                                             in_=ynv[:, kc * 128:(kc + 1) * 128],
                                                             identity=self.cb('ident')),
                     r=[YN.k, self.CB.k], w=keys, inc=(kc == 7))
            c.op('act', lambda: nc.scalar.activation(out=yntv[:, :, tsl], in_=pb.rearrange('p (k t) -> p k t', k=8),
                                                     func=AF.Copy), r=keys, w=[YNT.k])
        for ci in range(2):
            slot, W = self.wget('w_branch_c', l, ci)

            def evac_c(f, ps, keys, ci=ci):
                dd = ci * 4 + f
                tm = self.SA[dd % 2]
                c.op('dve', lambda: nc.vector.tensor_tensor(out=tm[:], in0=ps, in1=g2v[:, dd, :], op=ALU.mult),
                     r=keys + [G2.k], w=[tm.k])
                c.op('pool', lambda: nc.gpsimd.tensor_tensor(out=mgv[:, dd, :], in0=tm[:], in1=mpv[:, dd, :],
                                                             op=ALU.add), r=[tm.k, MP.k], w=[MGB.k])
            self.lin_fm(W, slot, 8, yntv, YNT.k, 512, evac_c)
        for ci in range(2):
            slot, W = self.wget('w_out', l, ci)

            def evac_o(f, ps, keys, ci=ci):
                dd = ci * 4 + f
                c.op('dve', lambda: nc.vector.tensor_tensor(out=X[:, dd, :], in0=ps, in1=X[:, dd, :], op=ALU.add),
                     r=keys + [X.k], w=[X.k])
            self.lin_fm(W, slot, 8, mgv, MGB.k, 512, evac_o)
        self.link([self.RS.k, CM.k])
        self.link([self.SQ.k, (self.SQ.k, 0), (self.SQ.k, 1), PRB.k, PNB.k, PT.k] + [(PRB.k, hh) for hh in range(4)])
        self.rmsnorm_fm('xattn_norm')
        self.link([self.SQ.k, (self.SQ.k, 0), (self.SQ.k, 1), PRB.k, PNB.k, PT.k] + [(PRB.k, hh) for hh in range(4)])
        for ci in range(2):
            slot, W = self.wget('xattn_wq', l, ci)

            def evac_q(f, ps, keys, ci=ci):
                dd = ci * 4 + f
                c.op('act', lambda: nc.scalar.activation(out=qtv[:, dd, :], in_=ps, func=AF.Copy, scale=1.0 / 16.0),
                     r=keys, w=[QT.k])
            self.lin_fm(W, slot, 8, XN, self.xnk(), 512, evac_q)
        KT, VV = self.KT, self.VV
        for cc in range(4):
            tsl = slice(cc * 128, (cc + 1) * 128)
            b, keys = self.ps(2)
            for hh in range(4):
                for j in range(2):
                    c.op('pe', lambda hh=hh, j=j: nc.tensor.matmul(
                        self.PS[:, b + hh // 2, (hh % 2) * 256:(hh % 2 + 1) * 256], qtv[:, hh * 2 + j, tsl],
                        KT[:, hh * 2 + j, :], start=(j == 0), stop=(j == 1)),
                        r=[QT.k, KT.k], w=[keys[hh // 2]], inc=(j == 1 and hh % 2 == 1))
            sc4 = self.PS[:, b:b + 2, :].rearrange('p b (h m) -> p (b h) m', h=2)
            c.op('act', lambda: nc.scalar.activation(out=prv, in_=sc4, func=AF.Copy), r=keys,
                 w=[PRB.k] + [(PRB.k, hh) for hh in range(4)])
            sc4 = prv
            keys = [PRB.k]
            c.op('dve', lambda: nc.vector.tensor_reduce(out=SM[:, 40:44], in_=sc4, axis=AX.X, op=ALU.max),
                 r=keys, w=[('SMx', 0)])
            c.op('dve', lambda: nc.vector.tensor_scalar(out=SM[:, 44:48], in0=SM[:, 40:44], scalar1=-1.0, scalar2=None,
                                                        op0=ALU.mult), r=[('SMx', 0)], w=[('SMx', 1)])
            for hh in range(4):
                c.op('act', lambda hh=hh: nc.scalar.activation(out=prv[:, hh, :], in_=sc4[:, hh, :], func=AF.Exp,
                                                               bias=SM[:, 44 + hh:45 + hh],
                                                               accum_out=SM[:, 48 + hh:49 + hh]),
                     r=keys + [('SMx', 1)], w=[(PRB.k, hh), ('SMx', 2, hh)])
            c.op('dve', lambda: nc.vector.reciprocal(out=SM[:, 52:56], in_=SM[:, 48:52]),
                 r=[('SMx', 2, hh) for hh in range(4)], w=[('SMx', 3)])
            ri = SM[:, 52:56]
            c.op('dve', lambda: nc.vector.tensor_tensor(
                out=pnv, in0=prv, in1=bass.AP(ri.tensor, ri.offset, [list(ri.ap[0]), [1, 4], [0, 256]]), op=ALU.mult),
                r=[PRB.k, ('SMx', 3)] + [(PRB.k, hh) for hh in range(4)], w=[PNB.k])
            b2, k2 = self.ps(1)
            pb = self.PS[:, b2, :].bitcast(BF16)
            for hh in range(4):
                for mc in range(2):
                    a = hh * 2 + mc
                    c.op('pe', lambda hh=hh, mc=mc, a=a: nc.tensor.transpose(
                        out=pb[:, a * 128:(a + 1) * 128], in_=pnv[:, hh, mc * 128:(mc + 1) * 128],
                        identity=self.cb('ident')), r=[PNB.k, self.CB.k], w=k2, inc=(a == 7))
            c.op('act', lambda: nc.scalar.activation(out=ptv, in_=pb.rearrange('p (a s) -> p a s', a=8), func=AF.Copy),
                 r=k2, w=[PT.k])
            b3, k3 = self.ps(2)
            for hh in range(4):
                for j in range(2):
                    dd = hh * 2 + j
                    for mc in range(2):
                        c.op('pe', lambda hh=hh, j=j, mc=mc, dd=dd: nc.tensor.matmul(
                            self.PS[:, b3 + dd // 4, (dd % 4) * 128:(dd % 4 + 1) * 128],
                            VV[:, mc, hh * 256 + j * 128:hh * 256 + (j + 1) * 128], ptv[:, hh * 2 + mc, :],
                            start=(mc == 0), stop=(mc == 1)),
                            r=[VV.k, PT.k], w=[k3[dd // 4]], inc=(mc == 1 and dd % 4 == 3))
            c.op('act', lambda: nc.scalar.activation(
                out=otv[:, :, tsl], in_=self.PS[:, b3:b3 + 2, :].rearrange('p b (k s) -> p (b k) s', k=4),
                func=AF.Copy), r=k3, w=[MGB.k])
        for ci in range(2):
            slot, W = self.wget('xattn_wo', l, ci)

            def evac_wo(f, ps, keys, ci=ci):
                dd = ci * 4 + f
                c.op('dve', lambda: nc.vector.tensor_tensor(out=X[:, dd, :], in0=ps, in1=X[:, dd, :], op=ALU.add),
                     r=keys + [X.k], w=[X.k])
            self.lin_fm(W, slot, 8, otv, MGB.k, 512, evac_wo)
        self.link([self.SQ.k, (self.SQ.k, 0), (self.SQ.k, 1), PRB.k, PNB.k, PT.k] + [(PRB.k, hh) for hh in range(4)])
        self.rmsnorm_fm('ffn2_norm')
        self.link([self.H.k, YT.k, YN.k, MGB.k, QT.k])
        self.ffn(l, 'ffn2')
        if l < self.depth - 1:
            c.dma('pool', sc['xT'][:, :, t0:t0 + T].rearrange('k p t -> p k t'), X[:], 'd_st_X', r=[X.k],
                  w=[('xT', ti)])
        else:
            ofn = lc['final_norm']
            for cc in range(4):
                i2 = cc % 2
                xo, xok = ylv[i2], YL[i2].k
                for half in range(2):
                    b, keys = self.ps(1)
                    for q in range(4):
                        kc = half * 4 + q
                        c.op('pe', lambda kc=kc, q=q, cc=cc: nc.tensor.transpose(
                            out=self.PS[:, b, q * 128:(q + 1) * 128], in_=X[:, kc, cc * 128:(cc + 1) * 128],
                            identity=self.cf('ident')), r=[X.k, self.CF.k], w=keys, inc=(q == 3))
                    c.op('act', lambda half=half: nc.scalar.activation(out=xo[:, half * 512:(half + 1) * 512],
                                                                       in_=self.PS[:, b, :], func=AF.Copy),
                         r=keys, w=[xok])
                c.op('act', lambda: nc.scalar.activation(out=ytv, in_=xo, func=AF.Square, accum_out=SM[:, 56:57]),
                     r=[xok], w=[self.H.k, YT.k, ('SMf', 0)])
                c.op('act', lambda: nc.scalar.activation(out=SM[:, 57:58], in_=SM[:, 56:57], func=AF.Sqrt,
                                                         bias=self.eps_ap(0), scale=1.0 / 1024),
                     r=[('SMf', 0)], w=[('SMf', 1)])
                c.op('dve', lambda: nc.vector.reciprocal(out=SM[:, 58:59], in_=SM[:, 57:58]), r=[('SMf', 1)],
                     w=[('SMf', 2)])
                c.op('dve', lambda: nc.vector.scalar_tensor_tensor(out=xo, in0=xo, scalar=SM[:, 58:59],
                                                                   in1=LC[:, ofn:ofn + 1024], op0=ALU.mult,
                                                                   op1=ALU.mult), r=[xok, ('SMf', 2), LC.k], w=[xok])
                c.dma('pool', self.dr['y'][t0 + cc * 128:t0 + (cc + 1) * 128, :], xo, 'd_st_y%d' % i2, r=[xok],
                      w=[('y', ti, cc)])


def run(depth, seqs, per_core_inputs, n_cores, trace=False, stop=None):
    bld = Builder(depth, seqs, stop)
    nc = bld.build()
    res = run_bass_kernel_spmd(nc, per_core_inputs, core_ids=list(range(n_cores)), trace=trace)
    return res, bld


def kernel(**inp):
    depth = 4
    seqs = [8192, 4096]
    n = 8
    xp, xs = np.asarray(inp['x_prompt']), np.asarray(inp['x_sample'])
    mp, ms = np.asarray(inp['mem_prompt']), np.asarray(inp['mem_sample'])
    shared = {k: np.ascontiguousarray(np.asarray(inp[k], dtype=np.float32)) for k in WNAMES + SMALL}
    shared['consts'] = CONSTF_ARR
    shared['constsb'] = CONSTB_ARR
    in_maps = []
    for i in range(n):
        d = dict(shared)
        d['x'] = np.ascontiguousarray(np.concatenate([xp[i], xs[i % 4]], axis=0))
        d['mem'] = np.ascontiguousarray(np.stack([mp[i], ms[i % 4]], axis=0))
        in_maps.append(d)
    res, _ = run(depth, seqs, in_maps, n)
    yp = np.stack([res.results[i]['y'][0:8192] for i in range(8)], axis=0)
    ysm = np.stack([res.results[i]['y'][8192:] for i in range(4)], axis=0)
    return (yp.astype(np.float32), ysm.astype(np.float32))
```

```python
import numpy as np
import concourse.bass as bass
import concourse.mybir as mybir
from concourse.bass_utils import run_bass_kernel_spmd

F32 = mybir.dt.float32
BF16 = mybir.dt.bfloat16
AF = mybir.ActivationFunctionType
ALU = mybir.AluOpType
AX = mybir.AxisListType

D = 1024
DFF = 2816
NMEM = 256
NIN = 3872
EPS = 1e-6
T = 512
CH = 128
WSLOT = 4096
import os
NWR = int(os.environ.get("NWR", "3"))
SAME_ENG_SYNC = bool(int(os.environ.get("SES", "1")))
BSTOP = int(os.environ.get('BSTOP', '0'))

WNAMES = ['ffn1_w13', 'ffn1_w2', 'w_in', 'w_gate', 'w_branch_a', 'w_branch_b', 'w_branch_c', 'w_out',
          'xattn_wq', 'xattn_wkv', 'xattn_wo', 'ffn2_w13', 'ffn2_w2']
WSHAPE = {'ffn1_w13': (D, 2 * DFF), 'ffn1_w2': (DFF, D), 'w_in': (D, NIN), 'w_gate': (D, 3 * D),
          'w_branch_a': (512, D), 'w_branch_b': (512, D), 'w_branch_c': (D, D), 'w_out': (D, D),
          'xattn_wq': (D, D), 'xattn_wkv': (D, 2 * D), 'xattn_wo': (D, D), 'ffn2_w13': (D, 2 * DFF),
          'ffn2_w2': (DFF, D)}
SMALL = ['ffn1_norm', 'mix_norm', 'b_gate', 'sgu_ln_g', 'sgu_ln_b', 'sgu_ws', 'sgu_bias', 'pool_w', 'pool_scale',
         'conv_w', 'conv_b', 'dt_bias', 'a_log', 'd_skip', 'ssd_norm', 'xattn_norm', 'mem_norm', 'ffn2_norm',
         'final_norm']
SMALL_SHAPE = {'ffn1_norm': (D,), 'mix_norm': (D,), 'b_gate': (3 * D,), 'sgu_ln_g': (512,), 'sgu_ln_b': (512,),
               'sgu_ws': (4, 128, 128), 'sgu_bias': (4, 128), 'pool_w': (4, 128, 128), 'pool_scale': (512,),
               'conv_w': (4, 1280), 'conv_b': (1280,), 'dt_bias': (2, 16), 'a_log': (2, 16), 'd_skip': (16,),
               'ssd_norm': (D,), 'xattn_norm': (D,), 'mem_norm': (D,), 'ffn2_norm': (D,)}


def wchunks(name):
    K, N = WSHAPE[name]
    KC = K // 128
    if name.endswith('w13'):
        return [(KC, 512, [(c * 256, 256, 0), (DFF + c * 256, 256, 256)]) for c in range(11)]
    if name.endswith('w2'):
        return [(KC, 128, [(c * 128, 128, 0)]) for c in range(8)]
    if name == 'w_in':
        out = [(KC, 512, [(c * 512, 512, 0)]) for c in range(7)]
        out.append((KC, 288, [(3584, 288, 0)]))
        return out
    if name in ('w_branch_a', 'w_branch_b'):
        return [(KC, 1024, [(0, 1024, 0)])]
    return [(KC, 512, [(c * 512, 512, 0)]) for c in range(N // 512)]


def host_consts():
    i = np.arange(128)
    k = i[:, None]
    q = i[None, :]
    blocks = {}
    blocks['ident'] = (k == q)
    blocks['tri_f'] = (k <= q)
    blocks['tri_b'] = (k >= q)
    blocks['su'] = (k > q)
    blocks['sl'] = (k < q)
    blocks['ones'] = np.ones((128, 128))
    wins = (2, 4, 8, 16)
    S3 = 3 * 128
    for g, w in enumerate(wins):
        def band(seq_len, t_off):
            pos = np.arange(seq_len)
            lo = np.clip(pos - w // 2, 0, seq_len)
            hi = np.clip(pos + w - w // 2, 0, seq_len)
            cnt = hi - lo
            out = {}
            for nb in (-1, 0, 1):
                m = np.zeros((128, 128))
                for tt in range(128):
                    tg = t_off + tt
                    for sg in range(lo[tg], hi[tg]):
                        sr = sg - (t_off + nb * 128)
                        if 0 <= sr < 128:
                            m[sr, tt] += 1.0
                    if nb == 0:
                        m[tt, tt] -= cnt[tg]
                out[nb] = m
            return out, 1.0 / cnt[t_off:t_off + 128]
        bm, rcm = band(S3, 128)
        bf, rcf = band(S3, 0)
        bl, rcl = band(S3, 256)
        blocks['band_%d_m-1' % g] = bm[-1]
        blocks['band_%d_m0' % g] = bm[0]
        blocks['band_%d_m1' % g] = bm[1]
        blocks['band_%d_f0' % g] = bf[0]
        blocks['band_%d_l0' % g] = bl[0]
        blocks['rc_%d_m' % g] = np.broadcast_to(rcm[None, :], (128, 128))
        blocks['rc_%d_f' % g] = np.broadcast_to(rcf[None, :], (128, 128))
        blocks['rc_%d_l' % g] = np.broadcast_to(rcl[None, :], (128, 128))
    fnames = ['ident', 'tri_f', 'tri_b', 'su', 'sl', 'ones']
    for g in range(4):
        fnames += ['rc_%d_m' % g, 'rc_%d_f' % g, 'rc_%d_l' % g]
    bnames = ['ident', 'ones', 'tri_f', 'tri_b', 'su', 'sl'] + [n for n in blocks if n.startswith('band_')]
    af = np.concatenate([np.asarray(blocks[n], dtype=np.float32) for n in fnames], axis=1)
    ab = np.concatenate([np.asarray(blocks[n], dtype=np.float32) for n in bnames], axis=1)
    return fnames, bnames, np.ascontiguousarray(af), np.ascontiguousarray(ab)


CONST_NAMES, CONSTB_NAMES, CONSTF_ARR, CONSTB_ARR = host_consts()
NCONST = CONSTF_ARR.shape[1]
NCONSTB = CONSTB_ARR.shape[1]


class Ctx:
    def __init__(self, nc):
        self.nc = nc
        self.E = {'pe': nc.tensor, 'act': nc.scalar, 'dve': nc.vector, 'pool': nc.gpsimd, 'sp': nc.sync}
        self.sem = {}
        self.cnt = {}
        self.seen = {e: {} for e in self.E}
        self.lastw = {}
        self.rd = {}
        self.nins = 0
        for e in self.E:
            self.newsem('E_' + e)

    def newsem(self, name):
        self.sem[name] = self.nc.alloc_semaphore(name)
        self.cnt[name] = 0

    def _need(self, r, w):
        need = {}

        def add(tok):
            if tok is None:
                return
            s, v = tok
            if need.get(s, 0) < v:
                need[s] = v
        for k in r:
            add(self.lastw.get(k))
        for k in w:
            add(self.lastw.get(k))
            for s, v in self.rd.get(k, {}).items():
                add((s, v))
        return need

    def _wait(self, e, need):
        seen = self.seen[e]
        own = 'E_' + e
        for s, v in need.items():
            if s == own and (e == 'pe' or not SAME_ENG_SYNC):
                continue
            if seen.get(s, 0) < v:
                self.E[e].wait_ge(self.sem[s], v)
                seen[s] = v
                self.nins += 1

    def _mark(self, tok, r, w):
        s, v = tok
        for k in r:
            d = self.rd.setdefault(k, {})
            if d.get(s, 0) < v:
                d[s] = v
        for k in w:
            self.lastw[k] = tok
            self.rd[k] = {}

    def op(self, e, fn, r=(), w=(), inc=True):
        self._wait(e, self._need(r, w))
        ins = fn()
        self.nins += 1
        s = 'E_' + e
        if inc:
            ins.then_inc(self.sem[s], 1)
            self.cnt[s] += 1
            tok = (s, self.cnt[s])
        else:
            tok = (s, self.cnt[s] + 1)
        self._mark(tok, r, w)
        return ins

    def dma(self, q, out, in_, sem, r=(), w=(), **kw):
        if sem not in self.sem:
            self.newsem(sem)
        self._wait(q, self._need(r, w))
        ins = self.E[q].dma_start(out=out, in_=in_, **kw)
        ins.then_inc(self.sem[sem], 16)
        self.nins += 1
        self.cnt[sem] += 16
        self._mark((sem, self.cnt[sem]), r, w)

    def final_wait(self, e):
        need = {}
        for s in self.sem:
            if self.cnt[s] > 0:
                need[s] = self.cnt[s]
        self._wait(e, need)


class Buf:
    def __init__(self, t, key):
        self.t = t
        self.k = key

    def __getitem__(self, idx):
        return self.t[idx]


def bcast_ap(ap, dims):
    return bass.AP(ap.tensor, ap.offset, dims)


class Builder:
    def __init__(self, depth, seqs, stop=None):
        self.stop = stop
        self.depth = depth
        self.seqs = list(seqs)
        self.NT = sum(seqs)
        self.NCHK = self.NT // CH
        self.tiles = []
        t0 = 0
        for si, L in enumerate(seqs):
            assert L % T == 0
            for j in range(L // T):
                self.tiles.append((si, t0 + j * T, j == 0, j == L // T - 1))
            t0 += L
        self.seq_start = [sum(seqs[:i]) for i in range(len(seqs))]

    def build(self):
        nc = bass.Bass("TRN2", target_bir_lowering=False)
        self.nc = nc
        c = Ctx(nc)
        self.c = c
        L = self.depth
        NT = self.NT
        nseq = len(self.seqs)
        dr = {}
        dr['x'] = nc.dram_tensor('x', [NT, D], F32, kind='ExternalInput').ap()
        dr['mem'] = nc.dram_tensor('mem', [nseq, NMEM, D], F32, kind='ExternalInput').ap()
        dr['consts'] = nc.dram_tensor('consts', [128, NCONST], F32, kind='ExternalInput').ap()
        dr['constsb'] = nc.dram_tensor('constsb', [128, NCONSTB], F32, kind='ExternalInput').ap()
        for n in WNAMES:
            dr[n] = nc.dram_tensor(n, [L] + list(WSHAPE[n]), F32, kind='ExternalInput').ap()
        for n in SMALL:
            shp = ([L] + list(SMALL_SHAPE[n])) if n != 'final_norm' else [D]
            dr[n] = nc.dram_tensor(n, shp, F32, kind='ExternalInput').ap()
        dr['y'] = nc.dram_tensor('y', [NT, D], F32, kind='ExternalOutput').ap()
        sc = {}
        for n in WNAMES:
            ch = wchunks(n)
            sc[n] = nc.dram_tensor('s_' + n, [L, len(ch), 128, WSLOT], BF16, kind='Internal').ap()
        sc['xT'] = nc.dram_tensor('s_xT', [8, 128, NT], F32, kind='Internal').ap()
        sc['mp'] = nc.dram_tensor('s_mp', [8, 128, NT], F32, kind='Internal').ap()
        sc['g12'] = nc.dram_tensor('s_g12', [16, 128, NT], BF16, kind='Internal').ap()
        sc['zs'] = nc.dram_tensor('s_zs', [NT, D], BF16, kind='Internal').ap()
        sc['pin'] = nc.dram_tensor('s_pin', [NT, 512], BF16, kind='Internal').ap()
        sc['xbc'] = nc.dram_tensor('s_xbc', [10, 128, NT], BF16, kind='Internal').ap()
        sc['dtr'] = nc.dram_tensor('s_dtr', [NT, 32], F32, kind='Internal').ap()
        sc['yl'] = nc.dram_tensor('s_yl', [NT, D], F32, kind='Internal').ap()
        sc['cm'] = nc.dram_tensor('s_cm', [2, 128, NT], BF16, kind='Internal').ap()
        sc['e4'] = nc.dram_tensor('s_e4', [NT, 64], F32, kind='Internal').ap()
        sc['S'] = nc.dram_tensor('s_S', [self.NCHK, 2, 128, 512], F32, kind='Internal').ap()
        sc['dec'] = nc.dram_tensor('s_dec', [self.NCHK, 2, 128, 8], F32, kind='Internal').ap()
        sc['hp'] = nc.dram_tensor('s_hp', [self.NCHK, 2, 128, 512], BF16, kind='Internal').ap()
        self.dr = dr
        self.sc = sc

        def sb(name, shape, dt):
            return Buf(nc.alloc_sbuf_tensor(name, shape, dt), name)
        self.sb = sb
        self.PS = nc.alloc_psum_tensor('psum', [128, 8, 512], F32)
        self.psn = 0
        self.CF = sb('CF', [128, NCONST], F32)
        self.CB = sb('CB', [128, NCONSTB], BF16)
        self.X = sb('X', [128, 8, T], F32)
        self.SQ = sb('SQ', [128, 8, T], BF16)
        self.XN = sb('XN', [128, 8, T], BF16)
        self.RS = sb('RS', [128, T], F32)
        self.H = sb('H', [128, 22, T], BF16)
        self.SA = [sb('SA%d' % i, [128, T], F32) for i in range(2)]
        self.WR = [sb('WR%d' % i, [128, WSLOT], BF16) for i in range(NWR)]
        self.WX = sb('WX', [128, 2048], F32)
        self.BIG = sb('BIG', [128, 12800], F32)
        self.ST = [sb('ST%d' % i, [128, 512], BF16) for i in range(6)]
        self.sti = 0
        self.LC = sb('LC', [128, 4864], F32)
        self.LCB = sb('LCB', [128, 1024], BF16)
        self.SM = sb('SM', [128, 1024], F32)
        self.wplan = []
        self.wpos = 0
        self.wissued = 0

        self.KT = sb('KT', [128, 8, 256], BF16)
        self.VV = sb('VV', [128, 2, 1024], BF16)
        c.dma('sp', self.CF[:], dr['consts'][:, :], 'd_const', w=[self.CF.k])
        c.dma('sp', self.BIG[:, 0:NCONSTB], dr['constsb'][:, :], 'd_const', w=[self.BIG.k])
        c.op('dve', lambda: nc.vector.tensor_copy(out=self.CB[:], in_=self.BIG[:, 0:NCONSTB]), r=[self.BIG.k],
             w=[self.CB.k])
        c.op('dve', lambda: nc.vector.memset(self.SM[:, 1010:1011], EPS), w=[('SMc0',)])
        c.op('dve', lambda: nc.vector.memset(self.SM[:, 1011:1012], 1024.0 * EPS), w=[('SMc1',)])
        c.op('dve', lambda: nc.vector.memset(self.SM[:, 1012:1013], 1.0), w=[('SMc2',)])
        self.barrier()
        stop = self.stop

        def done(tag):
            if stop == tag:
                self.barrier()
                return True
            return False
        self.prep_weights()
        self.barrier()
        if done('prep'):
            return nc
        for l in range(L):
            self.layer_consts(l)
            self.barrier()
            if done('lc'):
                return nc
            self.plan_layer(l)
            for ti in range(len(self.tiles)):
                self.sweep_A(l, ti)
                if done('A0'):
                    return nc
            self.barrier()
            if done('A'):
                return nc
            for ti in range(len(self.tiles)):
                self.sweep_B(l, ti)
            self.barrier()
            if done('B'):
                return nc
            self.recurrence(l)
            self.barrier()
            if done('R'):
                return nc
            for si in range(len(self.seqs)):
                self.mem_kv(l, si)
                self.barrier()
                if done('KV'):
                    return nc
                for ti in range(len(self.tiles)):
                    if self.tiles[ti][0] == si:
                        self.sweep_C(l, ti)
            assert self.wpos == len(self.wplan), (self.wpos, len(self.wplan))
            self.barrier()
        c.final_wait('pool')
        return nc

    def cf(self, name):
        i = CONST_NAMES.index(name)
        return self.CF[:, i * 128:(i + 1) * 128]

    def barrier(self):
        c = self.c
        need = {s: v for s, v in c.cnt.items() if v > 0}
        for e in c.E:
            c._wait(e, dict(need))
        c.lastw = {k: v for k, v in c.lastw.items() if isinstance(k, tuple) and k[0] == 'wsc'}
        c.rd = {}

    def link(self, keys):
        c, nc = self.c, self.nc
        c.op('dve', lambda: nc.vector.memset(self.SM[:, 1000:1001], 0.0), w=list(keys) + [('SMz',)])

    def xnk(self):
        return [(self.XN.k, kc) for kc in range(8)]

    def eps_ap(self, which):
        return self.SM[:, 1010 + which:1011 + which]

    def cb(self, name):
        i = CONSTB_NAMES.index(name)
        return self.CB[:, i * 128:(i + 1) * 128]

    def ps(self, n=1):
        if self.psn + n > 8:
            self.psn = 0
        b = self.psn
        self.psn = (self.psn + n) % 8
        keys = [('ps', b + i) for i in range(n)]
        return b, keys

    def stage(self):
        s = self.ST[self.sti % len(self.ST)]
        self.sti += 1
        return s

    def prep_weights(self):
        c, nc = self.c, self.nc
        n = 0
        for name in WNAMES:
            chs = wchunks(name)
            for l in range(self.depth):
                for ci, (KC, wc, ranges) in enumerate(chs):
                    dst = self.sc[name][l, ci]
                    for (c0, ncol, off) in ranges:
                        src = self.dr[name][l, :, c0:c0 + ncol].rearrange('(kc p) n -> p kc n', p=128)
                        d = dst[:, 0:KC * wc].rearrange('p (kc n) -> p kc n', kc=KC)[:, :, off:off + ncol]
                        sem = 'd_prep%d' % (n % 8)
                        n += 1
                        c.dma('pool', d, src, sem, w=[('wsc', name, l, ci)])

    def plan_layer(self, l):
        plan = []
        for ti in range(len(self.tiles)):
            for nm in ('ffn1_w13', 'ffn1_w2', 'w_gate', 'w_in', 'w_branch_a'):
                for ci in range(len(wchunks(nm))):
                    plan.append((nm, l, ci))
        for ti in range(len(self.tiles)):
            plan.append(('w_branch_b', l, 0))
        for si in range(len(self.seqs)):
            for ci in range(4):
                plan.append(('xattn_wkv', l, ci))
            for ti in range(len(self.tiles)):
                if self.tiles[ti][0] != si:
                    continue
                for nm in ('w_branch_c', 'w_out', 'xattn_wq', 'xattn_wo', 'ffn2_w13', 'ffn2_w2'):
                    for ci in range(len(wchunks(nm))):
                        plan.append((nm, l, ci))
        self.wplan.extend(plan)

    def _issue_w(self):
        i = self.wissued
        nm, l, ci = self.wplan[i]
        KC, wc, _ = wchunks(nm)[ci]
        slot = self.WR[i % NWR]
        n = KC * wc
        self.c.dma('sp', slot[:, 0:n], self.sc[nm][l, ci, :, 0:n], 'd_w%d' % (i % NWR),
                   r=[('wsc', nm, l, ci)], w=[slot.k])
        self.wissued += 1

    def wget(self, nm, l, ci):
        assert self.wplan[self.wpos] == (nm, l, ci), (self.wplan[self.wpos], (nm, l, ci))
        while self.wissued < min(len(self.wplan), self.wpos + NWR - 1) or self.wissued <= self.wpos:
            self._issue_w()
        slot = self.WR[self.wpos % NWR]
        self.wpos += 1
        KC, wc, _ = wchunks(nm)[ci]
        return slot, slot[:, 0:KC * wc].rearrange('p (kc n) -> p kc n', kc=KC)

    def layer_consts(self, l):
        c, nc, dr = self.c, self.nc, self.dr
        LC = self.LC
        lc = {}
        off = [0]

        def alloc(n):
            o = off[0]
            off[0] += n
            return o

        def load_bc(src_ap, n, key):
            o = alloc(n)
            src = bass.AP(src_ap.tensor, src_ap.offset, [[0, 128], [1, n]])
            c.dma('sp', LC[:, o:o + n], src, 'd_lc', w=[LC.k])
            lc[key] = o
        BG = self.BIG
        rows = 0
        pp = [('ffn1_norm', 8), ('mix_norm', 8), ('xattn_norm', 8), ('ffn2_norm', 8), ('b_gate', 24),
              ('pool_scale', 4), ('conv_b', 10)]
        for name, nk in pp:
            c.dma('sp', BG[rows:rows + nk, 2048:2176], dr[name][l].rearrange('(kc p) -> kc p', p=128), 'd_lcpp',
                  w=[('BGpp',)])
            lc[name] = alloc(nk)
            rows += nk
        c.dma('sp', BG[rows:rows + 40, 2048:2176], dr['conv_w'][l].rearrange('k (f p) -> (k f) p', p=128), 'd_lcpp',
              w=[('BGpp',)])
        lc['conv_w'] = alloc(40)
        rows += 40
        b0, keys0 = self.ps(1)
        c.op('pe', lambda: nc.tensor.transpose(out=self.PS[:, b0, 0:rows], in_=BG[0:rows, 2048:2176],
                                               identity=self.cf('ident')[0:rows, 0:rows]),
             r=[('BGpp',), self.CF.k], w=keys0)
        c.op('dve', lambda: nc.vector.tensor_copy(out=LC[:, 0:rows], in_=self.PS[:, b0, 0:rows]), r=keys0, w=[LC.k])
        load_bc(dr['sgu_ln_g'][l], 512, 'lg')
        load_bc(dr['sgu_ln_b'][l], 512, 'lb')
        load_bc(dr['sgu_bias'][l].rearrange('g t -> (g t)'), 512, 'sgu_bias')
        load_bc(dr['dt_bias'][l].rearrange('a h -> (a h)'), 32, 'dt_bias')
        load_bc(dr['a_log'][l].rearrange('a h -> (a h)'), 32, 'a_log')
        load_bc(dr['d_skip'][l], 16, 'd_skip')
        load_bc(dr['ssd_norm'][l], 1024, 'ssd_norm')
        load_bc(dr['mem_norm'][l], 1024, 'mem_norm')
        load_bc(dr['final_norm'], 1024, 'final_norm')
        for k in ('ffn1_norm', 'mix_norm', 'xattn_norm', 'ffn2_norm'):
            o = lc[k]
            c.op('dve', lambda o=o: nc.vector.tensor_scalar(out=LC[:, o:o + 8], in0=LC[:, o:o + 8], scalar1=32.0,
                                                            scalar2=None, op0=ALU.mult), r=[LC.k], w=[LC.k])
        o = lc['a_log']
        c.op('act', lambda: nc.scalar.activation(out=LC[:, o:o + 32], in_=LC[:, o:o + 32], func=AF.Exp),
             r=[LC.k], w=[LC.k])
        c.op('dve', lambda: nc.vector.tensor_scalar(out=LC[:, o:o + 32], in0=LC[:, o:o + 32], scalar1=-1.0,
                                                    scalar2=None, op0=ALU.mult), r=[LC.k], w=[LC.k])
        lc['a'] = o
        o = 0
        c.dma('sp', BG[:, o:o + 512].rearrange('p (g s) -> p g s', g=4),
              dr['sgu_ws'][l].rearrange('g t s -> t g s'), 'd_lc3', w=[BG.k])
        b, keys = self.ps(1)
        for g in range(4):
            c.op('pe', lambda g=g: nc.tensor.transpose(out=self.PS[:, b, g * 128:(g + 1) * 128],
                                                       in_=BG[:, o + g * 128:o + (g + 1) * 128],
                                                       identity=self.cf('ident')),
                 r=[BG.k, self.CF.k], w=keys, inc=(g == 3))
        c.op('dve', lambda: nc.vector.tensor_copy(out=self.LCB[:, 0:512], in_=self.PS[:, b, :]),
             r=keys, w=[self.LCB.k])
        o2 = 512
        c.dma('sp', BG[:, o2:o2 + 512].rearrange('p (g e) -> p g e', g=4),
              dr['pool_w'][l].rearrange('g d e -> d g e'), 'd_lc2', w=[('BG2',)])
        c.op('dve', lambda: nc.vector.tensor_copy(out=self.LCB[:, 512:1024], in_=BG[:, o2:o2 + 512]),
             r=[('BG2',)], w=[self.LCB.k])
        assert off[0] <= 4864
        self.lc = lc

    def rmsnorm_fm(self, gkey):
        c, nc = self.c, self.nc
        X, SQ, XN, RS = self.X, self.SQ, self.XN, self.RS
        for hh in range(2):
            c.op('act', lambda hh=hh: nc.scalar.activation(out=SQ[:, hh * 4:(hh + 1) * 4, :],
                                                           in_=X[:, hh * 4:(hh + 1) * 4, :], func=AF.Square),
                 r=[X.k], w=[(SQ.k, hh)])
        b, keys = self.ps(1)
        for kc in range(8):
            c.op('pe', lambda kc=kc: nc.tensor.matmul(self.PS[:, b, :], self.cb('ones'), SQ[:, kc, :],
                                                      start=(kc == 0), stop=(kc == 7)),
                 r=[(SQ.k, kc // 4), self.CB.k], w=keys, inc=(kc == 7))
        c.op('act', lambda: nc.scalar.activation(out=RS[:], in_=self.PS[:, b, :], func=AF.Sqrt, bias=self.eps_ap(1),
                                                 scale=1.0), r=keys, w=[RS.k])
        c.op('dve', lambda: nc.vector.reciprocal(out=RS[:], in_=RS[:]), r=[RS.k], w=[RS.k])
        o = self.lc[gkey]
        for kc in range(8):
            c.op('dve', lambda kc=kc: nc.vector.scalar_tensor_tensor(out=XN[:, kc, :], in0=X[:, kc, :],
                                                                     scalar=self.LC[:, o + kc:o + kc + 1],
                                                                     in1=RS[:], op0=ALU.mult, op1=ALU.mult),
                 r=[X.k, RS.k, self.LC.k], w=[(XN.k, kc)])

    def ffn(self, l, pref):
        c, nc = self.c, self.nc
        X, XN, H = self.X, self.XN, self.H
        for ci in range(11):
            slot, W = self.wget(pref + '_w13', l, ci)
            for jj in range(2):
                j = ci * 2 + jj
                b, keys = self.ps(2)
                for half in range(2):
                    col = half * 256 + jj * 128
                    for kc in range(8):
                        c.op('pe', lambda kc=kc, col=col, half=half: nc.tensor.matmul(
                            self.PS[:, b + half, :], W[:, kc, col:col + 128], XN[:, kc, :],
                            start=(kc == 0), stop=(kc == 7)),
                            r=[slot.k] + self.xnk(), w=[keys[half]], inc=(kc == 7))
                sa = self.SA[j % 2]
                c.op('act', lambda: nc.scalar.activation(out=sa[:], in_=self.PS[:, b, :], func=AF.Silu),
                     r=[keys[0]], w=[sa.k])
                c.op('dve', lambda j=j: nc.vector.tensor_tensor(out=H[:, j, :], in0=sa[:], in1=self.PS[:, b + 1, :],
                                                                op=ALU.mult), r=[sa.k, keys[1]], w=[H.k])
        for dd in range(8):
            slot, W = self.wget(pref + '_w2', l, dd)
            b, keys = self.ps(1)
            for kf in range(22):
                c.op('pe', lambda kf=kf: nc.tensor.matmul(self.PS[:, b, :], W[:, kf, :], H[:, kf, :],
                                                          start=(kf == 0), stop=(kf == 21)),
                     r=[slot.k, H.k], w=keys, inc=(kf == 21))
            c.op('dve', lambda dd=dd: nc.vector.scalar_tensor_tensor(out=X[:, dd, :], in0=self.PS[:, b, :], scalar=0.5,
                                                                     in1=X[:, dd, :], op0=ALU.mult, op1=ALU.add),
                 r=keys + [X.k], w=[X.k])

    def lin_fm(self, W, slot, kcs, rhs, rhs_key, ncol, evac):
        c, nc = self.c, self.nc
        for f in range(ncol // 128):
            b, keys = self.ps(1)
            for kc in range(kcs):
                c.op('pe', lambda kc=kc: nc.tensor.matmul(self.PS[:, b, :], W[:, kc, f * 128:(f + 1) * 128],
                                                          rhs[:, kc, :], start=(kc == 0), stop=(kc == kcs - 1)),
                     r=[slot.k] + (list(rhs_key) if isinstance(rhs_key, list) else [rhs_key]), w=keys,
                     inc=(kc == kcs - 1))
            evac(f, self.PS[:, b, :], keys)

    def store(self, dst, src_buf, src_ap, wkeys):
        self.c.dma('pool', dst, src_ap, 'd_st_' + src_buf.k, r=[src_buf.k], w=wkeys)

    def load_x(self, l, ti):
        c, nc = self.c, self.nc
        si, t0, first, last = self.tiles[ti]
        X = self.X
        if l == 0:
            XT = self.BIG
            for cc in range(4):
                c.dma('sp', XT[:, cc * 1024:(cc + 1) * 1024], self.dr['x'][t0 + cc * 128:t0 + (cc + 1) * 128, :],
                      'd_xin%d' % cc, w=[('BIGx', cc)])
            for cc in range(4):
                for half in range(2):
                    b, keys = self.ps(1)
                    for q in range(4):
                        kc = half * 4 + q
                        c.op('pe', lambda kc=kc, q=q: nc.tensor.transpose(
                            out=self.PS[:, b, q * 128:(q + 1) * 128],
                            in_=XT[:, cc * 1024 + kc * 128: cc * 1024 + (kc + 1) * 128], identity=self.cf('ident')),
                            r=[('BIGx', cc), self.CF.k], w=keys, inc=(q == 3))
                    c.op('act', lambda half=half, cc=cc: nc.scalar.activation(
                        out=X[:, half * 4:(half + 1) * 4, cc * 128:(cc + 1) * 128],
                        in_=self.PS[:, b, :].rearrange('p (q t) -> p q t', q=4), func=AF.Copy),
                        r=keys, w=[X.k])
        else:
            c.dma('sp', X[:], self.sc['xT'][:, :, t0:t0 + T].rearrange('k p t -> p k t'), 'd_x',
                  r=[('xT', ti)], w=[X.k])

    def sweep_A(self, l, ti):
        c, nc, sc = self.c, self.nc, self.sc
        si, t0, first, last = self.tiles[ti]
        X, XN, LC, lc = self.X, self.XN, self.LC, self.lc
        BIG = self.BIG
        G0 = Buf(BIG.t, 'A_G0')
        g0v = BIG[:, 4096:6144].bitcast(BF16).rearrange('p (k t) -> p k t', k=8)
        U = Buf(BIG.t, 'A_U')
        uv = BIG[:, 6144:8192].rearrange('p (k t) -> p k t', k=4)
        AOT = Buf(BIG.t, 'A_AOT')
        aov = BIG[:, 8192:9216].bitcast(BF16).rearrange('p (k t) -> p k t', k=4)
        VG = [Buf(BIG.t, 'A_VG%d' % i) for i in range(2)]
        vgv = [BIG[:, 9216 + i * 512: 9216 + (i + 1) * 512] for i in range(2)]
        VN = [Buf(BIG.t, 'A_VN%d' % i) for i in range(2)]
        vnv = [BIG[:, 10240 + i * 256: 10240 + (i + 1) * 256].bitcast(BF16) for i in range(2)]
        MPS = [Buf(BIG.t, 'A_MPS%d' % i) for i in range(2)]
        mpv = [BIG[:, 10752 + i * 512: 10752 + (i + 1) * 512] for i in range(2)]
        JK = Buf(BIG.t, 'A_JK')
        jkv = BIG[:, 11776:12288]
        SM = self.SM

        if getattr(self, 'x_pre', None) != (l, ti):
            self.load_x(l, ti)
        self.rmsnorm_fm('ffn1_norm')
        self.ffn(l, 'ffn1')
        c.dma('pool', sc['xT'][:, :, t0:t0 + T].rearrange('k p t -> p k t'), X[:], 'd_st_X', r=[X.k], w=[('xT', ti)])
        self.rmsnorm_fm('mix_norm')
        if ti + 1 < len(self.tiles):
            self.load_x(l, ti + 1)
            self.x_pre = (l, ti + 1)
        ob = lc['b_gate']
        for ci in range(6):
            slot, W = self.wget('w_gate', l, ci)

            def evac(f, ps, keys, ci=ci):
                fo = ci * 4 + f
                if fo < 8:
                    c.op('act', lambda: nc.scalar.activation(out=g0v[:, fo, :], in_=ps, func=AF.Sigmoid,
                                                             bias=LC[:, ob + fo:ob + fo + 1]),
                         r=keys + [LC.k], w=[G0.k])
                else:
                    st = self.stage()
                    c.op('act', lambda: nc.scalar.activation(out=st[:, 0:512], in_=ps, func=AF.Sigmoid,
                                                             bias=LC[:, ob + fo:ob + fo + 1]),
                         r=keys + [LC.k], w=[st.k])
                    self.store(sc['g12'][fo - 8, :, t0:t0 + T], st, st[:, 0:512], [('g12', ti, fo - 8)])
            self.lin_fm(W, slot, 8, XN, self.xnk(), 512, evac)
        slot, W = self.wget('w_in', l, 0)

        def evac_u(f, ps, keys):
            c.op('act', lambda: nc.scalar.activation(out=uv[:, f, :], in_=ps, func=AF.Gelu_apprx_tanh),
                 r=keys, w=[U.k])
        self.lin_fm(W, slot, 8, XN, self.xnk(), 512, evac_u)

        def lin_tm(W, slot, ncol, evac):
            for cc in range(4):
                b, keys = self.ps(1)
                for kc in range(8):
                    c.op('pe', lambda kc=kc, cc=cc: nc.tensor.matmul(self.PS[:, b, 0:ncol],
                                                                     XN[:, kc, cc * 128:(cc + 1) * 128],
                                                                     W[:, kc, 0:ncol], start=(kc == 0), stop=(kc == 7)),
                         r=[slot.k] + self.xnk(), w=keys, inc=(kc == 7))
                evac(cc, self.PS[:, b, 0:ncol], keys)
        slot, W = self.wget('w_in', l, 1)
        olg, olb, osb = lc['lg'], lc['lb'], lc['sgu_bias']

        def evac_v(cc, ps, keys):
            vg, vgk = vgv[cc % 2], VG[cc % 2].k
            vn, vnk = vnv[cc % 2], VN[cc % 2].k
            s0 = (cc % 2) * 8
            c.op('act', lambda: nc.scalar.activation(out=vg, in_=ps, func=AF.Gelu_apprx_tanh,
                                                     accum_out=SM[:, s0:s0 + 1]), r=keys, w=[vgk, ('SMa', cc % 2)])
            c.op('act', lambda: nc.scalar.activation(out=jkv, in_=vg, func=AF.Square,
                                                     accum_out=SM[:, s0 + 1:s0 + 2]),
                 r=[vgk], w=[JK.k, ('SMb', cc % 2)])
            c.op('dve', lambda: nc.vector.tensor_scalar(out=SM[:, s0 + 2:s0 + 3], in0=SM[:, s0:s0 + 1],
                                                        scalar1=1.0 / 512, scalar2=None, op0=ALU.mult),
                 r=[('SMa', cc % 2)], w=[('SMc', cc % 2)])
            c.op('dve', lambda: nc.vector.tensor_tensor(out=SM[:, s0 + 3:s0 + 4], in0=SM[:, s0 + 2:s0 + 3],
                                                        in1=SM[:, s0 + 2:s0 + 3], op=ALU.mult),
                 r=[('SMc', cc % 2)], w=[('SMd', cc % 2)])
            c.op('dve', lambda: nc.vector.scalar_tensor_tensor(out=SM[:, s0 + 4:s0 + 5], in0=SM[:, s0 + 1:s0 + 2],
                                                               scalar=1.0 / 512, in1=SM[:, s0 + 3:s0 + 4],
                                                               op0=ALU.mult, op1=ALU.subtract),
                 r=[('SMb', cc % 2), ('SMd', cc % 2)], w=[('SMe', cc % 2)])
            c.op('act', lambda: nc.scalar.activation(out=SM[:, s0 + 5:s0 + 6], in_=SM[:, s0 + 4:s0 + 5], func=AF.Sqrt,
                                                     bias=self.eps_ap(0), scale=1.0),
                 r=[('SMe', cc % 2)], w=[('SMf', cc % 2)])
            c.op('dve', lambda: nc.vector.reciprocal(out=SM[:, s0 + 5:s0 + 6], in_=SM[:, s0 + 5:s0 + 6]),
                 r=[('SMf', cc % 2)], w=[('SMf', cc % 2)])
            c.op('dve', lambda: nc.vector.tensor_scalar(out=vg, in0=vg, scalar1=SM[:, s0 + 2:s0 + 3],
                                                        scalar2=SM[:, s0 + 5:s0 + 6], op0=ALU.subtract, op1=ALU.mult),
                 r=[vgk, ('SMc', cc % 2), ('SMf', cc % 2)], w=[vgk])
            c.op('dve', lambda: nc.vector.tensor_tensor(out=vg, in0=vg, in1=LC[:, olg:olg + 512], op=ALU.mult),
                 r=[vgk, LC.k], w=[vgk])
            c.op('dve', lambda: nc.vector.tensor_tensor(out=vn, in0=vg, in1=LC[:, olb:olb + 512], op=ALU.add),
                 r=[vgk, LC.k], w=[vnk])
            def tail(cc=cc, vn=vn, vnk=vnk):
              b2, k2 = self.ps(1)
              for g in range(4):
                c.op('pe', lambda g=g: nc.tensor.matmul(self.PS[:, b2, g * 128:(g + 1) * 128],
                                                        vn[:, g * 128:(g + 1) * 128],
                                                        self.LCB[:, g * 128:(g + 1) * 128], start=True, stop=True),
                     r=[vnk, self.LCB.k], w=k2, inc=(g == 3))
              c.op('dve', lambda: nc.vector.tensor_tensor(out=jkv, in0=self.PS[:, b2, :], in1=LC[:, osb:osb + 512],
                                                          op=ALU.add), r=k2 + [LC.k], w=[JK.k])
              c.op('dve', lambda: nc.vector.tensor_tensor(out=aov[:, :, cc * 128:(cc + 1) * 128],
                                                          in0=jkv.rearrange('p (g t) -> p g t', g=4),
                                                          in1=uv[:, :, cc * 128:(cc + 1) * 128], op=ALU.mult),
                   r=[JK.k, U.k], w=[AOT.k])
            if pending:
                pending.pop(0)()
            pending.append(tail)
        pending = []
        lin_tm(W, slot, 512, evac_v)
        slot, W = self.wget('w_in', l, 2)

        def evac_pin(cc, ps, keys):
            st = self.stage()
            c.op('act', lambda: nc.scalar.activation(out=st[:, 0:512], in_=ps, func=AF.Copy), r=keys, w=[st.k])
            self.store(sc['pin'][t0 + cc * 128:t0 + (cc + 1) * 128, :], st, st[:, 0:512], [('pin', ti)])
        lin_tm(W, slot, 512, evac_pin)
        while pending:
            pending.pop(0)()
        for zi in range(2):
            slot, W = self.wget('w_in', l, 3 + zi)

            def evac_z(cc, ps, keys, zi=zi):
                st = self.stage()
                c.op('act', lambda: nc.scalar.activation(out=st[:, 0:512], in_=ps, func=AF.Silu), r=keys, w=[st.k])
                self.store(sc['zs'][t0 + cc * 128:t0 + (cc + 1) * 128, zi * 512:(zi + 1) * 512], st, st[:, 0:512],
                           [('zs', ti)])
            lin_tm(W, slot, 512, evac_z)
        for xi in range(2):
            slot, W = self.wget('w_in', l, 5 + xi)

            def evac_x(f, ps, keys, xi=xi):
                st = self.stage()
                c.op('dve', lambda: nc.vector.tensor_copy(out=st[:, 0:512], in_=ps), r=keys, w=[st.k])
                self.store(sc['xbc'][xi * 4 + f, :, t0:t0 + T], st, st[:, 0:512], [('xbc', ti)])
            self.lin_fm(W, slot, 8, XN, self.xnk(), 512, evac_x)
        slot, W = self.wget('w_in', l, 7)

        def evac_x2(f, ps, keys):
            st = self.stage()
            c.op('dve', lambda: nc.vector.tensor_copy(out=st[:, 0:512], in_=ps), r=keys, w=[st.k])
            self.store(sc['xbc'][8 + f, :, t0:t0 + T], st, st[:, 0:512], [('xbc', ti)])
        self.lin_fm(W, slot, 8, XN, self.xnk(), 256, evac_x2)
        for cc in range(4):
            b, keys = self.ps(1)
            for kc in range(8):
                c.op('pe', lambda kc=kc, cc=cc: nc.tensor.matmul(self.PS[:, b, 0:32], XN[:, kc, cc * 128:(cc + 1) * 128],
                                                                 W[:, kc, 256:288], start=(kc == 0), stop=(kc == 7)),
                     r=[slot.k] + self.xnk(), w=keys, inc=(kc == 7))
            c.op('act', lambda cc=cc: nc.scalar.activation(out=SM[:, 64 + cc * 32:64 + (cc + 1) * 32],
                                                           in_=self.PS[:, b, 0:32], func=AF.Copy),
                 r=keys, w=[('SMdt', cc)])
            c.dma('pool', sc['dtr'][t0 + cc * 128:t0 + (cc + 1) * 128, :], SM[:, 64 + cc * 32:64 + (cc + 1) * 32],
                  'd_st_dt%d' % cc, r=[('SMdt', cc)], w=[('dtr', ti)])
        slot, W = self.wget('w_branch_a', l, 0)

        def evac_a(f, ps, keys):
            mv, mk = mpv[f % 2], MPS[f % 2].k
            c.op('dve', lambda: nc.vector.tensor_tensor(out=mv, in0=ps, in1=g0v[:, f, :], op=ALU.mult),
                 r=keys + [G0.k], w=[mk])
            c.dma('pool', sc['mp'][f, :, t0:t0 + T], mv, 'd_st_' + mk, r=[mk], w=[('mp', ti)])
        self.lin_fm(W, slot, 4, aov, AOT.k, 1024, evac_a)

    def sweep_B(self, l, ti):
        c, nc, sc = self.c, self.nc, self.sc
        si, t0, first, last = self.tiles[ti]
        LC, lc, SM, BIG = self.LC, self.lc, self.SM, self.BIG
        XR = Buf(BIG.t, 'B_XR')
        xrv = BIG[:, 0:2580].bitcast(BF16).rearrange('p (f t) -> p f t', f=10)
        XC = Buf(BIG.t, 'B_XC')
        xcv = BIG[:, 2580:5140].bitcast(BF16).rearrange('p (f t) -> p f t', f=10)
        ACC = Buf(BIG.t, 'B_ACC')
        accv = BIG[:, 5140:5652]
        PINB = Buf(BIG.t, 'B_PIN')
        pinv = BIG[:, 5652:7188].bitcast(BF16).rearrange('p (c n) -> p c n', c=6)
        G1 = Buf(BIG.t, 'B_G1')
        g1v = BIG[:, 7188:9236].bitcast(BF16).rearrange('p (k t) -> p k t', k=8)
        POOLED = Buf(BIG.t, 'B_PO')
        pov = BIG[:, 9236:10260].bitcast(BF16).rearrange('p (g t) -> p g t', g=4)
        BO = Buf(BIG.t, 'B_BO')
        bov = BIG[:, 10260:11284].bitcast(BF16).rearrange('p (g t) -> p g t', g=4)
        DTR = Buf(BIG.t, 'B_DTR')
        dtrv = BIG[:, 11284:11412].rearrange('p (c n) -> p c n', c=4)
        HB = self.H
        hflat = HB[:, :, :].rearrange('p a t -> p (a t)')
        XS = Buf(HB.t, 'B_XS')
        xsv = hflat[:, 0:5120].rearrange('p (a n) -> p a n', a=5)
        BT = Buf(HB.t, 'B_BT')
        btv = hflat[:, 5120:5248]
        MM_ = Buf(HB.t, 'B_M')
        mv_ = hflat[:, 5248:9344].rearrange('p (a h q) -> p a h q', a=2, h=16)
        MCB = Buf(HB.t, 'B_MCB')
        mcbv = hflat[:, 9344:9856].rearrange('p (a g q) -> p a g q', a=2, g=2)
        XF = self.X
        xflat = XF[:, :, :].rearrange('p a t -> p (a t)')
        RF = Buf(XF.t, 'B_RF')
        rfq = [xflat[:, i * 768:(i + 1) * 768].bitcast(BF16).rearrange('p (j h q) -> p j h q', j=3, h=4) for i in range(2)]
        EB = Buf(XF.t, 'B_E')
        ebv = xflat[:, 2048:4096].rearrange('p (h q) -> p h q', h=16)
        YL = Buf(self.XN.t, 'B_YL')
        ylv = self.XN[:, :, :].rearrange('p a t -> p (a t)').bitcast(F32)[:, 0:1024]
        SST = [Buf(self.SQ.t, 'B_SST%d' % i) for i in range(2)]
        sstv = [self.SQ[:, :, :].rearrange('p a t -> p (a t)').bitcast(F32)[:, i * 512:(i + 1) * 512] for i in range(2)]
        MPB = Buf(self.RS.t, 'B_MPB')
        mpbv = self.RS[:]
        TMP = self.SA[0]
        TMP2 = self.SA[1]

        seq0 = self.seq_start[si]
        seqL = self.seqs[si]
        if first:
            c.op('pool', lambda: nc.gpsimd.memset(xrv[:, :, 0:2], 0.0), w=[XR.k])
        if last:
            c.op('pool', lambda: nc.gpsimd.memset(xrv[:, :, 514:516], 0.0), w=[XR.k])
        lo = t0 - (0 if first else 2)
        hi = t0 + T + (0 if last else 2)
        tis = [j for j in (ti - 1, ti, ti + 1) if 0 <= j < len(self.tiles)]
        c.dma('sp', xrv[:, :, 2 - (t0 - lo): 514 + (hi - t0 - T)],
              sc['xbc'][:, :, lo:hi].rearrange('f p t -> p f t'), 'd_xr', r=[('xbc', j) for j in tis], w=[XR.k])
        c0 = t0 // CH
        plo = c0 - (0 if first else 1)
        phi = c0 + 4 + (0 if last else 1)
        c.dma('sp', pinv[:, (plo - c0 + 1):(phi - c0 + 1), :],
              sc['pin'][plo * CH:phi * CH, :].rearrange('(c p) n -> p c n', p=128), 'd_pin',
              r=[('pin', j) for j in tis], w=[PINB.k])
        c.dma('sp', dtrv, sc['dtr'][t0:t0 + T, :].rearrange('(c p) n -> p c n', p=128), 'd_dtr',
              r=[('dtr', ti)], w=[DTR.k])
        c.dma('sp', g1v, sc['g12'][0:8, :, t0:t0 + T].rearrange('k p t -> p k t'), 'd_g1',
              r=[('g12', ti, k) for k in range(8)], w=[G1.k])
        if BSTOP == 1:
            return
        ocw, ocb = lc['conv_w'], lc['conv_b']
        self.link([EB.k] + [(EB.k, h) for h in range(16)] + [('B_CT', j) for j in range(4)])
        WX = self.WX
        cts = [xflat[:, 2048:2564], xflat[:, 2564:3080], WX[:, 0:516], WX[:, 516:1032]]
        ktf = self.KT[:, :, :].rearrange('p a t -> p (a t)').bitcast(F32)
        vvf = self.VV[:, :, :].rearrange('p a t -> p (a t)').bitcast(F32)
        accs = [ktf[:, 0:512], ktf[:, 512:1024], vvf[:, 0:512], vvf[:, 512:1024]]
        for g0 in range(0, 10, 4):
            fs = list(range(g0, min(g0 + 4, 10)))
            for j, f in enumerate(fs):
                c.op('act', lambda f=f, j=j: nc.scalar.activation(out=cts[j], in_=xrv[:, f, :], func=AF.Copy),
                     r=[XR.k], w=[('B_CT', j)])
            for k in range(4):
                for j, f in enumerate(fs):
                    if k == 0:
                        c.op('dve', lambda f=f, j=j: nc.vector.tensor_scalar(
                            out=accs[j], in0=cts[j][:, 0:512], scalar1=LC[:, ocw + f:ocw + f + 1], scalar2=None,
                            op0=ALU.mult), r=[('B_CT', j), LC.k], w=[('B_AC', j)])
                    else:
                        c.op('dve', lambda f=f, j=j, k=k: nc.vector.scalar_tensor_tensor(
                            out=accs[j], in0=cts[j][:, k:k + 512],
                            scalar=LC[:, ocw + k * 10 + f:ocw + k * 10 + f + 1], in1=accs[j], op0=ALU.mult,
                            op1=ALU.add), r=[('B_CT', j), LC.k, ('B_AC', j)], w=[('B_AC', j)])
            for j, f in enumerate(fs):
                c.op('act', lambda f=f, j=j: nc.scalar.activation(out=xcv[:, f, :], in_=accs[j], func=AF.Silu,
                                                                  bias=LC[:, ocb + f:ocb + f + 1]),
                     r=[('B_AC', j), LC.k], w=[(XC.k, f)])
        c.op('dve', lambda: nc.vector.memset(SM[:, 1001:1002], 0.0), r=[(XC.k, f) for f in range(10)],
             w=[XC.k, ('SMz2',)])
        if BSTOP == 2:
            return
        CMZ = Buf(BIG.t, 'B_CMZ')
        cmz = BIG[:, 11412:11924].bitcast(BF16).rearrange('p (g t) -> p g t', g=2)
        c.op('pool', lambda: nc.gpsimd.memset(cmz, 0.0), w=[CMZ.k])
        for g in range(2):
            c.op('act', lambda g=g: nc.scalar.activation(out=cmz[g * 64:(g + 1) * 64, g, :],
                                                         in_=xcv[g * 64:(g + 1) * 64, 9, :], func=AF.Copy),
                 r=[XC.k, CMZ.k], w=[CMZ.k])
        c.dma('pool', sc['cm'][:, :, t0:t0 + T].rearrange('g p t -> p g t'), cmz, 'd_st_cm', r=[CMZ.k],
              w=[('cm', ti)])
        odb, oa, ods = lc['dt_bias'], lc['a'], lc['d_skip']
        if BSTOP == 21:
            return
        for cc in range(4):
            chk = c0 + cc
            tsl = slice(cc * 128, (cc + 1) * 128)
            s_dt, s_da, s_t1, s_t2 = 192, 224, 256, 288
            c.op('dve', lambda: nc.vector.tensor_tensor(out=SM[:, s_t1:s_t1 + 32], in0=dtrv[:, cc, :],
                                                        in1=LC[:, odb:odb + 32], op=ALU.add),
                 r=[DTR.k, LC.k], w=[('SM', 't1')])
            c.op('dve', lambda: nc.vector.tensor_scalar(out=SM[:, s_t2:s_t2 + 32], in0=SM[:, s_t1:s_t1 + 32],
                                                        scalar1=-1.0, scalar2=None, op0=ALU.mult),
                 r=[('SM', 't1')], w=[('SM', 't2')])
            c.op('dve', lambda: nc.vector.tensor_tensor(out=SM[:, s_t2:s_t2 + 32], in0=SM[:, s_t2:s_t2 + 32],
                                                        in1=SM[:, s_t1:s_t1 + 32], op=ALU.min),
                 r=[('SM', 't1'), ('SM', 't2')], w=[('SM', 't2')])
            c.op('act', lambda: nc.scalar.activation(out=SM[:, s_t2:s_t2 + 32], in_=SM[:, s_t2:s_t2 + 32],
                                                     func=AF.Exp), r=[('SM', 't2')], w=[('SM', 't2')])
            c.op('act', lambda: nc.scalar.activation(out=SM[:, s_t2:s_t2 + 32], in_=SM[:, s_t2:s_t2 + 32],
                                                     func=AF.Ln, bias=self.eps_ap(2)), r=[('SM', 't2')], w=[('SM', 't2')])
            c.op('dve', lambda: nc.vector.scalar_tensor_tensor(out=SM[:, s_dt:s_dt + 32], in0=SM[:, s_t1:s_t1 + 32],
                                                               scalar=0.0, in1=SM[:, s_t2:s_t2 + 32],
                                                               op0=ALU.max, op1=ALU.add),
                 r=[('SM', 't1'), ('SM', 't2')], w=[('SM', 'dt')])
            c.op('dve', lambda: nc.vector.tensor_tensor(out=SM[:, s_da:s_da + 32], in0=SM[:, s_dt:s_dt + 32],
                                                        in1=LC[:, oa:oa + 32], op=ALU.mult),
                 r=[('SM', 'dt'), LC.k], w=[('SM', 'da')])
            if BSTOP == 22:
                continue
            da3 = SM[:, 480:528].bitcast(BF16).rearrange('p (j n) -> p j n', j=3)
            da_f = SM[:, s_da:s_da + 32]
            r1 = SM[:, 528:560]
            r2 = SM[:, 560:592]
            c.op('dve', lambda: nc.vector.tensor_copy(out=da3[:, 0, :], in_=da_f), r=[('SM', 'da')], w=[('SM', 'd3', 0)])
            c.op('dve', lambda: nc.vector.tensor_tensor(out=r1, in0=da_f, in1=da3[:, 0, :], op=ALU.subtract),
                 r=[('SM', 'da'), ('SM', 'd3', 0)], w=[('SM', 'r1')])
            c.op('dve', lambda: nc.vector.tensor_copy(out=da3[:, 1, :], in_=r1), r=[('SM', 'r1')], w=[('SM', 'd3', 1)])
            c.op('dve', lambda: nc.vector.tensor_tensor(out=r2, in0=r1, in1=da3[:, 1, :], op=ALU.subtract),
                 r=[('SM', 'r1'), ('SM', 'd3', 1)], w=[('SM', 'r2')])
            c.op('dve', lambda: nc.vector.tensor_copy(out=da3[:, 2, :], in_=r2), r=[('SM', 'r2')], w=[('SM', 'd3', 2)])
            d3k = [('SM', 'd3', j) for j in range(3)]
            if BSTOP == 23:
                continue
            b4, k4 = self.ps(1)
            for i, (nm, dcol) in enumerate((('tri_f', 0), ('su', 0), ('tri_b', 16), ('sl', 16))):
                for j in range(3):
                    c.op('pe', lambda i=i, nm=nm, dcol=dcol, j=j: nc.tensor.matmul(
                        self.PS[:, b4, i * 16:(i + 1) * 16], self.cb(nm), da3[:, j, dcol:dcol + 16],
                        start=(j == 0), stop=(j == 2)), r=d3k + [self.CB.k], w=k4, inc=(i == 3 and j == 2))
            if BSTOP == 24:
                continue
            s_e4, s_ac = 320, 384
            BSK = int(os.environ.get('BSK', '0'))
            if BSK != 1:
                c.op('act', lambda: nc.scalar.activation(out=SM[:, s_e4:s_e4 + 64], in_=self.PS[:, b4, 0:64], func=AF.Exp),
                     r=k4, w=[('SM', 'e4')])
            if BSK != 2:
                c.op('act', lambda: nc.scalar.activation(out=SM[:, s_ac:s_ac + 64], in_=self.PS[:, b4, 0:64],
                                                         func=AF.Copy), r=k4, w=[('SM', 'ac')])
            if BSK != 3:
                c.dma('pool', sc['e4'][t0 + cc * 128:t0 + (cc + 1) * 128, :], SM[:, s_e4:s_e4 + 64], 'd_st_e4',
                      r=[('SM', 'e4')], w=[('e4', ti)])
            if BSTOP == 25:
                continue
            s_dtd = 448
            for di in range(2):
                c.op('dve', lambda di=di: nc.vector.tensor_tensor(
                    out=SM[:, s_dtd + di * 16:s_dtd + (di + 1) * 16], in0=SM[:, s_dt + di * 16:s_dt + (di + 1) * 16],
                    in1=SM[:, s_e4 + 16 + di * 32:s_e4 + 32 + di * 32], op=ALU.mult),
                    r=[('SM', 'dt'), ('SM', 'e4')], w=[('SM', 'dtd')])
            if BSTOP == 3:
                continue
            bx, kx = self.ps(1)
            pxb = self.PS[:, bx, :].bitcast(BF16)
            for f in range(8):
                c.op('pe', lambda f=f: nc.tensor.transpose(out=pxb[:, f * 128:(f + 1) * 128], in_=xcv[:, f, tsl],
                                                           identity=self.cb('ident')),
                     r=[XC.k, self.CB.k], w=kx, inc=(f == 7))
            bb, kb = self.ps(1)
            pbb = self.PS[:, bb, :].bitcast(BF16)
            c.op('pe', lambda: nc.tensor.transpose(out=pbb[:, 0:128], in_=xcv[:, 8, tsl], identity=self.cb('ident')),
                 r=[XC.k, self.CB.k], w=kb)
            c.op('act', lambda: nc.scalar.activation(out=btv, in_=pbb[:, 0:128], func=AF.Copy), r=kb, w=[BT.k])
            XCT = Buf(HB.t, 'B_XCT')
            xctv = hflat[:, 9856:10880]
            c.op('act', lambda: nc.scalar.activation(out=xctv, in_=pxb, func=AF.Copy), r=kx, w=[XCT.k])
            kx = [XCT.k]
            px3 = xctv.rearrange('p (h d) -> p h d', h=16)

            def bc16(col):
                a = SM[:, col:col + 16]
                return bass.AP(a.tensor, a.offset, [list(a.ap[0]), [1, 16], [0, 64]])
            srcs = [(s_dt, 'dt'), (s_dt + 16, 'dt'), (s_dtd, 'dtd'), (s_dtd + 16, 'dtd')]
            for i, (col, kk) in enumerate(srcs):
                c.op('dve', lambda i=i, col=col: nc.vector.tensor_tensor(
                    out=xsv[:, i, :].rearrange('p (h d) -> p h d', h=16), in0=px3, in1=bc16(col), op=ALU.mult),
                    r=kx + [('SM', kk)], w=[(XS.k, i)])
            dsk = LC[:, ods:ods + 16]
            c.op('dve', lambda: nc.vector.tensor_tensor(
                out=xsv[:, 4, :].rearrange('p (h d) -> p h d', h=16), in0=px3,
                in1=bass.AP(dsk.tensor, dsk.offset, [list(dsk.ap[0]), [1, 16], [0, 64]]), op=ALU.mult),
                r=kx + [LC.k], w=[(XS.k, 4)])
            if BSTOP == 4:
                continue
            bc_, kc_ = self.ps(1)
            for g in range(2):
                c.op('pe', lambda g=g: nc.tensor.matmul(self.PS[:, bc_, g * 128:(g + 1) * 128],
                                                        xcv[:, 8, tsl], cmz[:, g, tsl],
                                                        start=True, stop=True), r=[XC.k, CMZ.k], w=kc_, inc=(g == 1))
            cbs = SM[:, 600:856]
            c.op('act', lambda: nc.scalar.activation(out=cbs, in_=self.PS[:, bc_, 0:256], func=AF.Copy),
                 r=kc_, w=[('SM', 'cb')])
            for di, nm in enumerate(('tri_f', 'tri_b')):
                tri = self.cf(nm)
                c.op('dve', lambda di=di, tri=tri: nc.vector.tensor_tensor(
                    out=mcbv[:, di, :, :], in0=cbs.rearrange('p (g q) -> p g q', g=2),
                    in1=bass.AP(tri.tensor, tri.offset, [list(tri.ap[0]), [0, 2], [1, 128]]), op=ALU.mult),
                    r=[('SM', 'cb'), self.CF.k], w=[(MCB.k, di)])
            if BSTOP == 5:
                continue
            for di, nm in enumerate(('tri_f', 'tri_b')):
                tri = self.cf(nm)
                trib = self.cb(nm)
                bd, kd = self.ps(4)
                for qq in range(4):
                    dsrc = da3[:, :, di * 16 + qq * 4: di * 16 + qq * 4 + 4]
                    rq = rfq[qq % 2]
                    rk = (RF.k, qq % 2)
                    c.op('pool', lambda trib=trib, dsrc=dsrc, rq=rq: nc.gpsimd.tensor_tensor(
                        out=rq, in0=bass.AP(dsrc.tensor, dsrc.offset, [list(dsrc.ap[0]), [32, 3], [1, 4], [0, 128]]),
                        in1=bass.AP(trib.tensor, trib.offset, [list(trib.ap[0]), [0, 3], [0, 4], [1, 128]]),
                        op=ALU.mult), r=d3k + [self.CB.k], w=[rk])
                    for j in range(3):
                        c.op('pe', lambda j=j, qq=qq, rq=rq: nc.tensor.matmul(
                            self.PS[:, bd + qq, :], self.cb('ones'), rq[:, j, :, :].rearrange('p h q -> p (h q)'),
                            start=(j == 0), stop=(j == 2)), r=[rk, self.CB.k], w=[kd[qq]], inc=(j == 2))
                acol = s_ac + (0 if di == 0 else 32)
                for h in range(16):
                    c.op('act', lambda h=h: nc.scalar.activation(
                        out=ebv[:, h, :], in_=self.PS[:, bd + h // 4, (h % 4) * 128:(h % 4 + 1) * 128], func=AF.Relu,
                        scale=-1.0, bias=SM[:, acol + h:acol + h + 1]),
                        r=[kd[h // 4], ('SM', 'ac')], w=[(EB.k, h)])
                qe = 127 if di == 0 else 0
                dst = self.stage()
                dv = dst[:, 0:16].bitcast(F32)
                for g in range(2):
                    src = self.PS[g * 64:(g + 1) * 64, bd + g * 2:bd + g * 2 + 2, :].rearrange(
                        'p b (h q) -> p (b h) q', h=4)[:, :, qe:qe + 1]
                    c.op('act', lambda g=g, src=src: nc.scalar.activation(
                        out=dv[g * 64:(g + 1) * 64, :].rearrange('p (h o) -> p h o', o=1), in_=src, func=AF.Exp),
                        r=[kd[g * 2], kd[g * 2 + 1]], w=[dst.k])
                self.store(sc['dec'][chk, di], dst, dv, [('dec', chk, di)])
                c.op('act', lambda: nc.scalar.activation(out=ebv, in_=ebv, func=AF.Exp, scale=-1.0),
                     r=[EB.k], w=[EB.k] + [(EB.k, h) for h in range(16)])
                for g in range(2):
                    m = mcbv[:, di, g, :]
                    c.op('dve', lambda g=g, m=m, di=di: nc.vector.tensor_tensor(
                        out=mv_[:, di, g * 8:(g + 1) * 8, :], in0=ebv[:, g * 8:(g + 1) * 8, :],
                        in1=bass.AP(m.tensor, m.offset, [list(m.ap[0]), [0, 8], [1, 128]]), op=ALU.mult),
                        r=[EB.k, (MCB.k, di)] + [(EB.k, h) for h in range(g * 8, g * 8 + 8)], w=[(MM_.k, di, g)])
            if BSTOP == 6:
                continue
            by, ky = self.ps(2)
            for half in range(2):
                c.op('pe', lambda half=half: nc.tensor.matmul(self.PS[:, by + half, :], self.cb('ident'),
                                                              xsv[:, 4, half * 512:(half + 1) * 512],
                                                              start=True, stop=False),
                     r=[(XS.k, 4), self.CB.k], w=[ky[half]], inc=False)
            for h in range(16):
                for di in range(2):
                    lastmm = (h % 8 == 7 and di == 1)
                    c.op('pe', lambda h=h, di=di: nc.tensor.matmul(
                        self.PS[:, by + h // 8, (h % 8) * 64:(h % 8 + 1) * 64], mv_[:, di, h, :],
                        xsv[:, di, h * 64:(h + 1) * 64], start=False, stop=(di == 1)),
                        r=[(MM_.k, di, h // 8), (XS.k, di)], w=[ky[h // 8]], inc=lastmm)
            c.op('act', lambda: nc.scalar.activation(out=ylv, in_=self.PS[:, by:by + 2, :].rearrange('p b n -> p (b n)'),
                                                     func=AF.Copy), r=ky, w=[YL.k])
            c.dma('pool', sc['yl'][t0 + cc * 128:t0 + (cc + 1) * 128, :], ylv, 'd_st_yl', r=[YL.k], w=[('yl', ti)])
            if BSTOP == 7:
                continue
            for di in range(2):
                bs, ks = self.ps(2)
                for g in range(2):
                    c.op('pe', lambda g=g, di=di: nc.tensor.matmul(
                        self.PS[:, bs + g, :], btv, xsv[:, 2 + di, g * 512:(g + 1) * 512], start=True, stop=True),
                        r=[BT.k, (XS.k, 2 + di)], w=[ks[g]])
                sv, sk = sstv[di], SST[di].k
                for g in range(2):
                    c.op('act', lambda sv=sv, g=g: nc.scalar.activation(out=sv[g * 64:(g + 1) * 64, :],
                                                                        in_=self.PS[g * 64:(g + 1) * 64, bs + g, :],
                                                                        func=AF.Copy), r=[ks[g]], w=[sk])
                c.dma('pool', sc['S'][chk, di], sv, 'd_st_' + sk, r=[sk], w=[('S', chk, di)])
            if BSTOP == 8:
                continue
            bp, kp = self.ps(1)
            cfirst = first and cc == 0
            clast = last and cc == 3
            var = 'f' if cfirst else ('l' if clast else 'm')
            for g in range(4):
                nbs = [nb for nb in (-1, 0, 1) if not ((nb == -1 and cfirst) or (nb == 1 and clast))]
                for ii, nb in enumerate(nbs):
                    bn = 'band_%d_%s0' % (g, var) if nb == 0 else 'band_%d_m%d' % (g, nb)
                    c.op('pe', lambda g=g, nb=nb, bn=bn, ii=ii: nc.tensor.matmul(
                        self.PS[:, bp, g * 128:(g + 1) * 128], pinv[:, cc + 1 + nb, g * 128:(g + 1) * 128],
                        self.cb(bn), start=(ii == 0), stop=(ii == len(nbs) - 1)),
                        r=[PINB.k, self.CB.k], w=kp, inc=(g == 3 and ii == len(nbs) - 1))
            irc = CONST_NAMES.index('rc_0_%s' % var)
            assert all(CONST_NAMES.index('rc_%d_%s' % (g, var)) == irc + 3 * g for g in range(4))
            rca = self.CF[:, irc * 128:(irc + 1) * 128]
            c.op('act', lambda: nc.scalar.activation(out=accv, in_=self.PS[:, bp, :], func=AF.Copy), r=kp, w=[ACC.k])
            c.op('dve', lambda: nc.vector.tensor_tensor(
                out=pov[:, :, tsl], in0=accv.rearrange('p (g t) -> p g t', g=4),
                in1=bass.AP(rca.tensor, rca.offset, [list(rca.ap[0]), [3 * 128, 4], [1, 128]]), op=ALU.mult),
                r=[ACC.k, self.CF.k], w=[POOLED.k])
        if BSTOP == 9:
            return
        ops_ = lc['pool_scale']
        for g in range(4):
            b, keys = self.ps(1)
            c.op('pe', lambda g=g: nc.tensor.matmul(self.PS[:, b, :], self.LCB[:, 512 + g * 128:512 + (g + 1) * 128],
                                                    pov[:, g, :], start=True, stop=True),
                 r=[POOLED.k, self.LCB.k], w=keys)
            c.op('act', lambda g=g: nc.scalar.activation(out=bov[:, g, :], in_=self.PS[:, b, :], func=AF.Copy,
                                                         scale=LC[:, ops_ + g:ops_ + g + 1]),
                 r=keys + [LC.k], w=[BO.k])
        slot, W = self.wget('w_branch_b', l, 0)

        sqf = self.SQ[:, :, :].rearrange('p a t -> p (a t)').bitcast(F32)
        xnf = self.XN[:, :, :].rearrange('p a t -> p (a t)').bitcast(F32)
        mpbs = [mpbv, sqf[:, 1024:1536], sqf[:, 1536:2048], xnf[:, 1024:1536]]

        def evac_b(f, ps, keys):
            mpb = mpbs[f % 4]
            mk = ('B_MPB', f % 4)
            c.dma('sp', mpb, sc['mp'][f, :, t0:t0 + T], 'd_mpb%d' % (f % 4), r=[('mp', ti, f)], w=[mk])
            tmv = [TMP[:], TMP2[:], xnf[:, 1536:2048], self.WX[:, 1032:1544]][f % 4]
            tk = ('B_TM', f % 4)
            c.op('dve', lambda: nc.vector.tensor_tensor(out=tmv, in0=ps, in1=g1v[:, f, :], op=ALU.mult),
                 r=keys + [G1.k], w=[tk])
            c.op('pool', lambda: nc.gpsimd.tensor_tensor(out=tmv, in0=tmv, in1=mpb, op=ALU.add),
                 r=[tk, mk], w=[tk])
            c.dma('pool', sc['mp'][f, :, t0:t0 + T], tmv, 'd_st_btm%d' % (f % 4), r=[tk], w=[('mp', ti, f)])
        self.lin_fm(W, slot, 4, bov, BO.k, 1024, evac_b)

    def recurrence(self, l):
        c, nc, sc = self.c, self.nc, self.sc
        BIG = self.BIG
        Hs = Buf(BIG.t, 'R_H')
        hv = BIG[:, 0:512]
        Tm = Buf(BIG.t, 'R_T')
        tv = BIG[:, 512:1024]
        SL = [Buf(BIG.t, 'R_S%d' % i) for i in range(4)]
        slv = [BIG[:, 1024 + i * 512:1024 + (i + 1) * 512] for i in range(4)]
        DL = [Buf(BIG.t, 'R_D%d' % i) for i in range(4)]
        dlv = [BIG[:, 3072 + i * 8:3072 + (i + 1) * 8] for i in range(4)]
        HO = [Buf(BIG.t, 'R_HO%d' % i) for i in range(2)]
        hov = [BIG[:, 3200 + i * 256:3200 + (i + 1) * 256].bitcast(BF16) for i in range(2)]
        n = 0
        for si, Ls in enumerate(self.seqs):
            c0 = self.seq_start[si] // CH
            nch = Ls // CH
            for di in range(2):
                order = list(range(c0, c0 + nch)) if di == 0 else list(range(c0 + nch - 1, c0 - 1, -1))
                c.op('dve', lambda: nc.vector.memset(hv, 0.0), w=[Hs.k])
                for chk in order:
                    i4 = n % 4
                    i2 = n % 2
                    n += 1
                    c.dma('sp', slv[i4], sc['S'][chk, di], 'd_rs%d' % i4, r=[('S', chk, di)], w=[SL[i4].k])
                    c.dma('sp', dlv[i4], sc['dec'][chk, di], 'd_rd%d' % i4, r=[('dec', chk, di)], w=[DL[i4].k])
                    c.op('act', lambda i2=i2: nc.scalar.activation(out=hov[i2], in_=hv, func=AF.Copy),
                         r=[Hs.k], w=[HO[i2].k])
                    c.dma('pool', sc['hp'][chk, di], hov[i2], 'd_st_ho%d' % i2, r=[HO[i2].k], w=[('hp', chk, di)])
                    d_ = dlv[i4]
                    c.op('dve', lambda d_=d_: nc.vector.tensor_tensor(
                        out=tv.rearrange('p (e d) -> p e d', e=8), in0=hv.rearrange('p (e d) -> p e d', e=8),
                        in1=bass.AP(d_.tensor, d_.offset, [list(d_.ap[0]), [1, 8], [0, 64]]), op=ALU.mult),
                        r=[Hs.k, DL[i4].k], w=[Tm.k])
                    c.op('dve', lambda i4=i4: nc.vector.tensor_tensor(out=hv, in0=tv, in1=slv[i4], op=ALU.add),
                         r=[Tm.k, SL[i4].k], w=[Hs.k])

    def mem_kv(self, l, si):
        c, nc = self.c, self.nc
        KT, VV, LC, lc, SM = self.KT, self.VV, self.LC, self.lc, self.SM
        BIG = self.BIG
        MT = Buf(BIG.t, 'K_MT')
        mtv = BIG[:, 0:2048].rearrange('p (c n) -> p c n', c=2)
        MNB = Buf(BIG.t, 'K_MN')
        mnv = BIG[:, 2048:3072].bitcast(BF16).rearrange('p (c n) -> p c n', c=2)
        MNT = Buf(BIG.t, 'K_MNT')
        mntv = BIG[:, 3072:4096].bitcast(BF16).rearrange('p (k m) -> p k m', k=8)
        JK = Buf(BIG.t, 'K_JK')
        jkv = BIG[:, 4096:5120]
        c.dma('sp', mtv, self.dr['mem'][si].rearrange('(c p) n -> p c n', p=128), 'd_mem', w=[MT.k])
        omn = lc['mem_norm']
        for mc in range(2):
            c.op('act', lambda mc=mc: nc.scalar.activation(out=jkv, in_=mtv[:, mc, :], func=AF.Square,
                                                           accum_out=SM[:, 16 + mc:17 + mc]),
                 r=[MT.k], w=[JK.k, ('SMk', mc)])
            c.op('act', lambda mc=mc: nc.scalar.activation(out=SM[:, 18 + mc:19 + mc], in_=SM[:, 16 + mc:17 + mc],
                                                           func=AF.Sqrt, bias=self.eps_ap(0), scale=1.0 / 1024),
                 r=[('SMk', mc)], w=[('SMk2', mc)])
            c.op('dve', lambda mc=mc: nc.vector.reciprocal(out=SM[:, 20 + mc:21 + mc], in_=SM[:, 18 + mc:19 + mc]),
                 r=[('SMk2', mc)], w=[('SMk3', mc)])
            c.op('dve', lambda mc=mc: nc.vector.scalar_tensor_tensor(
                out=mnv[:, mc, :], in0=mtv[:, mc, :], scalar=SM[:, 20 + mc:21 + mc], in1=LC[:, omn:omn + 1024],
                op0=ALU.mult, op1=ALU.mult), r=[MT.k, ('SMk3', mc), LC.k], w=[MNB.k])
            b, keys = self.ps(1)
            pb = self.PS[:, b, :].bitcast(BF16)
            for kc in range(8):
                c.op('pe', lambda kc=kc, mc=mc: nc.tensor.transpose(out=pb[:, kc * 128:(kc + 1) * 128],
                                                                    in_=mnv[:, mc, kc * 128:(kc + 1) * 128],
                                                                    identity=self.cb('ident')),
                     r=[MNB.k, self.CB.k], w=keys, inc=(kc == 7))
            c.op('act', lambda mc=mc: nc.scalar.activation(out=mntv[:, :, mc * 128:(mc + 1) * 128],
                                                           in_=pb.rearrange('p (k m) -> p k m', k=8), func=AF.Copy),
                 r=keys, w=[MNT.k])
        for ci in range(2):
            slot, W = self.wget('xattn_wkv', l, ci)
            for f in range(4):
                b, keys = self.ps(1)
                for kc in range(8):
                    c.op('pe', lambda kc=kc, f=f: nc.tensor.matmul(self.PS[:, b, 0:256], W[:, kc, f * 128:(f + 1) * 128],
                                                                   mntv[:, kc, :], start=(kc == 0), stop=(kc == 7)),
                         r=[slot.k, MNT.k], w=keys, inc=(kc == 7))
                c.op('act', lambda f=f, ci=ci: nc.scalar.activation(out=KT[:, ci * 4 + f, :], in_=self.PS[:, b, 0:256],
                                                                    func=AF.Copy), r=keys, w=[KT.k])
        for ci in range(2):
            slot, W = self.wget('xattn_wkv', l, 2 + ci)
            for mc in range(2):
                b, keys = self.ps(1)
                for kc in range(8):
                    c.op('pe', lambda kc=kc, mc=mc: nc.tensor.matmul(self.PS[:, b, :], mntv[:, kc, mc * 128:(mc + 1) * 128],
                                                                     W[:, kc, :], start=(kc == 0), stop=(kc == 7)),
                         r=[slot.k, MNT.k], w=keys, inc=(kc == 7))
                c.op('act', lambda mc=mc, ci=ci: nc.scalar.activation(out=VV[:, mc, ci * 512:(ci + 1) * 512],
                                                                      in_=self.PS[:, b, :], func=AF.Copy),
                     r=keys, w=[VV.k])

    def sweep_C(self, l, ti):
        c, nc, sc = self.c, self.nc, self.sc
        si, t0, first, last = self.tiles[ti]
        X, XN, LC, lc, SM, BIG = self.X, self.XN, self.LC, self.lc, self.SM, self.BIG
        c0 = t0 // CH
        MP = Buf(BIG.t, 'C_MP')
        mpv = BIG[:, 0:4096].rearrange('p (k t) -> p k t', k=8)
        G2 = Buf(BIG.t, 'C_G2')
        g2v = BIG[:, 4096:6144].bitcast(BF16).rearrange('p (k t) -> p k t', k=8)
        YNT = Buf(BIG.t, 'C_YNT')
        yntv = BIG[:, 6144:8192].bitcast(BF16).rearrange('p (k t) -> p k t', k=8)
        CM = Buf(BIG.t, 'C_CM')
        cmv = self.RS[:].bitcast(BF16).rearrange('p (g t) -> p g t', g=2)
        E4 = Buf(BIG.t, 'C_E4')
        e4v = BIG[:, 8448:8704].rearrange('p (c n) -> p c n', c=4)
        YL = [Buf(BIG.t, 'C_YL%d' % i) for i in range(2)]
        ylv = [BIG[:, 8704 + i * 1024:8704 + (i + 1) * 1024] for i in range(2)]
        ZS = [Buf(BIG.t, 'C_ZS%d' % i) for i in range(2)]
        zsv = [BIG[:, 10752 + i * 512:10752 + (i + 1) * 512].bitcast(BF16) for i in range(2)]
        HP = [Buf(BIG.t, 'C_HP%d' % i) for i in range(2)]
        hpv = [BIG[:, 11776 + i * 512:11776 + (i + 1) * 512].bitcast(BF16) for i in range(2)]
        HB = self.H
        hflat = HB[:, :, :].rearrange('p a t -> p (a t)')
        YT = Buf(HB.t, 'C_YT')
        ytv = hflat[:, 0:2048].bitcast(F32)
        YN = Buf(HB.t, 'C_YN')
        ynv = hflat[:, 2048:3072]
        MGB = Buf(HB.t, 'C_MGB')
        mgv = hflat[:, 3072:7168].rearrange('p (k t) -> p k t', k=8)
        QT = Buf(HB.t, 'C_QT')
        qtv = hflat[:, 7168:11264].rearrange('p (k t) -> p k t', k=8)
        OT = MGB
        otv = mgv
        PRB = Buf(self.SQ.t, 'C_PRB')
        sqflat = self.SQ[:, :, :].rearrange('p a t -> p (a t)')
        prv = sqflat[:, 0:2048].bitcast(F32).rearrange('p (h m) -> p h m', h=4)
        PNB = Buf(self.SQ.t, 'C_PNB')
        pnv = sqflat[:, 2048:3072].rearrange('p (h m) -> p h m', h=4)
        PT = Buf(self.SQ.t, 'C_PT')
        ptv = sqflat[:, 3072:4096].rearrange('p (a s) -> p a s', a=8)

        self.link([self.H.k, YT.k, YN.k, MGB.k, QT.k])
        self.link([self.SQ.k, (self.SQ.k, 0), (self.SQ.k, 1), PRB.k, PNB.k, PT.k] + [(PRB.k, hh) for hh in range(4)])
        c.dma('sp', mpv, sc['mp'][:, :, t0:t0 + T].rearrange('k p t -> p k t'), 'd_mp', r=[('mp', ti)], w=[MP.k])
        c.dma('sp', g2v, sc['g12'][8:16, :, t0:t0 + T].rearrange('k p t -> p k t'), 'd_g2',
              r=[('g12', ti, k) for k in range(8, 16)], w=[G2.k])
        self.link([self.RS.k, CM.k])
        c.dma('sp', cmv, sc['cm'][:, :, t0:t0 + T].rearrange('g p t -> p g t'), 'd_cm', r=[('cm', ti)], w=[CM.k])
        c.dma('sp', e4v, sc['e4'][t0:t0 + T, :].rearrange('(c p) n -> p c n', p=128), 'd_e4', r=[('e4', ti)], w=[E4.k])
        osn = lc['ssd_norm']
        for cc in range(4):
            chk = c0 + cc
            i2 = cc % 2
            tsl = slice(cc * 128, (cc + 1) * 128)
            c.dma('sp', ylv[i2], sc['yl'][t0 + cc * 128:t0 + (cc + 1) * 128, :], 'd_yl%d' % i2, r=[('yl', ti)],
                  w=[YL[i2].k])
            c.dma('sp', zsv[i2], sc['zs'][t0 + cc * 128:t0 + (cc + 1) * 128, :], 'd_zs%d' % i2, r=[('zs', ti)],
                  w=[ZS[i2].k])
            c.dma('sp', hpv[i2].rearrange('p (a n) -> p a n', a=2), sc['hp'][chk].rearrange('a p n -> p a n'),
                  'd_hp%d' % i2, r=[('hp', chk, 0), ('hp', chk, 1)], w=[HP[i2].k])
            hp3 = hpv[i2].rearrange('p (a n) -> p a n', a=2)
            yv = ylv[i2]
            for di in range(2):
                b, keys = self.ps(2)
                for g in range(2):
                    c.op('pe', lambda g=g, di=di: nc.tensor.matmul(self.PS[:, b + g, :], cmv[:, g, tsl],
                                                                   hp3[:, di, :], start=True, stop=True),
                         r=[CM.k, HP[i2].k], w=[keys[g]])
                ecol = 0 if di == 0 else 32
                ea = e4v[:, cc, ecol:ecol + 16]
                c.op('act', lambda b=b: nc.scalar.activation(
                    out=ytv, in_=self.PS[:, b:b + 2, :].rearrange('p b n -> p (b n)'), func=AF.Copy),
                    r=keys, w=[YT.k])
                c.op('dve', lambda ea=ea, b=b: nc.vector.tensor_tensor(
                    out=ytv.rearrange('p (h d) -> p h d', h=16),
                    in0=ytv.rearrange('p (h d) -> p h d', h=16),
                    in1=bass.AP(ea.tensor, ea.offset, [list(ea.ap[0]), [1, 16], [0, 64]]), op=ALU.mult),
                    r=[YT.k, E4.k], w=[YT.k])
                c.op('pool', lambda: nc.gpsimd.tensor_tensor(out=yv, in0=yv, in1=ytv, op=ALU.add),
                     r=[YL[i2].k, YT.k], w=[YL[i2].k])
            c.op('dve', lambda: nc.vector.tensor_tensor(out=yv, in0=yv, in1=zsv[i2], op=ALU.mult),
                 r=[YL[i2].k, ZS[i2].k], w=[YL[i2].k])
            for g in range(2):
                c.op('act', lambda g=g: nc.scalar.activation(out=ytv[:, g * 512:(g + 1) * 512],
                                                             in_=yv[:, g * 512:(g + 1) * 512], func=AF.Square,
                                                             accum_out=SM[:, 32 + g:33 + g]),
                     r=[YL[i2].k], w=[YT.k, ('SMg', g)])
            c.op('act', lambda: nc.scalar.activation(out=SM[:, 34:36], in_=SM[:, 32:34], func=AF.Sqrt,
                                                     bias=self.eps_ap(0), scale=1.0 / 512),
                 r=[('SMg', 0), ('SMg', 1)], w=[('SMg2',)])
            c.op('dve', lambda: nc.vector.reciprocal(out=SM[:, 36:38], in_=SM[:, 34:36]), r=[('SMg2',)], w=[('SMg3',)])
            for g in range(2):
                c.op('dve', lambda g=g: nc.vector.scalar_tensor_tensor(
                    out=ynv[:, g * 512:(g + 1) * 512], in0=yv[:, g * 512:(g + 1) * 512], scalar=SM[:, 36 + g:37 + g],
                    in1=LC[:, osn + g * 512:osn + (g + 1) * 512], op0=ALU.mult, op1=ALU.mult),
                    r=[YL[i2].k, ('SMg3',), LC.k], w=[YN.k])
            b, keys = self.ps(1)
            pb = self.PS[:, b, :].bitcast(BF16)
            for kc in range(8):
                c.op('pe', lambda kc=kc: nc.tensor.transpose(out=pb[:, kc * 128:(kc + 1) * 128],
                                                             in_=ynv[:, kc * 128:(kc + 1) * 128],
                                                             identity=self.cb('ident')),
                     r=[YN.k, self.CB.k], w=keys, inc=(kc == 7))
            c.op('act', lambda: nc.scalar.activation(out=yntv[:, :, tsl], in_=pb.rearrange('p (k t) -> p k t', k=8),
                                                     func=AF.Copy), r=keys, w=[YNT.k])
        c.dma('sp', X[:], sc['xT'][:, :, t0:t0 + T].rearrange('k p t -> p k t'), 'd_x', r=[('xT', ti)], w=[X.k])
        for ci in range(2):
            slot, W = self.wget('w_branch_c', l, ci)

            def evac_c(f, ps, keys, ci=ci):
                dd = ci * 4 + f
                tm = self.SA[dd % 2]
                c.op('dve', lambda: nc.vector.tensor_tensor(out=tm[:], in0=ps, in1=g2v[:, dd, :], op=ALU.mult),
                     r=keys + [G2.k], w=[tm.k])
                c.op('pool', lambda: nc.gpsimd.tensor_tensor(out=mgv[:, dd, :], in0=tm[:], in1=mpv[:, dd, :],
                                                             op=ALU.add), r=[tm.k, MP.k], w=[MGB.k])
            self.lin_fm(W, slot, 8, yntv, YNT.k, 512, evac_c)
        for ci in range(2):
            slot, W = self.wget('w_out', l, ci)

            def evac_o(f, ps, keys, ci=ci):
                dd = ci * 4 + f
                c.op('dve', lambda: nc.vector.tensor_tensor(out=X[:, dd, :], in0=ps, in1=X[:, dd, :], op=ALU.add),
                     r=keys + [X.k], w=[X.k])
            self.lin_fm(W, slot, 8, mgv, MGB.k, 512, evac_o)
        self.link([self.RS.k, CM.k])
        self.link([self.SQ.k, (self.SQ.k, 0), (self.SQ.k, 1), PRB.k, PNB.k, PT.k] + [(PRB.k, hh) for hh in range(4)])
        self.rmsnorm_fm('xattn_norm')
        self.link([self.SQ.k, (self.SQ.k, 0), (self.SQ.k, 1), PRB.k, PNB.k, PT.k] + [(PRB.k, hh) for hh in range(4)])
        for ci in range(2):
            slot, W = self.wget('xattn_wq', l, ci)

            def evac_q(f, ps, keys, ci=ci):
                dd = ci * 4 + f
                c.op('act', lambda: nc.scalar.activation(out=qtv[:, dd, :], in_=ps, func=AF.Copy, scale=1.0 / 16.0),
                     r=keys, w=[QT.k])
            self.lin_fm(W, slot, 8, XN, self.xnk(), 512, evac_q)
        KT, VV = self.KT, self.VV
        for cc in range(4):
            tsl = slice(cc * 128, (cc + 1) * 128)
            b, keys = self.ps(2)
            for hh in range(4):
                for j in range(2):
                    c.op('pe', lambda hh=hh, j=j: nc.tensor.matmul(
                        self.PS[:, b + hh // 2, (hh % 2) * 256:(hh % 2 + 1) * 256], qtv[:, hh * 2 + j, tsl],
                        KT[:, hh * 2 + j, :], start=(j == 0), stop=(j == 1)),
                        r=[QT.k, KT.k], w=[keys[hh // 2]], inc=(j == 1 and hh % 2 == 1))
            sc4 = self.PS[:, b:b + 2, :].rearrange('p b (h m) -> p (b h) m', h=2)
            c.op('act', lambda: nc.scalar.activation(out=prv, in_=sc4, func=AF.Copy), r=keys,
                 w=[PRB.k] + [(PRB.k, hh) for hh in range(4)])
            sc4 = prv
            keys = [PRB.k]
            c.op('dve', lambda: nc.vector.tensor_reduce(out=SM[:, 40:44], in_=sc4, axis=AX.X, op=ALU.max),
                 r=keys, w=[('SMx', 0)])
            c.op('dve', lambda: nc.vector.tensor_scalar(out=SM[:, 44:48], in0=SM[:, 40:44], scalar1=-1.0, scalar2=None,
                                                        op0=ALU.mult), r=[('SMx', 0)], w=[('SMx', 1)])
            for hh in range(4):
                c.op('act', lambda hh=hh: nc.scalar.activation(out=prv[:, hh, :], in_=sc4[:, hh, :], func=AF.Exp,
                                                               bias=SM[:, 44 + hh:45 + hh],
                                                               accum_out=SM[:, 48 + hh:49 + hh]),
                     r=keys + [('SMx', 1)], w=[(PRB.k, hh), ('SMx', 2, hh)])
            c.op('dve', lambda: nc.vector.reciprocal(out=SM[:, 52:56], in_=SM[:, 48:52]),
                 r=[('SMx', 2, hh) for hh in range(4)], w=[('SMx', 3)])
            ri = SM[:, 52:56]
            c.op('dve', lambda: nc.vector.tensor_tensor(
                out=pnv, in0=prv, in1=bass.AP(ri.tensor, ri.offset, [list(ri.ap[0]), [1, 4], [0, 256]]), op=ALU.mult),
                r=[PRB.k, ('SMx', 3)] + [(PRB.k, hh) for hh in range(4)], w=[PNB.k])
            b2, k2 = self.ps(1)
            pb = self.PS[:, b2, :].bitcast(BF16)
            for hh in range(4):
                for mc in range(2):
                    a = hh * 2 + mc
                    c.op('pe', lambda hh=hh, mc=mc, a=a: nc.tensor.transpose(
                        out=pb[:, a * 128:(a + 1) * 128], in_=pnv[:, hh, mc * 128:(mc + 1) * 128],
                        identity=self.cb('ident')), r=[PNB.k, self.CB.k], w=k2, inc=(a == 7))
            c.op('act', lambda: nc.scalar.activation(out=ptv, in_=pb.rearrange('p (a s) -> p a s', a=8), func=AF.Copy),
                 r=k2, w=[PT.k])
            b3, k3 = self.ps(2)
            for hh in range(4):
                for j in range(2):
                    dd = hh * 2 + j
                    for mc in range(2):
                        c.op('pe', lambda hh=hh, j=j, mc=mc, dd=dd: nc.tensor.matmul(
                            self.PS[:, b3 + dd // 4, (dd % 4) * 128:(dd % 4 + 1) * 128],
                            VV[:, mc, hh * 256 + j * 128:hh * 256 + (j + 1) * 128], ptv[:, hh * 2 + mc, :],
                            start=(mc == 0), stop=(mc == 1)),
                            r=[VV.k, PT.k], w=[k3[dd // 4]], inc=(mc == 1 and dd % 4 == 3))
            c.op('act', lambda: nc.scalar.activation(
                out=otv[:, :, tsl], in_=self.PS[:, b3:b3 + 2, :].rearrange('p b (k s) -> p (b k) s', k=4),
                func=AF.Copy), r=k3, w=[MGB.k])
        for ci in range(2):
            slot, W = self.wget('xattn_wo', l, ci)

            def evac_wo(f, ps, keys, ci=ci):
                dd = ci * 4 + f
                c.op('dve', lambda: nc.vector.tensor_tensor(out=X[:, dd, :], in0=ps, in1=X[:, dd, :], op=ALU.add),
                     r=keys + [X.k], w=[X.k])
            self.lin_fm(W, slot, 8, otv, MGB.k, 512, evac_wo)
        self.link([self.SQ.k, (self.SQ.k, 0), (self.SQ.k, 1), PRB.k, PNB.k, PT.k] + [(PRB.k, hh) for hh in range(4)])
        self.rmsnorm_fm('ffn2_norm')
        self.link([self.H.k, YT.k, YN.k, MGB.k, QT.k])
        self.ffn(l, 'ffn2')
        if l < self.depth - 1:
            c.dma('pool', sc['xT'][:, :, t0:t0 + T].rearrange('k p t -> p k t'), X[:], 'd_st_X', r=[X.k],
                  w=[('xT', ti)])
        else:
            ofn = lc['final_norm']
            for cc in range(4):
                i2 = cc % 2
                xo, xok = ylv[i2], YL[i2].k
                for half in range(2):
                    b, keys = self.ps(1)
                    for q in range(4):
                        kc = half * 4 + q
                        c.op('pe', lambda kc=kc, q=q, cc=cc: nc.tensor.transpose(
                            out=self.PS[:, b, q * 128:(q + 1) * 128], in_=X[:, kc, cc * 128:(cc + 1) * 128],
                            identity=self.cf('ident')), r=[X.k, self.CF.k], w=keys, inc=(q == 3))
                    c.op('act', lambda half=half: nc.scalar.activation(out=xo[:, half * 512:(half + 1) * 512],
                                                                       in_=self.PS[:, b, :], func=AF.Copy),
                         r=keys, w=[xok])
                c.op('act', lambda: nc.scalar.activation(out=ytv, in_=xo, func=AF.Square, accum_out=SM[:, 56:57]),
                     r=[xok], w=[self.H.k, YT.k, ('SMf', 0)])
                c.op('act', lambda: nc.scalar.activation(out=SM[:, 57:58], in_=SM[:, 56:57], func=AF.Sqrt,
                                                         bias=self.eps_ap(0), scale=1.0 / 1024),
                     r=[('SMf', 0)], w=[('SMf', 1)])
                c.op('dve', lambda: nc.vector.reciprocal(out=SM[:, 58:59], in_=SM[:, 57:58]), r=[('SMf', 1)],
                     w=[('SMf', 2)])
                c.op('dve', lambda: nc.vector.scalar_tensor_tensor(out=xo, in0=xo, scalar=SM[:, 58:59],
                                                                   in1=LC[:, ofn:ofn + 1024], op0=ALU.mult,
                                                                   op1=ALU.mult), r=[xok, ('SMf', 2), LC.k], w=[xok])
                c.dma('pool', self.dr['y'][t0 + cc * 128:t0 + (cc + 1) * 128, :], xo, 'd_st_y%d' % i2, r=[xok],
                      w=[('y', ti, cc)])


def run(depth, seqs, per_core_inputs, n_cores, trace=False, stop=None):
    bld = Builder(depth, seqs, stop)
    nc = bld.build()
    res = run_bass_kernel_spmd(nc, per_core_inputs, core_ids=list(range(n_cores)), trace=trace)
    return res, bld


def kernel(**inp):
    depth = 4
    seqs = [8192, 4096]
    n = 8
    xp, xs = np.asarray(inp['x_prompt']), np.asarray(inp['x_sample'])
    mp, ms = np.asarray(inp['mem_prompt']), np.asarray(inp['mem_sample'])
    shared = {k: np.ascontiguousarray(np.asarray(inp[k], dtype=np.float32)) for k in WNAMES + SMALL}
    shared['consts'] = CONSTF_ARR
    shared['constsb'] = CONSTB_ARR
    in_maps = []
    for i in range(n):
        d = dict(shared)
        d['x'] = np.ascontiguousarray(np.concatenate([xp[i], xs[i % 4]], axis=0))
        d['mem'] = np.ascontiguousarray(np.stack([mp[i], ms[i % 4]], axis=0))
        in_maps.append(d)
    res, _ = run(depth, seqs, in_maps, n)
    yp = np.stack([res.results[i]['y'][0:8192] for i in range(8)], axis=0)
    ysm = np.stack([res.results[i]['y'][8192:] for i in range(4)], axis=0)
    return (yp.astype(np.float32), ysm.astype(np.float32))
```
